# Optimizing a Trainium2 kernel written in Bass

```python
import math
import jax, jax.numpy as jnp
from jax import lax
import numpy as np

D_MODEL = 2048
BATCH = 4
SEQ = 2048
DEPTH = 2
DEC_BATCH = 128
DEC_SEQ = 8
PAST_LEN = 16384
PAGE_SIZE = 128

N_MEM = 256
MIX_W = D_MODEL // 2
N_BRANCH = 3
RW_HD = 64
RW_HEADS = MIX_W // RW_HD
RW_DECAY_LORA = 64
RW_AAA_LORA = 64
RW_GATE_LORA = 128
RW_COLS = 3 * MIX_W + RW_DECAY_LORA + RW_AAA_LORA + RW_GATE_LORA
RW_GN_EPS = 64e-5
ML_HEADS = 8
ML_HD = MIX_W // ML_HEADS
ML_COLS = 4 * MIX_W + 2 * ML_HEADS
ML_GATE_CAP = 15.0
RT_HEADS = 4
RT_HD = MIX_W // RT_HEADS
RT_COLS = 4 * MIX_W
ROPE_BASE = 10000.0
IN_COLS = RW_COLS + ML_COLS + RT_COLS + N_BRANCH * D_MODEL
X_HEADS = 4
X_HD = 128
X_W = X_HEADS * X_HD
D_FF = 4 * D_MODEL
CHUNK = 64
EPS = 1e-6

kernel_name = 'hybrid_rwkv7_mlstm_retnet_memxattn_step'


def _split(x, sizes):
    idx = [int(i) for i in np.cumsum(sizes)[:-1]]
    return jnp.split(x, idx, axis=-1)


def _rmsnorm(x, g):
    xf = x.astype(jnp.float32)
    y = xf * lax.rsqrt(jnp.mean(xf * xf, axis=-1, keepdims=True) + EPS)
    return (y * g.astype(jnp.float32)).astype(x.dtype)


def _head_rms(x):
    return x * lax.rsqrt(jnp.mean(x * x, axis=-1, keepdims=True) + EPS)


def _rope(x, pos):
    half = x.shape[-1] // 2
    inv = ROPE_BASE ** (-jnp.arange(half, dtype=jnp.float32) / half)
    ang = pos[:, None] * inv[None, :]
    cos = jnp.cos(ang)[None, :, None, :]
    sin = jnp.sin(ang)[None, :, None, :]
    x1, x2 = x[..., :half], x[..., half:]
    return jnp.concatenate([x1 * cos - x2 * sin, x1 * sin + x2 * cos], axis=-1)


def _ret_log_decay():
    return jnp.log(1.0 - jnp.exp(jnp.linspace(math.log(1.0 / 32), math.log(1.0 / 512), RT_HEADS)))


def _chunked(step, carry, xs, L):
    c = CHUNK if L % CHUNK == 0 else L
    n = L // c
    def to_blocks(t):
        return jnp.moveaxis(t.reshape((t.shape[0], n, c) + t.shape[2:]), 1, 0)
    carry, ys = lax.scan(step, carry, tuple(to_blocks(t) for t in xs))
    ys = jnp.moveaxis(ys, 0, 1)
    return carry, ys.reshape((ys.shape[0], L) + ys.shape[3:])


def _rwkv_branch(u, shift0, s0, mu, w0, w_up, a0, a_up, g_up, k_k, k_a, r_k, gn_g, gn_b):
    B, L, _ = u.shape
    f32 = jnp.float32
    uf = u.astype(f32)
    prev = jnp.concatenate([shift0.astype(f32)[:, None, :], uf[:, :-1]], axis=1)
    z = uf + (prev - uf) * mu
    r, k, v, wd, ad, gd = _split(z, [MIX_W, MIX_W, MIX_W, RW_DECAY_LORA, RW_AAA_LORA, RW_GATE_LORA])
    w_log = -jax.nn.softplus(-(w0 + jnp.tanh(wd) @ w_up)) - 0.5
    decay = jnp.exp(-jnp.exp(w_log))
    a = jax.nn.sigmoid(a0 + ad @ a_up)
    g = jax.nn.sigmoid(gd) @ g_up
    heads = lambda t: t.reshape(B, L, RW_HEADS, RW_HD)
    kk = heads(k * k_k)
    kk = kk / jnp.maximum(jnp.sqrt(jnp.sum(kk * kk, axis=-1, keepdims=True)), 1e-12)
    k = heads(k * (1.0 + (a - 1.0) * k_a))
    r, v, decay, a = heads(r), heads(v), heads(decay), heads(a)

    def step(S, inp):
        r_t, k_t, v_t, w_t, kk_t, a_t = inp
        sa = jnp.einsum('bhvk,bhk->bhv', S, kk_t)
        S = (S * w_t[:, :, None, :] - sa[..., None] * (kk_t * a_t)[:, :, None, :]
             + v_t[..., None] * k_t[:, :, None, :])
        return S, jnp.einsum('bhvk,bhk->bhv', S, r_t)

    tm = lambda t: jnp.moveaxis(t, 1, 0)
    S, y = lax.scan(step, s0.astype(f32), (tm(r), tm(k), tm(v), tm(decay), tm(kk), tm(a)))
    y = jnp.moveaxis(y, 0, 1)
    mean = jnp.mean(y, axis=-1, keepdims=True)
    var = jnp.mean(jnp.square(y - mean), axis=-1, keepdims=True)
    y = ((y - mean) * lax.rsqrt(var + RW_GN_EPS)).reshape(B, L, MIX_W) * gn_g + gn_b
    bonus = (jnp.sum(r * k * r_k, axis=-1, keepdims=True) * v).reshape(B, L, MIX_W)
    return (y + bonus) * g, S, u[:, -1]


def _mlstm_block(carry, inp):
    C0, n0, m0 = carry
    q, k, v, ig, lf = inp
    c = q.shape[1]
    b = jnp.moveaxis(jnp.cumsum(lf, axis=1), 1, 2)
    igh = jnp.moveaxis(ig, 1, 2)
    causal = jnp.tril(jnp.ones((c, c), bool))
    dlog = jnp.where(causal, b[..., :, None] - b[..., None, :] + igh[..., None, :], -jnp.inf)
    m_inter = b + m0[..., None]
    m_t = jnp.maximum(m_inter, jnp.max(dlog, axis=-1))
    wts = jnp.exp(dlog - m_t[..., None]) * jnp.einsum('bthd,bshd->bhts', q, k)
    s_inter = jnp.exp(m_inter - m_t)
    num = jnp.einsum('bhts,bshe->bthe', wts, v) + jnp.einsum('bht,bthd,bhde->bthe', s_inter, q, C0)
    den = jnp.sum(wts, axis=-1) + s_inter * jnp.einsum('bthd,bhd->bht', q, n0)
    h = num / jnp.moveaxis(jnp.maximum(jnp.abs(den), jnp.exp(-m_t)), 1, 2)[..., None]
    m_new = m_t[..., -1]
    w_end = jnp.exp(b[..., -1:] - b + igh - m_new[..., None])
    f_end = jnp.exp(b[..., -1] + m0 - m_new)
    C = f_end[..., None, None] * C0 + jnp.einsum('bhs,bshd,bshe->bhde', w_end, k, v)
    n = f_end[..., None] * n0 + jnp.einsum('bhs,bshd->bhd', w_end, k)
    return (C, n, m_new), h


def _mlstm_branch(cols, c0, n0, m0, i_b, f_b, norm_g):
    B, L, _ = cols.shape
    f32 = jnp.float32
    q, k, v, o, ig, fg = _split(cols.astype(f32), [MIX_W] * 4 + [ML_HEADS] * 2)
    heads = lambda t: t.reshape(B, L, ML_HEADS, ML_HD)
    q, k, v = heads(q), heads(k) * ML_HD ** -0.5, heads(v)
    ig = ML_GATE_CAP * jnp.tanh((ig + i_b) / ML_GATE_CAP)
    lf = jax.nn.log_sigmoid(ML_GATE_CAP * jnp.tanh((fg + f_b) / ML_GATE_CAP))
    (C, n, m), h = _chunked(_mlstm_block, (c0.astype(f32), n0.astype(f32), m0.astype(f32)), (q, k, v, ig, lf), L)
    h = _head_rms(h).reshape(B, L, MIX_W) * norm_g
    return jax.nn.sigmoid(o) * h, C, n, m


def _ret_branch(cols, s0, pos0):
    B, L, _ = cols.shape
    f32 = jnp.float32
    q, k, v, g = _split(cols.astype(f32), [MIX_W] * 4)
    heads = lambda t: t.reshape(B, L, RT_HEADS, RT_HD)
    pos = jnp.arange(L, dtype=f32) + float(pos0)
    q = _rope(heads(q), pos)
    k = _rope(heads(k), pos) * RT_HD ** -0.5
    v = heads(v)
    log_g = _ret_log_decay()

    def step(S0, inp):
        qc, kc, vc = inp
        c = qc.shape[1]
        t = jnp.arange(c, dtype=f32)
        diff = t[:, None] - t[None, :]
        dec = jnp.where(diff >= 0, jnp.exp(log_g[:, None, None] * jnp.maximum(diff, 0.0)), 0.0)
        inner = jnp.einsum('bhts,bshe->bthe', jnp.einsum('bthd,bshd->bhts', qc, kc) * dec, vc)
        cross = jnp.einsum('bthd,bhde->bthe', qc, S0) * jnp.exp(log_g[None, :] * (t[:, None] + 1.0))[None, :, :, None]
        S = (jnp.exp(log_g * c)[None, :, None, None] * S0
             + jnp.einsum('bshd,bshe,sh->bhde', kc, vc, jnp.exp(log_g[None, :] * (c - 1.0 - t)[:, None])))
        return S, inner + cross

    S, y = _chunked(step, s0.astype(f32), (q, k, v), L)
    y = _head_rms(y).reshape(B, L, MIX_W)
    return jax.nn.silu(g) * y, S


def _mem_kv(mem, g, wkv):
    B, M, _ = mem.shape
    kv = _rmsnorm(mem, g) @ wkv
    k, v = jnp.split(kv, 2, axis=-1)
    return k.reshape(B, M, X_HEADS, X_HD), v.reshape(B, M, X_HEADS, X_HD)


def _cross_attn(h, mk, mv, wq, wo):
    B, L, _ = h.shape
    q = (h @ wq).reshape(B, L, X_HEADS, X_HD)
    s = jnp.einsum('blhd,bmhd->bhlm', q, mk).astype(jnp.float32) * X_HD ** -0.5
    p = jax.nn.softmax(s, axis=-1).astype(h.dtype)
    o = jnp.einsum('bhlm,bmhd->blhd', p, mv).reshape(B, L, X_W)
    return o @ wo


def _layer(x, mem_k, mem_v, init, pos0, l, W):
    shift0, s_rw0, c0, n0, m0, s_rt0 = init
    B, L, _ = x.shape
    dt = x.dtype
    h = _rmsnorm(x, W['g_pre_mix'][l])
    z = h @ W['w_in'][l]
    u_rw, c_ml, c_rt, c_gate = _split(z, [RW_COLS, ML_COLS, RT_COLS, N_BRANCH * D_MODEL])
    y_rw, s_rw, shift = _rwkv_branch(u_rw, shift0, s_rw0, W['rw_mu'][l], W['rw_w0'][l], W['rw_w_up'][l],
                                     W['rw_a0'][l], W['rw_a_up'][l], W['rw_g_up'][l], W['rw_k_k'][l],
                                     W['rw_k_a'][l], W['rw_r_k'][l], W['rw_gn_g'][l], W['rw_gn_b'][l])
    y_ml, c_new, n_new, m_new = _mlstm_branch(c_ml, c0, n0, m0, W['ml_i_b'][l], W['ml_f_b'][l], W['ml_norm_g'][l])
    y_rt, s_rt = _ret_branch(c_rt, s_rt0, pos0)
    ys = jnp.stack([y_rw, y_ml, y_rt], axis=2).astype(dt)
    proj = jnp.einsum('blcw,cwd->blcd', ys, W['w_br'][l])
    gates = jax.nn.sigmoid(c_gate.astype(jnp.float32)).reshape(B, L, N_BRANCH, D_MODEL)
    merged = jnp.sum(gates * proj.astype(jnp.float32), axis=2).astype(dt)
    x = x + _rmsnorm(merged @ W['w_out'][l], W['g_post_mix'][l])
    h = _rmsnorm(x, W['g_pre_x'][l])
    x = x + _rmsnorm(_cross_attn(h, mem_k, mem_v, W['x_wq'][l], W['x_wo'][l]), W['g_post_x'][l])
    h = _rmsnorm(x, W['g_pre_ff'][l])
    ff = jnp.square(jax.nn.relu(h @ W['ff_w1'][l])) @ W['ff_w2'][l]
    x = x + _rmsnorm(ff, W['g_post_ff'][l])
    new = (shift.astype(dt), s_rw.astype(dt), c_new.astype(dt), n_new.astype(dt), m_new.astype(dt), s_rt.astype(dt))
    return x, new


def setup_inputs(seed: int = 0) -> dict:
    key = jax.random.key(seed)
    ks = list(jax.random.split(key, 64))
    cnt = [0]
    f32 = jnp.float32
    def nk():
        cnt[0] += 1
        return ks[cnt[0] - 1]
    def nrm(shape, scale=1.0):
        return jax.random.normal(nk(), shape, f32) * scale
    def uni(shape, lo, hi):
        return jax.random.uniform(nk(), shape, f32, lo, hi)
    def gain(shape):
        return 1.0 + nrm(shape, 0.05)
    D = D_MODEL
    inp = {}
    inp['x_prompt'] = nrm((BATCH, SEQ, D))
    inp['x_sample'] = nrm((DEC_BATCH, DEC_SEQ, D))
    inp['mem_prompt'] = nrm((BATCH, N_MEM, D))
    inp['state_rwkv_shift'] = nrm((DEPTH, DEC_BATCH, RW_COLS))
    inp['state_rwkv'] = nrm((DEPTH, DEC_BATCH, RW_HEADS, RW_HD, RW_HD), 0.3)
    inp['state_mlstm_c'] = nrm((DEPTH, DEC_BATCH, ML_HEADS, ML_HD, ML_HD), 0.3)
    inp['state_mlstm_n'] = nrm((DEPTH, DEC_BATCH, ML_HEADS, ML_HD), 0.3)
    inp['state_mlstm_m'] = uni((DEPTH, DEC_BATCH, ML_HEADS), -2.0, 2.0)
    inp['state_ret'] = nrm((DEPTH, DEC_BATCH, RT_HEADS, RT_HD, RT_HD))
    inp['cache_mem_k'] = nrm((DEPTH, DEC_BATCH, N_MEM, X_HEADS, X_HD))
    inp['cache_mem_v'] = nrm((DEPTH, DEC_BATCH, N_MEM, X_HEADS, X_HD))
    for name in ['g_pre_mix', 'g_post_mix', 'g_pre_x', 'g_post_x', 'g_pre_ff', 'g_post_ff', 'g_mem']:
        inp[name] = gain((DEPTH, D))
    inp['w_in'] = nrm((DEPTH, D, IN_COLS), D ** -0.5)
    inp['rw_mu'] = uni((DEPTH, RW_COLS), 0.0, 1.0)
    inp['rw_w0'] = uni((DEPTH, MIX_W), -6.0, -1.0)
    inp['rw_w_up'] = nrm((DEPTH, RW_DECAY_LORA, MIX_W), 0.5 * RW_DECAY_LORA ** -0.5)
    inp['rw_a0'] = nrm((DEPTH, MIX_W), 0.1)
    inp['rw_a_up'] = nrm((DEPTH, RW_AAA_LORA, MIX_W), RW_AAA_LORA ** -0.5)
    inp['rw_g_up'] = nrm((DEPTH, RW_GATE_LORA, MIX_W), RW_GATE_LORA ** -0.5)
    inp['rw_k_k'] = 0.85 + nrm((DEPTH, MIX_W), 0.05)
    inp['rw_k_a'] = 1.0 + nrm((DEPTH, MIX_W), 0.05)
    inp['rw_r_k'] = nrm((DEPTH, RW_HEADS, RW_HD), 0.1)
    inp['rw_gn_g'] = gain((DEPTH, MIX_W))
    inp['rw_gn_b'] = nrm((DEPTH, MIX_W), 0.01)
    inp['ml_i_b'] = nrm((DEPTH, ML_HEADS), 0.1)
    inp['ml_f_b'] = jnp.linspace(3.0, 6.0, ML_HEADS)[None, :] + nrm((DEPTH, ML_HEADS), 0.1)
    inp['ml_norm_g'] = gain((DEPTH, MIX_W))
    inp['w_br'] = nrm((DEPTH, N_BRANCH, MIX_W, D), MIX_W ** -0.5)
    inp['w_out'] = nrm((DEPTH, D, D), D ** -0.5)
    inp['x_wq'] = nrm((DEPTH, D, X_W), D ** -0.5)
    inp['x_wkv'] = nrm((DEPTH, D, 2 * X_W), D ** -0.5)
    inp['x_wo'] = nrm((DEPTH, X_W, D), X_W ** -0.5)
    inp['ff_w1'] = nrm((DEPTH, D, D_FF), D ** -0.5)
    inp['ff_w2'] = nrm((DEPTH, D_FF, D), D_FF ** -0.5)
    return inp


def reference(x_prompt, x_sample, mem_prompt, state_rwkv_shift, state_rwkv, state_mlstm_c, state_mlstm_n,
              state_mlstm_m, state_ret, cache_mem_k, cache_mem_v, g_pre_mix, g_post_mix, g_pre_x, g_post_x,
              g_pre_ff, g_post_ff, g_mem, w_in, rw_mu, rw_w0, rw_w_up, rw_a0, rw_a_up, rw_g_up, rw_k_k, rw_k_a,
              rw_r_k, rw_gn_g, rw_gn_b, ml_i_b, ml_f_b, ml_norm_g, w_br, w_out, x_wq, x_wkv, x_wo, ff_w1, ff_w2):
    W = dict(g_pre_mix=g_pre_mix, g_post_mix=g_post_mix, g_pre_x=g_pre_x, g_post_x=g_post_x,
             g_pre_ff=g_pre_ff, g_post_ff=g_post_ff, w_in=w_in, rw_mu=rw_mu, rw_w0=rw_w0, rw_w_up=rw_w_up,
             rw_a0=rw_a0, rw_a_up=rw_a_up, rw_g_up=rw_g_up, rw_k_k=rw_k_k, rw_k_a=rw_k_a, rw_r_k=rw_r_k,
             rw_gn_g=rw_gn_g, rw_gn_b=rw_gn_b, ml_i_b=ml_i_b, ml_f_b=ml_f_b, ml_norm_g=ml_norm_g,
             w_br=w_br, w_out=w_out, x_wq=x_wq, x_wo=x_wo, ff_w1=ff_w1, ff_w2=ff_w2)
    f32 = jnp.float32
    bp = x_prompt.shape[0]
    xp = x_prompt
    st_p, mk_p, mv_p = [], [], []
    for l in range(DEPTH):
        mk, mv = _mem_kv(mem_prompt, g_mem[l], x_wkv[l])
        init = (jnp.zeros((bp, RW_COLS), f32), jnp.zeros((bp, RW_HEADS, RW_HD, RW_HD), f32),
                jnp.zeros((bp, ML_HEADS, ML_HD, ML_HD), f32), jnp.zeros((bp, ML_HEADS, ML_HD), f32),
                jnp.zeros((bp, ML_HEADS), f32), jnp.zeros((bp, RT_HEADS, RT_HD, RT_HD), f32))
        xp, st = _layer(xp, mk, mv, init, 0, l, W)
        st_p.append(st)
        mk_p.append(mk)
        mv_p.append(mv)
    xs = x_sample
    st_s = []
    for l in range(DEPTH):
        init = (state_rwkv_shift[l], state_rwkv[l], state_mlstm_c[l], state_mlstm_n[l], state_mlstm_m[l], state_ret[l])
        xs, st = _layer(xs, cache_mem_k[l], cache_mem_v[l], init, PAST_LEN, l, W)
        st_s.append(st)
    p_rwkv_shift = jnp.stack([s[0] for s in st_p])
    p_rwkv = jnp.stack([s[1] for s in st_p])
    p_mlstm_c = jnp.stack([s[2] for s in st_p])
    p_mlstm_n = jnp.stack([s[3] for s in st_p])
    p_mlstm_m = jnp.stack([s[4] for s in st_p])
    p_ret = jnp.stack([s[5] for s in st_p])
    p_mem_k = jnp.stack(mk_p)
    p_mem_v = jnp.stack(mv_p)
    s_rwkv_shift = jnp.stack([s[0] for s in st_s])
    s_rwkv = jnp.stack([s[1] for s in st_s])
    s_mlstm_c = jnp.stack([s[2] for s in st_s])
    s_mlstm_n = jnp.stack([s[3] for s in st_s])
    s_mlstm_m = jnp.stack([s[4] for s in st_s])
    s_ret = jnp.stack([s[5] for s in st_s])
    return (xp, xs, p_rwkv_shift, p_rwkv, p_mlstm_c, p_mlstm_n, p_mlstm_m, p_ret, p_mem_k, p_mem_v,
            s_rwkv_shift, s_rwkv, s_mlstm_c, s_mlstm_n, s_mlstm_m, s_ret)
```

```python
import math
import numpy as np
from contextlib import ExitStack
import concourse.bass as bass
import concourse.mybir as mybir
from concourse.bass_utils import run_bass_kernel_spmd

F32 = mybir.dt.float32
BF16 = mybir.dt.bfloat16
ALU = mybir.AluOpType
AF = mybir.ActivationFunctionType
AX = mybir.AxisListType

D = 2048
DEPTH = 2
NCORE = 8
BATCH = 4
SEQ = 2048
DEC_B = 128
DEC_T = 8
SB = DEC_B // NCORE
PAST_LEN = 16384
N_MEM = 256
MIXW = 1024
EPS = 1e-6
RW_GN_EPS = 64e-5
RW0 = 0
ML0 = 27
RT0 = 61
GT0 = 93
NBLK_IN = 141


class V:
    def __init__(s, k, a):
        s.k = k
        s.a = a

    def __getitem__(s, i):
        return V(s.k, s.a[i])

    def bc(s, shape):
        return V(s.k, s.a.to_broadcast(shape))


class Prog:
    ENGS = ('pe', 'act', 'dve', 'pool', 'sp')

    def __init__(self, nc, es, ndma=16):
        self.nc = nc
        self.e = {'pe': nc.tensor, 'act': nc.scalar, 'dve': nc.vector, 'pool': nc.gpsimd, 'sp': nc.sync}
        self.es = es
        self.epoch = {k: 0 for k in self.ENGS}
        self.sem = {(k, 0): es.enter_context(nc.semaphore('s_' + k + '0')) for k in self.ENGS}
        self.cnt = {k: 0 for k in self.ENGS}
        import os as _o
        self.EPOCH_MAX = int(_o.environ.get("MK_EPOCH", "30000"))
        self.dsem = [es.enter_context(nc.semaphore('d%d' % i)) for i in range(ndma)]
        self.dcnt = [0] * ndma
        self.dnext = 0
        self.dnext_pool = ndma // 2
        self.known = {k: {} for k in self.ENGS}
        self.lastw = {}
        self.readers = {}
        self.ninst = 0
        self.stopped = False
        self.nbar = 0
        import os
        self.stop_after = int(os.environ['MK_STOP']) if 'MK_STOP' in os.environ else None
        self.stop_i = int(os.environ['MK_STOPI']) if 'MK_STOPI' in os.environ else None

    def _wait(self, eng, ev):
        kind, who, val = ev
        if kind == 'c' and who[0] == eng and eng == 'pe':
            return
        key = (kind, who)
        if self.known[eng].get(key, 0) >= val:
            return
        if kind == 'c':
            for (kk, ww), vv in self.known[eng].items():
                if kk == 'c' and ww[0] == who[0] and ww[1] > who[1] and vv > 0:
                    return
        sem = self.sem[who] if kind == 'c' else self.dsem[who]
        self.e[eng].wait_ge(sem, val)
        self.known[eng][key] = val
        self.ninst += 1

    def _deps(self, eng, reads, writes):
        for r in reads:
            ev = self.lastw.get(r)
            if ev is not None:
                self._wait(eng, ev)
        for w in writes:
            ev = self.lastw.get(w)
            if ev is not None:
                self._wait(eng, ev)
            for ev in self.readers.get(w, ()):
                self._wait(eng, ev)

    def _commit(self, ev, reads, writes):
        for w in writes:
            self.lastw[w] = ev
            self.readers[w] = []
        for r in reads:
            if r in writes:
                continue
            l = self.readers.setdefault(r, [])
            l.append(ev)
            if len(l) > 40:
                d = {}
                for x in l:
                    d[(x[0], x[1])] = x
                self.readers[r] = list(d.values())

    def I(self, eng, fn, reads=(), writes=()):
        if self.stop_i is not None and self.ninst >= self.stop_i:
            self.stopped = True
        if self.stopped:
            return
        reads = [r.k if isinstance(r, V) else r for r in reads]
        writes = [w.k if isinstance(w, V) else w for w in writes]
        pr = [r for r in reads if r.startswith('ps')]
        if pr:
            writes = writes + [r for r in pr if r not in writes]
            reads = [r for r in reads if not r.startswith('ps')]
        self._deps(eng, reads, writes)
        ins = fn(self.e[eng])
        if self.cnt[eng] >= self.EPOCH_MAX:
            self.epoch[eng] += 1
            self.cnt[eng] = 0
            self.sem[(eng, self.epoch[eng])] = self.es.enter_context(
                self.nc.semaphore('s_%s%d' % (eng, self.epoch[eng])))
        self.cnt[eng] += 1
        ins.then_inc(self.sem[(eng, self.epoch[eng])], 1)
        self._commit(('c', (eng, self.epoch[eng]), self.cnt[eng]), reads, writes)
        self.ninst += 1

    def dma(self, q, out, in_, reads=(), writes=(), **kw):
        if self.stop_i is not None and self.ninst >= self.stop_i:
            self.stopped = True
        if self.stopped:
            return
        reads = [r.k if isinstance(r, V) else r for r in reads]
        writes = [w.k if isinstance(w, V) else w for w in writes]
        half = len(self.dsem) // 2
        if q == 'pool':
            i = self.dnext_pool
            self.dnext_pool = half + (self.dnext_pool - half + 1) % (len(self.dsem) - half)
        else:
            i = self.dnext
            self.dnext = (self.dnext + 1) % half
        if self.dcnt[i] > 0:
            self._wait(q, ('d', i, self.dcnt[i]))
        self._deps(q, reads, writes)
        ins = self.e[q].dma_start(out=out, in_=in_, **kw)
        self.dcnt[i] += 16
        ins.then_inc(self.dsem[i], 16)
        self._commit(('d', i, self.dcnt[i]), reads, writes)
        self.ninst += 1

    def barrier(self):
        if self.stopped:
            return
        self.nbar += 1
        if self.stop_after is not None and self.nbar >= self.stop_after:
            self.stopped = True
        for eng in self.ENGS:
            for i, v in enumerate(self.dcnt):
                if v:
                    self._wait(eng, ('d', i, v))
            for k in self.ENGS:
                if self.cnt[k]:
                    self._wait(eng, ('c', (k, self.epoch[k]), self.cnt[k]))
        self.lastw = {}
        self.readers = {}

    def finish(self):
        for i, v in enumerate(self.dcnt):
            if v:
                self._wait('sp', ('d', i, v))
        for k in self.ENGS:
            if k != 'sp' and self.cnt[k]:
                self._wait('sp', ('c', (k, self.epoch[k]), self.cnt[k]))


def _ret_gamma():
    lg = np.log(1.0 - np.exp(np.linspace(math.log(1.0 / 32), math.log(1.0 / 512), 4)))
    return np.exp(lg).astype(np.float64)


def make_consts(seglen):
    t = np.arange(128)
    seg = t // seglen
    tau = t % seglen
    same = seg[:, None] == seg[None, :]
    c = {}
    le = (same & (t[:, None] <= t[None, :])).astype(np.float32)
    lt = (same & (t[:, None] < t[None, :])).astype(np.float32)
    c['le'] = le
    c['lt'] = lt
    c['gt'] = lt.T.copy()
    g = _ret_gamma()
    dt = np.zeros((4, 128, 128), np.float32)
    gp = np.zeros((4, 128, 128), np.float32)
    kd = np.zeros((128, 4), np.float32)
    for h in range(4):
        diff = (t[None, :] - t[:, None]).clip(0)
        dt[h] = (g[h] ** diff) * le * (256.0 ** -0.5)
        gp[h] = np.broadcast_to((g[h] ** (tau + 1.0))[None, :], (128, 128))
        kd[:, h] = (g[h] ** (seglen - 1.0 - tau)) * (256.0 ** -0.5)
    c['dt'] = dt
    c['gp'] = gp
    c['kd'] = kd
    c['gl'] = [float(g[h] ** seglen) for h in range(4)]
    reset = np.ones((128, 128), np.float32)
    reset[:, tau == 0] = 0.0
    c['reset'] = reset
    nseg = 128 // seglen
    sm = np.zeros((128, nseg), np.float32)
    sm[t, seg] = 1.0
    c['segcol'] = sm
    el = np.zeros((128, 128), np.float32)
    el[t, seg * seglen + seglen - 1] = 1.0
    c['elast'] = el
    return c


def rope_tables(pos):
    inv = 10000.0 ** (-np.arange(128, dtype=np.float32) / 128.0)
    ang = pos[None, :].astype(np.float32) * inv[:, None].astype(np.float32)
    ang = ang.astype(np.float32)
    return np.cos(ang).astype(np.float32), np.sin(ang).astype(np.float32)


def build(seq=SEQ):
    NPT = seq
    NT = NPT + 128
    groups = []
    t0 = 0
    while t0 < NPT:
        tg = min(512, NPT - t0)
        groups.append((t0, tg, 'p'))
        t0 += tg
    groups.append((NPT, 128, 's'))
    NTILE = NT // 128

    nc = bass.Bass("TRN2", target_bir_lowering=False)
    dt_in = lambda n, s: nc.dram_tensor(n, list(s), F32, kind="ExternalInput").ap()
    dt_out = lambda n, s: nc.dram_tensor(n, list(s), F32, kind="ExternalOutput").ap()
    dt_scr = lambda n, s: nc.dram_tensor(n, list(s), F32, kind="Internal").ap()

    I_ = {}
    I_['xT'] = dt_in('xT', [128, 16, NT])
    I_['memT'] = dt_in('memT', [128, 16, 256])
    I_['w_in'] = dt_in('w_in', [DEPTH, NBLK_IN, 128, 16, 128])
    I_['w_br'] = dt_in('w_br', [DEPTH, 3, 16, 128, 8, 128])
    I_['w_out'] = dt_in('w_out', [DEPTH, 16, 128, 16, 128])
    I_['x_wq'] = dt_in('x_wq', [DEPTH, 4, 128, 16, 128])
    I_['x_wkv'] = dt_in('x_wkv', [DEPTH, 8, 128, 16, 128])
    I_['x_wo'] = dt_in('x_wo', [DEPTH, 16, 128, 4, 128])
    I_['ff_w1'] = dt_in('ff_w1', [DEPTH, 64, 128, 16, 128])
    I_['ff_w2'] = dt_in('ff_w2', [DEPTH, 16, 128, 64, 128])
    I_['gains'] = dt_in('gains', [DEPTH, 128, 7, 16])
    I_['rwp'] = dt_in('rwp', [DEPTH, 128, 8, 8])
    I_['rw_mu'] = dt_in('rw_mu', [DEPTH, 128, 27])
    I_['rw_wup'] = dt_in('rw_wup', [DEPTH, 64, 1024])
    I_['rw_aup'] = dt_in('rw_aup', [DEPTH, 64, 1024])
    I_['rw_gup'] = dt_in('rw_gup', [DEPTH, 128, 1024])
    I_['ml_b'] = dt_in('ml_b', [DEPTH, 8, 2])
    I_['s_shift'] = dt_in('s_shift', [DEPTH, 128, 27, SB])
    I_['s_rw'] = dt_in('s_rw', [DEPTH, SB, 8, 128, 64])
    I_['s_mlc'] = dt_in('s_mlc', [DEPTH, SB, 8, 128, 129])
    I_['s_mlm'] = dt_in('s_mlm', [DEPTH, 8, SB])
    I_['s_rt'] = dt_in('s_rt', [DEPTH, SB, 4, 128, 2, 256])
    I_['s_kT'] = dt_in('s_kT', [DEPTH, SB, 4, 128, 256])
    I_['s_v'] = dt_in('s_v', [DEPTH, SB, 128, 2, 512])
    for g_, sl in (('p', 128), ('s', 8)):
        I_['c_le_' + g_] = dt_in('c_le_' + g_, [128, 128])
        I_['c_lt_' + g_] = dt_in('c_lt_' + g_, [128, 128])
        I_['c_gt_' + g_] = dt_in('c_gt_' + g_, [128, 128])
        I_['c_dt_' + g_] = dt_in('c_dt_' + g_, [128, 4, 128])
        I_['c_gp_' + g_] = dt_in('c_gp_' + g_, [128, 4, 128])
        I_['c_kd_' + g_] = dt_in('c_kd_' + g_, [128, 4])
        I_['c_reset_' + g_] = dt_in('c_reset_' + g_, [128, 128])
        I_['c_el_' + g_] = dt_in('c_el_' + g_, [128, 128])
    I_['c_segcol'] = dt_in('c_segcol', [128, 16])
    I_['c_segrow'] = dt_in('c_segrow', [128, 16, 128])
    I_['c_cos'] = dt_in('c_cos', [128, NT])
    I_['c_sin'] = dt_in('c_sin', [128, NT])
    I_['c_ident'] = dt_in('c_ident', [128, 128])
    I_['c_half'] = dt_in('c_half', [128, 128])
    I_['c_sel'] = dt_in('c_sel', [8, 8, 128])

    O_ = {}
    O_['yT'] = dt_out('yT', [128, 16, NT])
    O_['p_shift'] = dt_out('p_shift', [DEPTH, 128, 27])
    O_['p_rw'] = dt_out('p_rw', [DEPTH, 8, 128, 64])
    O_['p_mlc'] = dt_out('p_mlc', [DEPTH, 8, 128, 129])
    O_['p_mlm'] = dt_out('p_mlm', [DEPTH, 8, 1])
    O_['p_rt'] = dt_out('p_rt', [DEPTH, 4, 128, 2, 256])
    O_['p_kT'] = dt_out('p_kT', [DEPTH, 4, 128, 256])
    O_['p_v'] = dt_out('p_v', [DEPTH, 128, 2, 512])
    O_['o_shift'] = dt_out('o_shift', [DEPTH, 128, 27, SB])
    O_['o_rw'] = dt_out('o_rw', [DEPTH, SB, 8, 128, 64])
    O_['o_mlc'] = dt_out('o_mlc', [DEPTH, SB, 8, 128, 129])
    O_['o_mlm'] = dt_out('o_mlm', [DEPTH, 8, SB])
    O_['o_rt'] = dt_out('o_rt', [DEPTH, SB, 4, 128, 2, 256])
    xs = [dt_scr('xs0', [128, 16, NT])]
    BIGW = ['w_in', 'w_br', 'w_out', 'x_wq', 'x_wkv', 'x_wo', 'ff_w1', 'ff_w2']
    WBF = {n: nc.dram_tensor('bf_' + n, list(I_[n].shape), BF16, kind="Internal").ap() for n in BIGW}

    with ExitStack() as es:
        P = Prog(nc, es)
        uid = [0]

        def tile(stk, name, shape, dt=F32):
            uid[0] += 1
            nm = '%s_%d' % (name, uid[0])
            t = stk.enter_context(nc.sbuf_tensor(nm, list(shape), dt))
            return V(nm, t[:])

        def ptile(name, shape, dt=F32):
            t = es.enter_context(nc.psum_tensor(name, list(shape), dt))
            return V(name, t[:])

        def mm(ps, lhsT, rhs, start=True, stop=True):
            P.I('pe', lambda e: e.matmul(ps.a, lhsT=lhsT.a, rhs=rhs.a, start=start, stop=stop),
                reads=[lhsT, rhs], writes=[ps])

        def tr(ps, in_, idn):
            P.I('pe', lambda e: e.transpose(ps.a, in_.a, idn.a), reads=[in_, idn], writes=[ps])

        def act(out, in_, func, bias=None, scale=1.0, eng='act'):
            rd = [in_]
            kw = {}
            if isinstance(bias, V):
                rd.append(bias)
                kw['bias'] = bias.a
            elif bias is not None:
                kw['bias'] = float(bias)
            P.I('act', lambda e: e.activation(out=out.a, in_=in_.a, func=func, scale=scale, **kw),
                reads=rd, writes=[out])

        def tt(out, a, b, op, eng='dve'):
            P.I(eng, lambda e: e.tensor_tensor(out=out.a, in0=a.a, in1=b.a, op=op), reads=[a, b], writes=[out])

        def ts(out, a, s1, op0, s2=None, op1=None, eng='dve'):
            rd = [a]
            v1 = s1
            if isinstance(s1, V):
                rd.append(s1)
                v1 = s1.a
            v2 = s2
            if isinstance(s2, V):
                rd.append(s2)
                v2 = s2.a
            if op1 is None:
                P.I(eng, lambda e: e.tensor_scalar(out=out.a, in0=a.a, scalar1=v1, scalar2=None, op0=op0),
                    reads=rd, writes=[out])
            else:
                P.I(eng, lambda e: e.tensor_scalar(out=out.a, in0=a.a, scalar1=v1, scalar2=v2, op0=op0, op1=op1),
                    reads=rd, writes=[out])

        def stt(out, a, s, b, op0, op1):
            rd = [a, b]
            sv = s
            if isinstance(s, V):
                rd.append(s)
                sv = s.a
            P.I('dve', lambda e: e.scalar_tensor_tensor(out=out.a, in0=a.a, scalar=sv, in1=b.a, op0=op0, op1=op1),
                reads=rd, writes=[out])

        def cp(out, in_, eng='dve'):
            if eng == 'act':
                act(out, in_, AF.Copy)
            else:
                P.I(eng, lambda e: e.tensor_copy(out=out.a, in_=in_.a), reads=[in_], writes=[out])

        def recip(out, in_):
            P.I('dve', lambda e: e.reciprocal(out=out.a, in_=in_.a), reads=[in_], writes=[out])

        def memset(t, val, eng='pool'):
            P.I(eng, lambda e: e.memset(t.a, val), writes=[t])

        def rsqrt_from(out, in_, scale, bias):
            act(out, in_, AF.Sqrt, bias=bias, scale=scale)
            recip(out, out)

        def ld(out, src, q='sp', **kw):
            P.dma(q, out.a, src, writes=[out], **kw)

        def st(dst, in_, q='sp', **kw):
            P.dma(q, dst, in_.a, reads=[in_], **kw)

        for n_ in BIGW:
            src = I_[n_]
            dst = WBF[n_]
            lead = list(src.shape[:-3])
            idxs = [()]
            for d_ in lead:
                idxs = [i + (j,) for i in idxs for j in range(d_)]
            for ix in idxs:
                sa, da = src, dst
                for j in ix:
                    sa = sa[j]
                    da = da[j]
                P.dma('pool', da, sa, writes=['wbf_' + n_], max_dma_last_dim=4096)
        P.barrier()

        hT = tile(es, 'hT', [128, 16, 512], BF16)
        postT = tile(es, 'postT', [128, 16, 512], F32)
        wbuf = [tile(es, 'wbuf%d' % i, [128, 8192], BF16) for i in range(2)]
        wi = [0]
        psA = [ptile('psA%d' % i, [128, 512]) for i in range(3)]
        pai = [0]
        psB = [ptile('psB%d' % i, [128, 512]) for i in range(5)]
        pbi = [0]

        def nextA():
            pai[0] = (pai[0] + 1) % 3
            return psA[pai[0]]

        def nextB():
            pbi[0] = (pbi[0] + 1) % 4
            return psB[pbi[0]]

        ident = tile(es, 'ident', [128, 128])
        ident_b = tile(es, 'ident_b', [128, 128], BF16)
        ones_b = tile(es, 'ones_b', [128, 128], BF16)
        half_b = tile(es, 'half_b', [128, 128], BF16)
        sel = tile(es, 'sel', [8, 8, 128])
        zero8 = tile(es, 'zero8', [8, 128])
        ld(ident, I_['c_ident'])
        cp(ident_b, ident)
        memset(ones_b, 1.0)
        memset(zero8, 0.0)
        P.dma('pool', half_b.a, I_['c_half'], writes=[half_b])
        ld(sel, I_['c_sel'])
        C = {}
        for g_ in ('p', 's'):
            C[g_] = {}
            for nm, shp in (('le', [128, 128]), ('lt', [128, 128]), ('gt', [128, 128]), ('dt', [128, 4, 128]),
                            ('gp', [128, 4, 128]), ('kd', [128, 4]), ('reset', [128, 128]), ('el', [128, 128])):
                C[g_][nm] = tile(es, 'c_%s_%s' % (nm, g_), shp)
                ld(C[g_][nm], I_['c_%s_%s' % (nm, g_)])
        segcol = tile(es, 'segcol', [128, 16])
        ld(segcol, I_['c_segcol'])
        gains = tile(es, 'gains', [128, 7, 16])
        rwp = tile(es, 'rwp', [128, 8, 8])
        negw0 = tile(es, 'negw0', [128, 8])
        mu = tile(es, 'mu', [128, 27])
        mlb = tile(es, 'mlb', [8, 2])
        carry = tile(es, 'carry', [128, 27])
        Trw = tile(es, 'Trw', [128, 8, 64])
        Cml = tile(es, 'Cml', [128, 8, 129])
        mml = tile(es, 'mml', [8, 1])
        Srt = tile(es, 'Srt', [128, 4, 2, 256])
        SrtB = tile(es, 'SrtB', [128, 4, 2, 256], BF16)
        CmlB = tile(es, 'CmlB', [128, 8, 128], BF16)
        TrwB = tile(es, 'TrwB', [128, 8, 64], BF16)
        NrepB = tile(es, 'NrepB', [128, 8, 128], BF16)

        def proj(Wd, blocks, KB, rhs_fn, rhs_keys, ntok, evac):
            for bi, blk in enumerate(blocks):
                wb = wbuf[wi[0] % 2]
                wi[0] += 1
                wv = V(wb.k, wb.a[:, 0:KB * 128].rearrange("p (k c) -> p k c", c=128))
                P.dma('sp', wv.a, Wd[blk], writes=[wb])
                ps = nextA()
                for kb in range(KB):
                    P.I('pe', lambda e, kb=kb: e.matmul(ps.a[:, 0:ntok], lhsT=wv.a[:, kb, :], rhs=rhs_fn(kb),
                                                        start=(kb == 0), stop=(kb == KB - 1)),
                        reads=[wb] + list(rhs_keys), writes=[ps])
                evac(bi, ps[:, 0:ntok])

        def fm_norm(src, gidx, TG, dst=None, resid=None, stk=None):
            sq = hT[:, :, 0:TG]
            act(sq, src, AF.Square)
            ps = nextB()
            for kb in range(16):
                mm(ps[:, 0:TG], ones_b, sq[:, kb, :], start=(kb == 0), stop=(kb == 15))
            rstd = tile(stk, 'rstd', [128, TG])
            rsqrt_from(rstd, ps[:, 0:TG], 1.0 / D, EPS)
            for kb in range(16):
                if resid is None:
                    stt(dst[:, kb, :], src[:, kb, :], gains[:, gidx, kb:kb + 1], rstd, ALU.mult, ALU.mult)
                else:
                    stt(src[:, kb, :], src[:, kb, :], gains[:, gidx, kb:kb + 1], rstd, ALU.mult, ALU.mult)
                    tt(resid[:, kb, :], resid[:, kb, :], src[:, kb, :], ALU.add, eng='pool')

        def sub_begin(xcur, t0, TG, gidx):
            with ExitStack() as stk:
                xT = tile(stk, 'xT', [128, 16, TG])
                ld(xT, xcur[:, :, t0:t0 + TG])
                fm_norm(xT, gidx, TG, dst=hT[:, :, 0:TG], stk=stk)
                P.barrier()

        def sub_end(xcur, xnext, t0, TG, gidx):
            with ExitStack() as stk:
                xT = tile(stk, 'xT', [128, 16, TG])
                ld(xT, xcur[:, :, t0:t0 + TG])
                fm_norm(postT[:, :, 0:TG], gidx, TG, resid=xT, stk=stk)
                st(xnext[:, :, t0:t0 + TG], xT)
                P.barrier()

        def scan(out, d0, d1, init, op0, op1):
            rd = [d0, d1]
            iv = init
            if isinstance(init, V):
                rd.append(init)
                iv = init.a
            P.I('dve', lambda e: e.tensor_tensor_scan(out=out.a, data0=d0.a, data1=d1.a, initial=iv, op0=op0, op1=op1),
                reads=rd, writes=[out])

        def mlstm(l, t0, TG, kind, NCH, nseg, slen, Cg, ysT, last_group):
            with ExitStack() as ms:
                gi = tile(ms, 'gi', [8, TG])
                gf = tile(ms, 'gf', [8, TG])

                def ev_i(bi, ps):
                    act(gi, ps[0:8, :], AF.Tanh, bias=mlb[:, 0:1], scale=1.0 / 15.0)

                def ev_f(bi, ps):
                    act(gf, ps[0:8, :], AF.Tanh, bias=mlb[:, 1:2], scale=1.0 / 15.0)
                proj(WBF['w_in'][l], [ML0 + 32], 16, lambda kb: hT.a[:, kb, 0:TG], [hT], TG, ev_i)
                proj(WBF['w_in'][l], [ML0 + 33], 16, lambda kb: hT.a[:, kb, 0:TG], [hT], TG, ev_f)
                ts(gi, gi, 15.0, ALU.mult)
                act(gf, gf, AF.Exp, scale=-15.0)
                act(gf, gf, AF.Ln, bias=1.0)
                ts(gf, gf, -1.0, ALU.mult)
                R_all = tile(ms, 'R_all', [8, NCH, 3, 128])
                gT_all = tile(ms, 'gT_all', [128, NCH, 8])
                bb = tile(ms, 'mbb', [8, 128])
                gg = tile(ms, 'mgg', [8, 128])
                cm = tile(ms, 'mcm', [8, 128])
                mm_ = tile(ms, 'mmm', [8, 128])
                m0tok = tile(ms, 'm0tok', [8, 128])
                if kind == 'p':
                    m0s = mml
                else:
                    m0s = tile(ms, 'm0s', [8, SB])
                    ld(m0s, I_['s_mlm'][l])
                for ch in range(NCH):
                    cs = slice(ch * 128, (ch + 1) * 128)
                    scan(bb, Cg['reset'][0:8, :], gf[:, cs], 0.0, ALU.mult, ALU.add)
                    tt(gg, gi[:, cs], bb, ALU.subtract)
                    for seg in range(nseg):
                        c0 = seg * slen
                        scan(cm[:, c0:c0 + slen], zero8[:, 0:slen], gg[:, c0:c0 + slen], m0s[:, seg:seg + 1],
                             ALU.add, ALU.max)
                        cp(m0tok[:, c0:c0 + slen], m0s[:, seg:seg + 1].bc([8, slen]))
                    tt(mm_, bb, cm, ALU.add)
                    ts(R_all[:, ch, 0, :], cm, -1.0, ALU.mult)
                    tt(m0tok, m0tok, cm, ALU.subtract)
                    act(R_all[:, ch, 1, :], m0tok, AF.Exp)
                    act(R_all[:, ch, 2, :], mm_, AF.Exp, scale=-1.0)
                    pst = nextB()
                    tr(pst[:, 0:8], gg, ident[0:8, 0:8])
                    cp(gT_all[:, ch, :], pst[:, 0:8])
                    for seg in range(nseg):
                        c0 = seg * slen
                        cp(m0s[:, seg:seg + 1], mm_[:, c0 + slen - 1:c0 + slen])
                if kind == 'p':
                    if last_group:
                        st(O_['p_mlm'][l], mml)
                else:
                    st(O_['o_mlm'][l], m0s)
                z4 = tile(ms, 'z4', [128, 4, TG])
                qb = tile(ms, 'mqb', [128, 128], BF16)
                kb_ = tile(ms, 'mkb', [128, 128], BF16)
                ktf = tile(ms, 'mktf', [128, 128])
                kw = tile(ms, 'mkw', [128, 128], BF16)
                vext = tile(ms, 'mvext', [128, 129], BF16)
                vm = tile(ms, 'mvm', [128, 129], BF16)
                memset(vext, 1.0)
                bcs = tile(ms, 'mbcs', [128, 3, 128])
                ex = tile(ms, 'mex', [128, 128])
                Dx = tile(ms, 'mDx', [128, 128])
                w1 = tile(ms, 'mw1', [128, 128])
                wts = tile(ms, 'mwts', [128, 128], BF16)
                cd = tile(ms, 'mcd', [128, 2, 128])
                num = tile(ms, 'mnum', [128, 128])
                den = tile(ms, 'mden', [128, 128])
                sq = tile(ms, 'msq', [128, 128], BF16)
                rstd = tile(ms, 'mrstd', [128, 128])
                sgo = tile(ms, 'msgo', [128, 128])
                wend = tile(ms, 'mwend', [128, 1])
                if kind == 's':
                    Cxs = tile(ms, 'Cxs', [128, SB, 129])
                    CBs = tile(ms, 'CBs', [128, SB, 128], BF16)
                    NBs = tile(ms, 'NBs', [128, SB, 128], BF16)
                for h in range(8):
                    def ev(bi, ps):
                        cp(z4[:, bi, :], ps, eng='act')
                    proj(WBF['w_in'][l], [ML0 + h, ML0 + 8 + h, ML0 + 16 + h, ML0 + 24 + h], 16,
                         lambda kb: hT.a[:, kb, 0:TG], [hT], TG, ev)
                    if kind == 's':
                        ld(Cxs, I_['s_mlc'][l, :, h].rearrange("b p e -> p b e"))
                        cp(CBs, Cxs[:, :, 0:128], eng='pool')
                        cp(NBs, Cxs[:, :, 128:129].bc([128, SB, 128]), eng='pool')
                    for ch in range(NCH):
                        cs = slice(ch * 128, (ch + 1) * 128)
                        cp(qb, z4[:, 0, cs], eng='act')
                        ts(kb_, z4[:, 1, cs], 128.0 ** -0.5, ALU.mult, eng='pool')
                        pst = nextB()
                        tr(pst[:, 0:128], z4[:, 1, cs], ident)
                        ts(ktf, pst[:, 0:128], 128.0 ** -0.5, ALU.mult)
                        pst = nextB()
                        tr(pst[:, 0:128], z4[:, 2, cs], ident)
                        cp(vext[:, 0:128], pst[:, 0:128], eng='act')
                        psb = nextB()
                        mm(psb[:, 0:384], sel[:, h, :], V(R_all.k, R_all.a[:, ch].rearrange("p a b -> p (a b)")))
                        cp(bcs, V(psb.k, psb.a[:, 0:384].rearrange("p (a b) -> p a b", b=128)), eng='act')
                        pss = nextB()
                        mm(pss[:, 0:128], kb_, qb)
                        ts(ex, bcs[:, 0, :], gT_all[:, ch, h:h + 1], ALU.add, 0.0, ALU.min)
                        act(Dx, ex, AF.Exp)
                        tt(w1, Dx, Cg['le'], ALU.mult, eng='pool')
                        tt(wts, w1, pss[:, 0:128], ALU.mult)
                        psn = nextB()
                        mm(psn[:, 0:128], vext[:, 0:128], wts)
                        mm(psn[:, 128:256], ones_b, wts)
                        for seg in range(nseg):
                            c0 = seg * slen
                            CB = CmlB[:, h, :] if kind == 'p' else CBs[:, seg, :]
                            NB = NrepB[:, h, :] if kind == 'p' else NBs[:, seg, :]
                            mm(psn[:, 256 + c0:256 + c0 + slen], CB, qb[:, c0:c0 + slen])
                            mm(psn[:, 384 + c0:384 + c0 + slen], NB, qb[:, c0:c0 + slen])
                        cp(cd, V(psn.k, psn.a[:, 256:512].rearrange("p (a b) -> p a b", b=128)), eng='act')
                        tt(num, cd[:, 0, :], bcs[:, 1, :], ALU.mult)
                        tt(num, num, psn[:, 0:128], ALU.add)
                        tt(den, cd[:, 1, :], bcs[:, 1, :], ALU.mult)
                        tt(den, den, psn[:, 128:256], ALU.add)
                        stt(den, den, -1.0, den, ALU.mult, ALU.max)
                        tt(den, den, bcs[:, 2, :], ALU.max)
                        recip(den, den)
                        tt(num, num, den, ALU.mult)
                        act(sq, num, AF.Square)
                        psr = nextB()
                        mm(psr[:, 0:128], ones_b, sq)
                        rsqrt_from(rstd, psr[:, 0:128], 1.0 / 128.0, EPS)
                        act(sgo, z4[:, 3, cs], AF.Sigmoid)
                        stt(num, num, rwp[:, 7, h:h + 1], rstd, ALU.mult, ALU.mult)
                        tt(ysT[:, 8 + h, cs], num, sgo, ALU.mult)
                        tt(w1, Dx, Cg['el'], ALU.mult, eng='pool')
                        P.I('dve', lambda e: e.reduce_sum(out=wend.a, in_=w1.a, axis=AX.X), reads=[w1], writes=[wend])
                        ts(kw, ktf, wend[:, 0:1], ALU.mult)
                        for seg in range(nseg):
                            c0 = seg * slen
                            if nseg > 1:
                                ts(vm, vext, segcol[:, seg:seg + 1], ALU.mult, eng='pool')
                                v_ = vm
                            else:
                                v_ = vext
                            pd = nextB()
                            mm(pd[:, 0:129], kw, v_)
                            Cx = Cml[:, h, :] if kind == 'p' else Cxs[:, seg, :]
                            stt(Cx, Cx, bcs[:, 1, c0 + slen - 1:c0 + slen], pd[:, 0:129], ALU.mult, ALU.add)
                        if kind == 'p':
                            cp(CmlB[:, h, :], Cml[:, h, 0:128], eng='pool')
                            cp(NrepB[:, h, :], Cml[:, h, 128:129].bc([128, 128]), eng='pool')
                            if last_group and ch == NCH - 1:
                                st(O_['p_mlc'][l, h], Cml[:, h, :])
                    if kind == 's':
                        st(O_['o_mlc'][l, :, h].rearrange("b p e -> p b e"), Cxs)

        def rwkv(l, t0, TG, kind, NCH, nseg, slen, Cg, ysT, last_group):
            nlev = int(round(math.log2(slen))) - 1
            with ExitStack() as ws:
                pv = tile(ws, 'pv', [128, TG])
                if kind == 's':
                    sh0 = tile(ws, 'sh0', [128, 27, SB])
                    osh = tile(ws, 'osh', [128, 27, SB])
                    ld(sh0, I_['s_shift'][l])
                    segrow = tile(ws, 'segrow', [128, SB, 128])
                    ld(segrow, I_['c_segrow'])
                    atm_all = tile(ws, 'atm_all', [128, SB, 128], BF16)
                    Ts = tile(ws, 'Ts', [128, SB, 64])
                    TsB = tile(ws, 'TsB', [128, SB, 64], BF16)
                    ktm = tile(ws, 'ktm_', [128, 128], BF16)
                    btm = tile(ws, 'btm_', [128, 128], BF16)

                def tshift(u, blk):
                    if kind == 'p':
                        cp(pv[:, 0:1], carry[:, blk:blk + 1], eng='pool')
                        cp(pv[:, 1:TG], u[:, 0:TG - 1], eng='pool')
                        cp(carry[:, blk:blk + 1], u[:, TG - 1:TG], eng='pool')
                    else:
                        u3d = V(u.k, u.a.rearrange("p (b t) -> p b t", t=8))
                        p3d = V(pv.k, pv.a.rearrange("p (b t) -> p b t", t=8))
                        cp(p3d[:, :, 0:1], V(sh0.k, sh0.a[:, blk, :].unsqueeze(2)), eng='pool')
                        cp(p3d[:, :, 1:8], u3d[:, :, 0:7], eng='pool')
                        cp(V(osh.k, osh.a[:, blk, :].unsqueeze(2)), u3d[:, :, 7:8], eng='pool')
                    tt(pv, pv, u, ALU.subtract, eng='pool')
                    stt(u, pv, mu[:, blk:blk + 1], u, ALU.mult, ALU.add)

                u3 = tile(ws, 'u3', [128, 3, TG])

                def ev3(bi, ps):
                    cp(u3[:, bi, :], ps, eng='act')
                proj(WBF['w_in'][l], [RW0 + 24, RW0 + 25, RW0 + 26], 16, lambda kb: hT.a[:, kb, 0:TG], [hT], TG, ev3)
                for bi in range(3):
                    tshift(u3[:, bi, :], 24 + bi)
                tw = tile(ws, 'tw', [64, TG], BF16)
                ab = tile(ws, 'ab', [64, TG], BF16)
                sgd = tile(ws, 'sgd', [128, TG], BF16)
                act(tw, u3[0:64, 0, :], AF.Tanh)
                cp(ab, u3[0:64, 1, :])
                act(sgd, u3[:, 2, :], AF.Sigmoid)
                rkv = tile(ws, 'rkv', [128, 3, TG])
                e2 = tile(ws, 'e2', [128, TG])
                a_ = tile(ws, 'a_', [128, TG])
                g_ = tile(ws, 'g_', [128, TG])
                kk = tile(ws, 'kk', [128, TG])
                kkn = tile(ws, 'kkn', [128, TG])
                kmod = tile(ws, 'kmod', [128, TG])
                bvec = tile(ws, 'bvec', [128, TG])
                bon = tile(ws, 'bon', [128, TG])
                tmp = tile(ws, 'rtmp', [128, TG])
                tb = tile(ws, 'rtb', [128, TG], BF16)
                c = {}
                for nm in ('cwp', 'ecn', 'ecp', 'ktf', 'btf', 'dd', 'PTf', 'Pf', 'yf', 'cen', 'rs', 'tmpT'):
                    c[nm] = tile(ws, 'rw_' + nm, [128, 128])
                for nm in ('rt', 'kt', 'bt', 'at', 'kt_tok', 'bt_tok', 'v_tok', 'MT', 'Aqk', 'Aqb', 'yb', 'cb'):
                    c[nm] = tile(ws, 'rw_' + nm, [128, 128], BF16)
                for nm in ('X', 'Nn', 'X2', 'N2', 'Xb', 'Nb', 'PTb', 'Pb'):
                    c[nm] = tile(ws, 'rw_' + nm, [128, 128], F32)
                Gb = tile(ws, 'rw_Gb', [128, 64], F32)
                Uneg = tile(ws, 'rw_Un', [128, 64], BF16)
                for p in range(8):
                    pc = slice(p * 128, (p + 1) * 128)

                    def evr(bi, ps):
                        cp(rkv[:, bi, :], ps, eng='act')
                    proj(WBF['w_in'][l], [RW0 + p, RW0 + 8 + p, RW0 + 16 + p], 16,
                         lambda kb: hT.a[:, kb, 0:TG], [hT], TG, evr)
                    for bi in range(3):
                        tshift(rkv[:, bi, :], bi * 8 + p)
                    r = rkv[:, 0, :]
                    k = rkv[:, 1, :]
                    v = rkv[:, 2, :]
                    psw = nextB()
                    mm(psw[:, 0:TG], wup[:, pc], tw)
                    act(tmp, psw[:, 0:TG], AF.Exp, scale=-1.0, bias=negw0[:, p:p + 1])
                    act(tmp, tmp, AF.Ln, bias=1.0)
                    act(e2, tmp, AF.Exp, scale=-1.0, bias=-0.5)
                    psa = nextB()
                    mm(psa[:, 0:TG], aup[:, pc], ab)
                    act(a_, psa[:, 0:TG], AF.Sigmoid, bias=rwp[:, 1, p:p + 1])
                    psg = nextB()
                    mm(psg[:, 0:TG], gup[:, pc], sgd)
                    cp(g_, psg[:, 0:TG], eng='act')
                    ts(kk, k, rwp[:, 2, p:p + 1], ALU.mult)
                    act(tb, kk, AF.Square)
                    pss = nextB()
                    mm(pss[:, 0:TG], half_b, tb)
                    act(tmp, pss[:, 0:TG], AF.Sqrt)
                    ts(tmp, tmp, 1e-12, ALU.max)
                    recip(tmp, tmp)
                    tt(kkn, kk, tmp, ALU.mult)
                    ts(tmp, a_, -1.0, ALU.add, rwp[:, 3, p:p + 1], ALU.mult)
                    ts(tmp, tmp, 1.0, ALU.add)
                    tt(kmod, k, tmp, ALU.mult)
                    tt(bvec, kkn, a_, ALU.mult, eng='pool')
                    tt(tmp, r, kmod, ALU.mult)
                    ts(tb, tmp, rwp[:, 4, p:p + 1], ALU.mult)
                    psb = nextB()
                    mm(psb[:, 0:TG], half_b, tb)
                    tt(bon, psb[:, 0:TG], v, ALU.mult)
                    if kind == 's':
                        ld(Ts, I_['s_rw'][l, :, p].rearrange("b q v -> q b v"))
                        cp(TsB, Ts, eng='pool')
                    for ch in range(NCH):
                        cs = slice(ch * 128, (ch + 1) * 128)
                        scan(c['cwp'], Cg['reset'], e2[:, cs], 0.0, ALU.mult, ALU.add)
                        act(c['ecn'], c['cwp'], AF.Exp, scale=-1.0)
                        act(c['ecp'], c['cwp'], AF.Exp)
                        tt(c['rt'], r[:, cs], c['ecn'], ALU.mult)
                        tt(c['ktf'], kmod[:, cs], c['ecp'], ALU.mult)
                        cp(c['kt'], c['ktf'], eng='pool')
                        tt(c['btf'], bvec[:, cs], c['ecp'], ALU.mult)
                        cp(c['bt'], c['btf'], eng='pool')
                        tt(c['dd'], e2[:, cs], c['cwp'], ALU.subtract, eng='pool')
                        act(c['dd'], c['dd'], AF.Exp)
                        tt(c['at'], kkn[:, cs], c['dd'], ALU.mult)
                        for src, dst in ((c['ktf'], c['kt_tok']), (c['btf'], c['bt_tok']), (v[:, cs], c['v_tok'])):
                            pst = nextB()
                            tr(pst[:, 0:128], src, ident)
                            cp(dst, pst[:, 0:128], eng='act')
                        if kind == 's':
                            tt(atm_all, V(c['at'].k, c['at'].a.unsqueeze(1).to_broadcast([128, SB, 128])), segrow,
                               ALU.mult, eng='pool')
                        psY = psB[4]
                        for j in range(2):
                            pr = slice(j * 64, (j + 1) * 64)
                            jc = slice(j * 64, (j + 1) * 64)
                            psN = nextB()
                            mm(psN[:, 0:128], c['bt'][pr, :], c['at'][pr, :])
                            mm(psN[:, 128:256], c['at'][pr, :], c['bt'][pr, :])
                            mm(psN[:, 256:384], c['kt'][pr, :], c['at'][pr, :])
                            tt(c['X'], psN[:, 0:128], Cg['lt'], ALU.mult)
                            tt(c['Nn'], psN[:, 128:256], Cg['gt'], ALU.mult)
                            tt(c['MT'], psN[:, 256:384], Cg['lt'], ALU.mult)
                            psQ = nextB()
                            mm(psQ[:, 0:128], c['kt'][pr, :], c['rt'][pr, :])
                            mm(psQ[:, 128:256], c['bt'][pr, :], c['rt'][pr, :])
                            tt(c['Aqk'], psQ[:, 0:128], Cg['le'], ALU.mult)
                            tt(c['Aqb'], psQ[:, 128:256], Cg['le'], ALU.mult)
                            psG = nextB()
                            mm(psG[:, 0:64], c['MT'], c['v_tok'][:, jc], start=True, stop=False)
                            for seg in range(nseg):
                                lhs = c['at'][pr, :] if kind == 'p' else atm_all[pr, seg, :]
                                T0b = TrwB[pr, p, :] if kind == 'p' else TsB[pr, seg, :]
                                mm(psG[:, 0:64], lhs, T0b, start=False, stop=(seg == nseg - 1))
                            cp(Gb, psG[:, 0:64], eng='act')
                            tt(c['PTf'], ident, c['X'], ALU.subtract, eng='pool')
                            tt(c['Pf'], ident, c['Nn'], ALU.subtract, eng='pool')
                            cp(c['PTb'], c['PTf'], eng='pool')
                            cp(c['Pb'], c['Pf'], eng='pool')
                            Xk, Nk = c['X'], c['Nn']
                            X2s = (c['X2'], c['Xb'])
                            N2s = (c['N2'], c['Nb'])
                            for lv in range(nlev):
                                X2 = X2s[lv % 2]
                                N2 = N2s[lv % 2]
                                psq = nextB()
                                mm(psq[:, 0:128], Nk, Xk)
                                mm(psq[:, 128:256], Xk, Nk)
                                cp(X2, psq[:, 0:128], eng='act')
                                cp(N2, psq[:, 128:256])
                                psp = nextB()
                                mm(psp[:, 0:128], c['Pb'], X2)
                                mm(psp[:, 128:256], X2, c['Pb'])
                                tt(c['PTf'], c['PTf'], psp[:, 0:128], ALU.add)
                                tt(c['Pf'], c['Pf'], psp[:, 128:256], ALU.add)
                                cp(c['PTb'], c['PTf'], eng='pool')
                                cp(c['Pb'], c['Pf'], eng='pool')
                                Xk, Nk = X2, N2
                            psU = nextB()
                            mm(psU[:, 0:64], c['PTb'], Gb)
                            ts(Uneg, psU[:, 0:64], -1.0, ALU.mult)
                            mm(psY[pr, 0:128], c['v_tok'][:, jc], c['Aqk'], start=True, stop=False)
                            mm(psY[pr, 0:128], Uneg, c['Aqb'], start=False, stop=False)
                            for seg in range(nseg):
                                c0 = seg * slen
                                T0b = TrwB[pr, p, :] if kind == 'p' else TsB[pr, seg, :]
                                mm(psY[pr, c0:c0 + slen], T0b, c['rt'][pr, c0:c0 + slen], start=False,
                                   stop=(seg == nseg - 1))
                            for seg in range(nseg):
                                c0 = seg * slen
                                if nseg > 1:
                                    ts(ktm, c['kt_tok'], segcol[:, seg:seg + 1], ALU.mult, eng='pool')
                                    ts(btm, c['bt_tok'], segcol[:, seg:seg + 1], ALU.mult, eng='pool')
                                    k_, b_ = ktm, btm
                                else:
                                    k_, b_ = c['kt_tok'], c['bt_tok']
                                psD = nextB()
                                mm(psD[pr, 0:64], k_[:, pr], c['v_tok'][:, jc], start=True, stop=False)
                                mm(psD[pr, 0:64], b_[:, pr], Uneg, start=False, stop=True)
                                Tf = Trw[pr, p, :] if kind == 'p' else Ts[pr, seg, :]
                                tt(c['tmpT'][pr, 0:64], Tf, psD[pr, 0:64], ALU.add)
                                ts(Tf, c['tmpT'][pr, 0:64], c['ecn'][pr, c0 + slen - 1:c0 + slen], ALU.mult)
                        if kind == 'p':
                            cp(TrwB[:, p, :], Trw[:, p, :], eng='pool')
                            if last_group and ch == NCH - 1:
                                st(O_['p_rw'][l, p], Trw[:, p, :])
                        cp(c['yf'], psY[:, 0:128], eng='act')
                        cp(c['yb'], c['yf'], eng='pool')
                        psm = nextB()
                        mm(psm[:, 0:128], half_b, c['yb'])
                        stt(c['cen'], psm[:, 0:128], -1.0 / 64.0, c['yf'], ALU.mult, ALU.add)
                        act(c['cb'], c['cen'], AF.Square)
                        psv = nextB()
                        mm(psv[:, 0:128], half_b, c['cb'])
                        rsqrt_from(c['rs'], psv[:, 0:128], 1.0 / 64.0, RW_GN_EPS)
                        tt(c['cen'], c['cen'], c['rs'], ALU.mult)
                        ts(c['cen'], c['cen'], rwp[:, 5, p:p + 1], ALU.mult, rwp[:, 6, p:p + 1], ALU.add)
                        tt(c['cen'], c['cen'], bon[:, cs], ALU.add)
                        tt(ysT[:, p, cs], c['cen'], g_[:, cs], ALU.mult)
                    if kind == 's':
                        st(O_['o_rw'][l, :, p].rearrange("b q v -> q b v"), Ts)
                if kind == 'p':
                    if last_group:
                        st(O_['p_shift'][l], carry)
                else:
                    st(O_['o_shift'][l], osh)

        def mixer(l, t0, TG, kind, NCH, nseg, slen, Cg, ysT, last_group):
            rwkv(l, t0, TG, kind, NCH, nseg, slen, Cg, ysT, last_group)
            P.barrier()
            mlstm(l, t0, TG, kind, NCH, nseg, slen, Cg, ysT, last_group)
            P.barrier()
            gl = make_consts(slen)['gl']
            with ExitStack() as rs:
                z = tile(rs, 'rtz', [128, 8, TG])
                cosT = tile(rs, 'cosT', [128, TG])
                sinT = tile(rs, 'sinT', [128, TG])
                ld(cosT, I_['c_cos'][:, t0:t0 + TG])
                ld(sinT, I_['c_sin'][:, t0:t0 + TG])
                t1 = tile(rs, 't1', [128, 128])
                t2 = tile(rs, 't2', [128, 128])
                t3 = tile(rs, 't3', [128, 128])
                t4 = tile(rs, 't4', [128, 128])
                qrf = tile(rs, 'qrf', [128, 2, 128])
                krf = tile(rs, 'krf', [128, 2, 128])
                qr = tile(rs, 'qr', [128, 2, 128], BF16)
                kr = tile(rs, 'kr', [128, 2, 128], BF16)
                qs = tile(rs, 'qs', [128, 2, 128], BF16)
                ktok = tile(rs, 'ktok', [128, 256], BF16)
                ktm = tile(rs, 'ktm', [128, 256], BF16)
                vtok = tile(rs, 'vtok', [128, 256], BF16)
                AT = tile(rs, 'AT', [128, 128], BF16)
                yc = tile(rs, 'yc', [128, 2, 128])
                ysum = tile(rs, 'ysum', [128, 2, 128])
                ysq = tile(rs, 'ysq', [128, 2, 128], BF16)
                rstd = tile(rs, 'rrstd', [128, 128])
                sg = tile(rs, 'rsg', [128, 2, 128])
                Ssm = [tile(rs, 'Ssm%d' % i, [128, 2, 256]) for i in range(2)]
                Sbs = [tile(rs, 'Sbs%d' % i, [128, 2, 256], BF16) for i in range(2)]
                for h in range(4):
                    blocks = [RT0 + 2 * h, RT0 + 2 * h + 1, RT0 + 8 + 2 * h, RT0 + 9 + 2 * h,
                              RT0 + 16 + 2 * h, RT0 + 17 + 2 * h, RT0 + 24 + 2 * h, RT0 + 25 + 2 * h]

                    def ev(bi, ps):
                        cp(z[:, bi, :], ps, eng='act')
                    proj(WBF['w_in'][l], blocks, 16, lambda kb: hT.a[:, kb, 0:TG], [hT], TG, ev)
                    for ch in range(NCH):
                        cs = slice(ch * 128, (ch + 1) * 128)
                        for s0, dstf, dstb in ((0, qrf, qr), (2, krf, kr)):
                            x1 = z[:, s0, cs]
                            x2 = z[:, s0 + 1, cs]
                            tt(t1, x1, cosT[:, cs], ALU.mult)
                            tt(t2, x2, sinT[:, cs], ALU.mult)
                            tt(dstf[:, 0, :], t1, t2, ALU.subtract)
                            tt(t3, x1, sinT[:, cs], ALU.mult, eng='pool')
                            tt(t4, x2, cosT[:, cs], ALU.mult, eng='pool')
                            tt(dstf[:, 1, :], t3, t4, ALU.add, eng='pool')
                            cp(dstb, dstf, eng='act')
                        for b in range(2):
                            tt(qs[:, b, :], qrf[:, b, :], Cg['gp'][:, h, :], ALU.mult)
                        for b in range(2):
                            pst = nextB()
                            tr(pst[:, 0:128], krf[:, b, :], ident)
                            ts(ktok[:, b * 128:(b + 1) * 128], pst[:, 0:128], Cg['kd'][:, h:h + 1], ALU.mult)
                            pst = nextB()
                            tr(pst[:, 0:128], z[:, 4 + b, cs], ident)
                            cp(vtok[:, b * 128:(b + 1) * 128], pst[:, 0:128], eng='act')
                        pss = nextB()
                        for b in range(2):
                            mm(pss[:, 0:128], kr[:, b, :], qr[:, b, :], start=(b == 0), stop=(b == 1))
                        tt(AT, pss[:, 0:128], Cg['dt'][:, h, :], ALU.mult)
                        psy = psB[4]
                        for eb in range(2):
                            mm(psy[:, eb * 128:(eb + 1) * 128], vtok[:, eb * 128:(eb + 1) * 128], AT)
                        for seg in range(nseg):
                            if kind == 'p':
                                S = Srt[:, h]
                                Sb = SrtB[:, h]
                            else:
                                S = Ssm[seg % 2]
                                Sb = Sbs[seg % 2]
                                ld(S, I_['s_rt'][l, seg, h])
                                cp(Sb, S, eng='pool')
                            c0 = seg * slen
                            for eb in range(2):
                                for db in range(2):
                                    mm(psy[:, 256 + eb * 128 + c0:256 + eb * 128 + c0 + slen],
                                       Sb[:, db, eb * 128:(eb + 1) * 128], qs[:, db, c0:c0 + slen],
                                       start=(db == 0), stop=(db == 1))
                            if nseg > 1:
                                ts(ktm, ktok, segcol[:, seg:seg + 1], ALU.mult)
                                kt_ = ktm
                            else:
                                kt_ = ktok
                            for db in range(2):
                                pd = nextB()
                                mm(pd[:, 0:256], kt_[:, db * 128:(db + 1) * 128], vtok)
                                stt(S[:, db, :], S[:, db, :], gl[h], pd[:, 0:256], ALU.mult, ALU.add)
                            if kind == 'p':
                                cp(Sb, S, eng='pool')
                                if last_group and ch == NCH - 1:
                                    st(O_['p_rt'][l, h], S)
                            else:
                                st(O_['o_rt'][l, seg, h], S)
                        cp(yc, V(psy.k, psy.a[:, 256:512].rearrange("p (e t) -> p e t", t=128)), eng='act')
                        tt(ysum, V(psy.k, psy.a[:, 0:256].rearrange("p (e t) -> p e t", t=128)), yc, ALU.add)
                        act(ysq, ysum, AF.Square)
                        pn = nextB()
                        for eb in range(2):
                            mm(pn[:, 0:128], ones_b, ysq[:, eb, :], start=(eb == 0), stop=(eb == 1))
                        rsqrt_from(rstd, pn[:, 0:128], 1.0 / 256.0, EPS)
                        act(sg, z[:, 6:8, cs], AF.Sigmoid)
                        tt(sg, sg, z[:, 6:8, cs], ALU.mult, eng='pool')
                        for eb in range(2):
                            tt(ysum[:, eb, :], ysum[:, eb, :], rstd, ALU.mult)
                            tt(ysT[:, 16 + 2 * h + eb, cs], ysum[:, eb, :], sg[:, eb, :], ALU.mult)

        for l in range(DEPTH):
            xcur = I_['xT'] if l == 0 else xs[0]
            xmid = xs[0] if l == 0 else xs[0]
            ld(gains, I_['gains'][l])
            ld(rwp, I_['rwp'][l])
            ld(mu, I_['rw_mu'][l])
            ld(mlb, I_['ml_b'][l])
            ts(negw0, rwp[:, 0, :], -1.0, ALU.mult)
            ts(mlb, mlb, 1.0 / 15.0, ALU.mult)
            memset(carry, 0.0)
            memset(Trw, 0.0)
            memset(Cml, 0.0)
            memset(mml, 0.0)
            memset(Srt, 0.0)
            memset(SrtB, 0.0)
            memset(CmlB, 0.0)
            memset(TrwB, 0.0)
            memset(NrepB, 0.0)
            with ExitStack() as lst:
                wup = tile(lst, 'wup', [64, 1024], BF16)
                aup = tile(lst, 'aup', [64, 1024], BF16)
                gup = tile(lst, 'gup', [128, 1024], BF16)
                P.dma('pool', wup.a, I_['rw_wup'][l], writes=[wup])
                P.dma('pool', aup.a, I_['rw_aup'][l], writes=[aup])
                P.dma('pool', gup.a, I_['rw_gup'][l], writes=[gup])
                KTp = tile(lst, 'KTp', [128, 4, 256], BF16)
                Vp = tile(lst, 'Vp', [128, 2, 512], BF16)
                with ExitStack() as stk:
                    mT = tile(stk, 'mT', [128, 16, 256])
                    ld(mT, I_['memT'])
                    fm_norm(mT, 6, 256, dst=hT[:, :, 0:256], stk=stk)
                    kf = tile(stk, 'kf', [128, 4, 256])

                    def ev_k(bi, ps):
                        cp(kf[:, bi, :], ps, eng='act')
                        cp(KTp[:, bi, :], ps)
                    proj(WBF['x_wkv'][l], [0, 1, 2, 3], 16, lambda kb: hT.a[:, kb, 0:256], [hT], 256, ev_k)
                    st(O_['p_kT'][l].rearrange("h p m -> p h m"), kf)
                    vf = tile(stk, 'vf', [128, 2, 512])
                    for bi in range(4):
                        wb = wbuf[wi[0] % 2]
                        wi[0] += 1
                        wv = V(wb.k, wb.a[:, 0:2048].rearrange("p (k c) -> p k c", c=128))
                        P.dma('sp', wv.a, WBF['x_wkv'][l][4 + bi], writes=[wb])
                        for mb in range(2):
                            ps = nextA()
                            for kb in range(16):
                                mm(ps[:, 0:128], hT[:, kb, mb * 128:(mb + 1) * 128], wv[:, kb, :],
                                   start=(kb == 0), stop=(kb == 15))
                            cp(vf[:, mb, bi * 128:(bi + 1) * 128], ps[:, 0:128], eng='act')
                            cp(Vp[:, mb, bi * 128:(bi + 1) * 128], ps[:, 0:128])
                    st(O_['p_v'][l], vf)
                    P.barrier()

                for gi, (t0, TG, kind) in enumerate(groups):
                    NCH = TG // 128
                    nseg = 1 if kind == 'p' else SB
                    slen = 128 // nseg
                    Cg = C[kind]
                    last_group = (kind == 'p' and t0 + TG == NPT) or kind == 's'
                    sub_begin(xcur, t0, TG, 0)
                    with ExitStack() as mst:
                        ysT = tile(mst, 'ysT', [128, 24, TG], BF16)
                        mixer(l, t0, TG, kind, NCH, nseg, slen, Cg, ysT, last_group)
                        P.barrier()
                        import os as _os
                        for _z in _os.environ.get('MK_ZERO', ''):
                            memset(ysT[:, int(_z) * 8:int(_z) * 8 + 8, :], 0.0, eng='dve')
                        mergedT = tile(mst, 'mergedT', [128, 16, TG], BF16)
                        acc = tile(mst, 'acc', [128, TG])
                        sg = tile(mst, 'sg', [128, TG])
                        for cb in range(16):
                            for c in range(3):
                                def ev_g(bi, ps):
                                    act(sg, ps, AF.Sigmoid)
                                proj(WBF['w_in'][l], [GT0 + c * 16 + cb], 16, lambda kb: hT.a[:, kb, 0:TG], [hT], TG, ev_g)

                                def ev_p(bi, ps, c=c):
                                    if c == 0:
                                        tt(acc, sg, ps, ALU.mult)
                                    else:
                                        tt(sg, sg, ps, ALU.mult)
                                        tt(acc, acc, sg, ALU.add, eng='pool')
                                proj(WBF['w_br'][l][c], [cb], 8, lambda kb, c=c: ysT.a[:, c * 8 + kb, :], [ysT], TG, ev_p)
                            cp(mergedT[:, cb, :], acc, eng='act')

                        def ev_o(bi, ps):
                            cp(postT[:, bi, 0:TG], ps, eng='act')
                        proj(WBF['w_out'][l], list(range(16)), 16, lambda kb: mergedT.a[:, kb, :], [mergedT], TG, ev_o)
                        P.barrier()
                    sub_end(xcur, xs[0], t0, TG, 1)
                    sub_begin(xs[0], t0, TG, 2)
                    with ExitStack() as ast:
                        qT = tile(ast, 'qT', [128, 4, TG], BF16)

                        def ev_q(bi, ps):
                            act(qT[:, bi, :], ps, AF.Copy, scale=128.0 ** -0.5)
                        proj(WBF['x_wq'][l], [0, 1, 2, 3], 16, lambda kb: hT.a[:, kb, 0:TG], [hT], TG, ev_q)
                        oT = tile(ast, 'oT', [128, 4, TG], BF16)
                        if kind == 's':
                            KTs = tile(ast, 'KTs', [128, SB, 4, 256], BF16)
                            Vs = tile(ast, 'Vs', [128, SB, 2, 512], BF16)
                            for b in range(SB):
                                P.dma('pool', KTs.a[:, b], I_['s_kT'][l, b].rearrange("h p m -> p h m"), writes=[KTs])
                                P.dma('pool', Vs.a[:, b], I_['s_v'][l, b], writes=[Vs])
                        eT = tile(ast, 'eT', [128, 2, 128], BF16)
                        rec = tile(ast, 'rec', [128, 128])
                        for ch in range(NCH):
                            cs = slice(ch * 128, (ch + 1) * 128)
                            for h in range(4):
                                pss = nextB()
                                pso = nextB()
                                if kind == 'p':
                                    for mb in range(2):
                                        mm(pss[:, mb * 128:(mb + 1) * 128], KTp[:, h, mb * 128:(mb + 1) * 128], qT[:, h, cs])
                                else:
                                    for b in range(SB):
                                        for mb in range(2):
                                            mm(pss[:, mb * 128 + b * 8:mb * 128 + b * 8 + 8],
                                               KTs[:, b, h, mb * 128:(mb + 1) * 128], qT[:, h, b * 8:b * 8 + 8])
                                act(eT, V(pss.k, pss.a[:, 0:256].rearrange("p (m t) -> p m t", t=128)), AF.Exp)
                                for mb in range(2):
                                    mm(pso[:, 128:256], ones_b, eT[:, mb, :], start=(mb == 0), stop=(mb == 1))
                                if kind == 'p':
                                    for mb in range(2):
                                        mm(pso[:, 0:128], Vp[:, mb, h * 128:(h + 1) * 128], eT[:, mb, :],
                                           start=(mb == 0), stop=(mb == 1))
                                else:
                                    for b in range(SB):
                                        for mb in range(2):
                                            mm(pso[:, b * 8:b * 8 + 8], Vs[:, b, mb, h * 128:(h + 1) * 128],
                                               eT[:, mb, b * 8:b * 8 + 8], start=(mb == 0), stop=(mb == 1))
                                recip(rec, pso[:, 128:256])
                                tt(oT[:, h, cs], pso[:, 0:128], rec, ALU.mult)

                        def ev_xo(bi, ps):
                            cp(postT[:, bi, 0:TG], ps, eng='act')
                        proj(WBF['x_wo'][l], list(range(16)), 4, lambda kb: oT.a[:, kb, :], [oT], TG, ev_xo)
                        P.barrier()
                    sub_end(xs[0], xs[0], t0, TG, 3)
                    sub_begin(xs[0], t0, TG, 4)
                    with ExitStack() as fst:
                        aT = tile(fst, 'aT', [128, 64, TG], BF16)
                        rl = [tile(fst, 'rl%d' % i, [128, TG]) for i in range(2)]

                        def ev_1(bi, ps):
                            r_ = rl[bi % 2]
                            act(r_, ps, AF.Relu)
                            tt(aT[:, bi, :], r_, r_, ALU.mult)
                        proj(WBF['ff_w1'][l], list(range(64)), 16, lambda kb: hT.a[:, kb, 0:TG], [hT], TG, ev_1)

                        def ev_2(bi, ps):
                            cp(postT[:, bi, 0:TG], ps, eng='act')
                        proj(WBF['ff_w2'][l], list(range(16)), 64, lambda kb: aT.a[:, kb, :], [aT], TG, ev_2)
                        P.barrier()
                    if l == DEPTH - 1:
                        sub_end(xs[0], O_['yT'], t0, TG, 5)
                    else:
                        sub_end(xs[0], xs[0], t0, TG, 5)
                P.barrier()
        P.finish()
    return nc


def _blk(w, kb):
    K, N = w.shape
    return np.ascontiguousarray(w.reshape(kb, 128, N // 128, 128).transpose(2, 1, 0, 3))


def _fm(v, nb):
    return np.ascontiguousarray(v.reshape(nb, 128).T)


def _pad_cols(w, blocks):
    out = np.zeros((w.shape[0], 128 * len(blocks)), w.dtype)
    for i, (s, n) in enumerate(blocks):
        out[:, i * 128:i * 128 + n] = w[:, s:s + n]
    return out


def _win_blocks():
    b = []
    o = 0
    for i in range(24):
        b.append((o + i * 128, 128))
    o = 3072
    b += [(o, 64), (o + 64, 64), (o + 128, 128)]
    o = 3328
    for i in range(32):
        b.append((o + i * 128, 128))
    o = 3328 + 4096
    b += [(o, 8), (o + 8, 8)]
    o = 3328 + 4112
    for i in range(32):
        b.append((o + i * 128, 128))
    o = 3328 + 4112 + 4096
    for i in range(48):
        b.append((o + i * 128, 128))
    assert len(b) == NBLK_IN
    return b


_NC_CACHE = {}


def kernel(**inp):
    inp = {k: np.asarray(v) for k, v in inp.items()}
    seq = inp['x_prompt'].shape[1]
    NPT = seq
    NT = NPT + 128
    if seq not in _NC_CACHE:
        _NC_CACHE[seq] = build(seq)
    nc = _NC_CACHE[seq]
    f32 = np.float32
    wb = _win_blocks()
    shared = {}
    shared['w_in'] = np.stack([_blk(_pad_cols(inp['w_in'][l], wb), 16) for l in range(DEPTH)])
    shared['w_br'] = np.stack([np.stack([_blk(inp['w_br'][l, c], 8) for c in range(3)]) for l in range(DEPTH)])
    shared['w_out'] = np.stack([_blk(inp['w_out'][l], 16) for l in range(DEPTH)])
    shared['x_wq'] = np.stack([_blk(inp['x_wq'][l], 16) for l in range(DEPTH)])
    shared['x_wkv'] = np.stack([_blk(inp['x_wkv'][l], 16) for l in range(DEPTH)])
    shared['x_wo'] = np.stack([_blk(inp['x_wo'][l], 4) for l in range(DEPTH)])
    shared['ff_w1'] = np.stack([_blk(inp['ff_w1'][l], 16) for l in range(DEPTH)])
    shared['ff_w2'] = np.stack([_blk(inp['ff_w2'][l], 64) for l in range(DEPTH)])
    gn = ['g_pre_mix', 'g_post_mix', 'g_pre_x', 'g_post_x', 'g_pre_ff', 'g_post_ff', 'g_mem']
    shared['gains'] = np.stack([np.stack([_fm(inp[n][l], 16) for n in gn], axis=1) for l in range(DEPTH)])
    rn = ['rw_w0', 'rw_a0', 'rw_k_k', 'rw_k_a', 'rw_r_k', 'rw_gn_g', 'rw_gn_b', 'ml_norm_g']
    shared['rwp'] = np.stack([np.stack([_fm(inp[n][l].reshape(-1), 8) for n in rn], axis=1) for l in range(DEPTH)])
    mu_pad = np.zeros((DEPTH, 27 * 128), f32)
    for l in range(DEPTH):
        mu_pad[l] = _pad_cols(inp['rw_mu'][l][None, :], wb[:27])[0]
    shared['rw_mu'] = np.stack([_fm(mu_pad[l], 27) for l in range(DEPTH)])
    shared['rw_wup'] = inp['rw_w_up']
    shared['rw_aup'] = inp['rw_a_up']
    shared['rw_gup'] = inp['rw_g_up']
    shared['ml_b'] = np.stack([inp['ml_i_b'], inp['ml_f_b']], axis=-1)
    for g_, sl in (('p', 128), ('s', 8)):
        c = make_consts(sl)
        shared['c_le_' + g_] = c['le']
        shared['c_lt_' + g_] = c['lt']
        shared['c_gt_' + g_] = c['gt']
        shared['c_dt_' + g_] = np.ascontiguousarray(c['dt'].transpose(1, 0, 2))
        shared['c_gp_' + g_] = np.ascontiguousarray(c['gp'].transpose(1, 0, 2))
        shared['c_kd_' + g_] = c['kd']
        shared['c_reset_' + g_] = c['reset']
        shared['c_el_' + g_] = c['elast']
        if g_ == 's':
            shared['c_segcol'] = c['segcol']
            shared['c_segrow'] = np.ascontiguousarray(np.broadcast_to(c['segcol'].T[None, :, :], (128, 16, 128)))
    pos = np.concatenate([np.arange(NPT), PAST_LEN + (np.arange(128) % 8)]).astype(f32)
    shared['c_cos'], shared['c_sin'] = rope_tables(pos)
    shared['c_ident'] = np.eye(128, dtype=f32)
    hb = np.zeros((128, 128), f32)
    hb[:64, :64] = 1
    hb[64:, 64:] = 1
    shared['c_half'] = hb
    selm = np.zeros((8, 8, 128), f32)
    for h in range(8):
        selm[h, h, :] = 1
    shared['c_sel'] = selm

    def fmT(x):
        return np.ascontiguousarray(x.T.reshape(16, 128, x.shape[0]).transpose(1, 0, 2))

    in_maps = []
    for c in range(NCORE):
        m = dict(shared)
        bp = c // 2
        bs = slice(c * SB, (c + 1) * SB)
        xtok = np.concatenate([inp['x_prompt'][bp], inp['x_sample'][bs].reshape(SB * DEC_T, D)], axis=0)
        m['xT'] = fmT(xtok)
        m['memT'] = fmT(inp['mem_prompt'][bp])
        sh = np.zeros((DEPTH, SB, 27 * 128), f32)
        for l in range(DEPTH):
            sh[l] = _pad_cols(inp['state_rwkv_shift'][l, bs], wb[:27])
        m['s_shift'] = np.ascontiguousarray(sh.reshape(DEPTH, SB, 27, 128).transpose(0, 3, 2, 1))
        srw = inp['state_rwkv'][:, bs]
        m['s_rw'] = np.ascontiguousarray(srw.transpose(0, 1, 2, 4, 3).reshape(DEPTH, SB, 8, 128, 64))
        m['s_mlc'] = np.ascontiguousarray(np.concatenate(
            [inp['state_mlstm_c'][:, bs], inp['state_mlstm_n'][:, bs][..., None]], axis=-1))
        m['s_mlm'] = np.ascontiguousarray(inp['state_mlstm_m'][:, bs].transpose(0, 2, 1))
        srt = inp['state_ret'][:, bs]
        m['s_rt'] = np.ascontiguousarray(srt.reshape(DEPTH, SB, 4, 2, 128, 256).transpose(0, 1, 2, 4, 3, 5))
        ck = inp['cache_mem_k'][:, bs]
        m['s_kT'] = np.ascontiguousarray(ck.transpose(0, 1, 3, 4, 2))
        cv = inp['cache_mem_v'][:, bs]
        m['s_v'] = np.ascontiguousarray(cv.reshape(DEPTH, SB, 2, 128, 512).transpose(0, 1, 3, 2, 4))
        in_maps.append(m)
    res = run_bass_kernel_spmd(nc, in_maps, core_ids=list(range(NCORE)))
    R = res.results

    def tokT(y):
        return y.transpose(1, 0, 2).reshape(D, -1).T

    nb = inp['x_prompt'].shape[0]
    y_p = np.stack([tokT(R[2 * b]['yT'][:, :, :NPT]) for b in range(nb)]).astype(f32)
    y_s = np.concatenate([tokT(R[c]['yT'][:, :, NPT:]).reshape(SB, DEC_T, D) for c in range(NCORE)]).astype(f32)

    def unpad_shift(a):
        return np.concatenate([a[..., 0:3072], a[..., 3072:3136], a[..., 3200:3264], a[..., 3328:3456]], axis=-1)

    pc = [2 * b for b in range(nb)]
    p_shift = np.stack([unpad_shift(R[c]['p_shift'].transpose(0, 2, 1).reshape(DEPTH, 27 * 128)) for c in pc], axis=1)
    p_rw = np.stack([R[c]['p_rw'].reshape(DEPTH, 16, 64, 64).transpose(0, 1, 3, 2) for c in pc], axis=1)
    p_mlc = np.stack([R[c]['p_mlc'][..., :128] for c in pc], axis=1)
    p_mln = np.stack([R[c]['p_mlc'][..., 128] for c in pc], axis=1)
    p_mlm = np.stack([R[c]['p_mlm'][..., 0] for c in pc], axis=1)
    p_rt = np.stack([R[c]['p_rt'].transpose(0, 1, 3, 2, 4).reshape(DEPTH, 4, 256, 256) for c in pc], axis=1)
    p_mk = np.stack([R[c]['p_kT'].transpose(0, 3, 1, 2) for c in pc], axis=1)
    p_mv = np.stack([R[c]['p_v'].transpose(0, 2, 1, 3).reshape(DEPTH, 256, 4, 128) for c in pc], axis=1)
    s_shift = np.concatenate([unpad_shift(R[c]['o_shift'].transpose(0, 3, 2, 1).reshape(DEPTH, SB, 27 * 128))
                              for c in range(NCORE)], axis=1)
    s_rw = np.concatenate([R[c]['o_rw'].reshape(DEPTH, SB, 16, 64, 64).transpose(0, 1, 2, 4, 3)
                           for c in range(NCORE)], axis=1)
    s_mlc = np.concatenate([R[c]['o_mlc'][..., :128] for c in range(NCORE)], axis=1)
    s_mln = np.concatenate([R[c]['o_mlc'][..., 128] for c in range(NCORE)], axis=1)
    s_mlm = np.concatenate([R[c]['o_mlm'].transpose(0, 2, 1) for c in range(NCORE)], axis=1)
    s_rt = np.concatenate([R[c]['o_rt'].transpose(0, 1, 2, 4, 3, 5).reshape(DEPTH, SB, 4, 256, 256)
                           for c in range(NCORE)], axis=1)
    outs = (y_p, y_s, p_shift, p_rw, p_mlc, p_mln, p_mlm, p_rt, p_mk, p_mv,
            s_shift, s_rw, s_mlc, s_mln, s_mlm, s_rt)
    return tuple(np.ascontiguousarray(o, dtype=f32) for o in outs)
```

```python
import math
import numpy as np
from contextlib import ExitStack
import concourse.bass as bass
import concourse.mybir as mybir
from concourse.bass_utils import run_bass_kernel_spmd

F32 = mybir.dt.float32
BF16 = mybir.dt.bfloat16
ALU = mybir.AluOpType
AF = mybir.ActivationFunctionType
AX = mybir.AxisListType

D = 2048
DEPTH = 2
NCORE = 8
BATCH = 4
SEQ = 2048
DEC_B = 128
DEC_T = 8
SB = DEC_B // NCORE
PAST_LEN = 16384
N_MEM = 256
MIXW = 1024
EPS = 1e-6
RW_GN_EPS = 64e-5
RW0 = 0
ML0 = 27
RT0 = 61
GT0 = 93
NBLK_IN = 141


class V:
    def __init__(s, k, a):
        s.k = k
        s.a = a

    def __getitem__(s, i):
        return V(s.k, s.a[i])

    def bc(s, shape):
        return V(s.k, s.a.to_broadcast(shape))


class Prog:
    ENGS = ('pe', 'act', 'dve', 'pool', 'sp')

    def __init__(self, nc, es, ndma=16):
        self.nc = nc
        self.e = {'pe': nc.tensor, 'act': nc.scalar, 'dve': nc.vector, 'pool': nc.gpsimd, 'sp': nc.sync}
        self.es = es
        self.epoch = {k: 0 for k in self.ENGS}
        self.sem = {(k, 0): es.enter_context(nc.semaphore('s_' + k + '0')) for k in self.ENGS}
        self.cnt = {k: 0 for k in self.ENGS}
        import os as _o
        self.EPOCH_MAX = int(_o.environ.get("MK_EPOCH", "30000"))
        self.dsem = [es.enter_context(nc.semaphore('d%d' % i)) for i in range(ndma)]
        self.dcnt = [0] * ndma
        self.dnext = 0
        self.dnext_pool = ndma // 2
        self.known = {k: {} for k in self.ENGS}
        self.lastw = {}
        self.readers = {}
        self.ninst = 0
        self.stopped = False
        self.nbar = 0
        import os
        self.stop_after = int(os.environ['MK_STOP']) if 'MK_STOP' in os.environ else None
        self.stop_i = int(os.environ['MK_STOPI']) if 'MK_STOPI' in os.environ else None

    def _wait(self, eng, ev):
        kind, who, val = ev
        if kind == 'c' and who[0] == eng and eng == 'pe':
            return
        key = (kind, who)
        if self.known[eng].get(key, 0) >= val:
            return
        if kind == 'c':
            for (kk, ww), vv in self.known[eng].items():
                if kk == 'c' and ww[0] == who[0] and ww[1] > who[1] and vv > 0:
                    return
        sem = self.sem[who] if kind == 'c' else self.dsem[who]
        self.e[eng].wait_ge(sem, val)
        self.known[eng][key] = val
        self.ninst += 1

    def _deps(self, eng, reads, writes):
        for r in reads:
            ev = self.lastw.get(r)
            if ev is not None:
                self._wait(eng, ev)
        for w in writes:
            ev = self.lastw.get(w)
            if ev is not None:
                self._wait(eng, ev)
            for ev in self.readers.get(w, ()):
                self._wait(eng, ev)

    def _commit(self, ev, reads, writes):
        for w in writes:
            self.lastw[w] = ev
            self.readers[w] = []
        for r in reads:
            if r in writes:
                continue
            l = self.readers.setdefault(r, [])
            l.append(ev)
            if len(l) > 40:
                d = {}
                for x in l:
                    d[(x[0], x[1])] = x
                self.readers[r] = list(d.values())

    def I(self, eng, fn, reads=(), writes=()):
        if self.stop_i is not None and self.ninst >= self.stop_i:
            self.stopped = True
        if self.stopped:
            return
        reads = [r.k if isinstance(r, V) else r for r in reads]
        writes = [w.k if isinstance(w, V) else w for w in writes]
        pr = [r for r in reads if r.startswith('ps')]
        if pr:
            writes = writes + [r for r in pr if r not in writes]
            reads = [r for r in reads if not r.startswith('ps')]
        self._deps(eng, reads, writes)
        ins = fn(self.e[eng])
        if self.cnt[eng] >= self.EPOCH_MAX:
            self.epoch[eng] += 1
            self.cnt[eng] = 0
            self.sem[(eng, self.epoch[eng])] = self.es.enter_context(
                self.nc.semaphore('s_%s%d' % (eng, self.epoch[eng])))
        self.cnt[eng] += 1
        ins.then_inc(self.sem[(eng, self.epoch[eng])], 1)
        self._commit(('c', (eng, self.epoch[eng]), self.cnt[eng]), reads, writes)
        self.ninst += 1

    def dma(self, q, out, in_, reads=(), writes=(), **kw):
        if self.stop_i is not None and self.ninst >= self.stop_i:
            self.stopped = True
        if self.stopped:
            return
        reads = [r.k if isinstance(r, V) else r for r in reads]
        writes = [w.k if isinstance(w, V) else w for w in writes]
        half = len(self.dsem) // 2
        if q == 'pool':
            i = self.dnext_pool
            self.dnext_pool = half + (self.dnext_pool - half + 1) % (len(self.dsem) - half)
        else:
            i = self.dnext
            self.dnext = (self.dnext + 1) % half
        if self.dcnt[i] > 0:
            self._wait(q, ('d', i, self.dcnt[i]))
        self._deps(q, reads, writes)
        ins = self.e[q].dma_start(out=out, in_=in_, **kw)
        self.dcnt[i] += 16
        ins.then_inc(self.dsem[i], 16)
        self._commit(('d', i, self.dcnt[i]), reads, writes)
        self.ninst += 1

    def barrier(self):
        if self.stopped:
            return
        self.nbar += 1
        if self.stop_after is not None and self.nbar >= self.stop_after:
            self.stopped = True
        for eng in self.ENGS:
            for i, v in enumerate(self.dcnt):
                if v:
                    self._wait(eng, ('d', i, v))
            for k in self.ENGS:
                if self.cnt[k]:
                    self._wait(eng, ('c', (k, self.epoch[k]), self.cnt[k]))
        self.lastw = {}
        self.readers = {}

    def finish(self):
        for i, v in enumerate(self.dcnt):
            if v:
                self._wait('sp', ('d', i, v))
        for k in self.ENGS:
            if k != 'sp' and self.cnt[k]:
                self._wait('sp', ('c', (k, self.epoch[k]), self.cnt[k]))


def _ret_gamma():
    lg = np.log(1.0 - np.exp(np.linspace(math.log(1.0 / 32), math.log(1.0 / 512), 4)))
    return np.exp(lg).astype(np.float64)


def make_consts(seglen):
    t = np.arange(128)
    seg = t // seglen
    tau = t % seglen
    same = seg[:, None] == seg[None, :]
    c = {}
    le = (same & (t[:, None] <= t[None, :])).astype(np.float32)
    lt = (same & (t[:, None] < t[None, :])).astype(np.float32)
    c['le'] = le
    c['lt'] = lt
    c['gt'] = lt.T.copy()
    g = _ret_gamma()
    dt = np.zeros((4, 128, 128), np.float32)
    gp = np.zeros((4, 128, 128), np.float32)
    kd = np.zeros((128, 4), np.float32)
    for h in range(4):
        diff = (t[None, :] - t[:, None]).clip(0)
        dt[h] = (g[h] ** diff) * le * (256.0 ** -0.5)
        gp[h] = np.broadcast_to((g[h] ** (tau + 1.0))[None, :], (128, 128))
        kd[:, h] = (g[h] ** (seglen - 1.0 - tau)) * (256.0 ** -0.5)
    c['dt'] = dt
    c['gp'] = gp
    c['kd'] = kd
    c['gl'] = [float(g[h] ** seglen) for h in range(4)]
    reset = np.ones((128, 128), np.float32)
    reset[:, tau == 0] = 0.0
    c['reset'] = reset
    nseg = 128 // seglen
    sm = np.zeros((128, nseg), np.float32)
    sm[t, seg] = 1.0
    c['segcol'] = sm
    el = np.zeros((128, 128), np.float32)
    el[t, seg * seglen + seglen - 1] = 1.0
    c['elast'] = el
    return c


def rope_tables(pos):
    inv = 10000.0 ** (-np.arange(128, dtype=np.float32) / 128.0)
    ang = pos[None, :].astype(np.float32) * inv[:, None].astype(np.float32)
    ang = ang.astype(np.float32)
    return np.cos(ang).astype(np.float32), np.sin(ang).astype(np.float32)


def build(seq=SEQ):
    NPT = seq
    NT = NPT + 128
    groups = []
    t0 = 0
    while t0 < NPT:
        tg = min(512, NPT - t0)
        groups.append((t0, tg, 'p'))
        t0 += tg
    groups.append((NPT, 128, 's'))
    NTILE = NT // 128

    nc = bass.Bass("TRN2", target_bir_lowering=False)
    dt_in = lambda n, s: nc.dram_tensor(n, list(s), F32, kind="ExternalInput").ap()
    dt_out = lambda n, s: nc.dram_tensor(n, list(s), F32, kind="ExternalOutput").ap()
    dt_scr = lambda n, s: nc.dram_tensor(n, list(s), F32, kind="Internal").ap()

    I_ = {}
    I_['xT'] = dt_in('xT', [128, 16, NT])
    I_['memT'] = dt_in('memT', [128, 16, 256])
    I_['w_in'] = dt_in('w_in', [DEPTH, NBLK_IN, 128, 16, 128])
    I_['w_br'] = dt_in('w_br', [DEPTH, 3, 16, 128, 8, 128])
    I_['w_out'] = dt_in('w_out', [DEPTH, 16, 128, 16, 128])
    I_['x_wq'] = dt_in('x_wq', [DEPTH, 4, 128, 16, 128])
    I_['x_wkv'] = dt_in('x_wkv', [DEPTH, 8, 128, 16, 128])
    I_['x_wo'] = dt_in('x_wo', [DEPTH, 16, 128, 4, 128])
    I_['ff_w1'] = dt_in('ff_w1', [DEPTH, 64, 128, 16, 128])
    I_['ff_w2'] = dt_in('ff_w2', [DEPTH, 16, 128, 64, 128])
    I_['gains'] = dt_in('gains', [DEPTH, 128, 7, 16])
    I_['rwp'] = dt_in('rwp', [DEPTH, 128, 8, 8])
    I_['rw_mu'] = dt_in('rw_mu', [DEPTH, 128, 27])
    I_['rw_wup'] = dt_in('rw_wup', [DEPTH, 64, 1024])
    I_['rw_aup'] = dt_in('rw_aup', [DEPTH, 64, 1024])
    I_['rw_gup'] = dt_in('rw_gup', [DEPTH, 128, 1024])
    I_['ml_b'] = dt_in('ml_b', [DEPTH, 8, 2])
    I_['s_shift'] = dt_in('s_shift', [DEPTH, 128, 27, SB])
    I_['s_rw'] = dt_in('s_rw', [DEPTH, SB, 8, 128, 64])
    I_['s_mlc'] = dt_in('s_mlc', [DEPTH, SB, 8, 128, 129])
    I_['s_mlm'] = dt_in('s_mlm', [DEPTH, 8, SB])
    I_['s_rt'] = dt_in('s_rt', [DEPTH, SB, 4, 128, 2, 256])
    I_['s_kT'] = dt_in('s_kT', [DEPTH, SB, 4, 128, 256])
    I_['s_v'] = dt_in('s_v', [DEPTH, SB, 128, 2, 512])
    for g_, sl in (('p', 128), ('s', 8)):
        I_['c_le_' + g_] = dt_in('c_le_' + g_, [128, 128])
        I_['c_lt_' + g_] = dt_in('c_lt_' + g_, [128, 128])
        I_['c_gt_' + g_] = dt_in('c_gt_' + g_, [128, 128])
        I_['c_dt_' + g_] = dt_in('c_dt_' + g_, [128, 4, 128])
        I_['c_gp_' + g_] = dt_in('c_gp_' + g_, [128, 4, 128])
        I_['c_kd_' + g_] = dt_in('c_kd_' + g_, [128, 4])
        I_['c_reset_' + g_] = dt_in('c_reset_' + g_, [128, 128])
        I_['c_el_' + g_] = dt_in('c_el_' + g_, [128, 128])
    I_['c_segcol'] = dt_in('c_segcol', [128, 16])
    I_['c_segrow'] = dt_in('c_segrow', [128, 16, 128])
    I_['c_cos'] = dt_in('c_cos', [128, NT])
    I_['c_sin'] = dt_in('c_sin', [128, NT])
    I_['c_ident'] = dt_in('c_ident', [128, 128])
    I_['c_half'] = dt_in('c_half', [128, 128])
    I_['c_sel'] = dt_in('c_sel', [8, 8, 128])

    O_ = {}
    O_['yT'] = dt_out('yT', [128, 16, NT])
    O_['p_shift'] = dt_out('p_shift', [DEPTH, 128, 27])
    O_['p_rw'] = dt_out('p_rw', [DEPTH, 8, 128, 64])
    O_['p_mlc'] = dt_out('p_mlc', [DEPTH, 8, 128, 129])
    O_['p_mlm'] = dt_out('p_mlm', [DEPTH, 8, 1])
    O_['p_rt'] = dt_out('p_rt', [DEPTH, 4, 128, 2, 256])
    O_['p_kT'] = dt_out('p_kT', [DEPTH, 4, 128, 256])
    O_['p_v'] = dt_out('p_v', [DEPTH, 128, 2, 512])
    O_['o_shift'] = dt_out('o_shift', [DEPTH, 128, 27, SB])
    O_['o_rw'] = dt_out('o_rw', [DEPTH, SB, 8, 128, 64])
    O_['o_mlc'] = dt_out('o_mlc', [DEPTH, SB, 8, 128, 129])
    O_['o_mlm'] = dt_out('o_mlm', [DEPTH, 8, SB])
    O_['o_rt'] = dt_out('o_rt', [DEPTH, SB, 4, 128, 2, 256])
    xs = [dt_scr('xs0', [128, 16, NT])]
    BIGW = ['w_in', 'w_br', 'w_out', 'x_wq', 'x_wkv', 'x_wo', 'ff_w1', 'ff_w2']
    WBF = {n: nc.dram_tensor('bf_' + n, list(I_[n].shape), BF16, kind="Internal").ap() for n in BIGW}

    with ExitStack() as es:
        P = Prog(nc, es)
        uid = [0]

        def tile(stk, name, shape, dt=F32):
            uid[0] += 1
            nm = '%s_%d' % (name, uid[0])
            t = stk.enter_context(nc.sbuf_tensor(nm, list(shape), dt))
            return V(nm, t[:])

        def ptile(name, shape, dt=F32):
            t = es.enter_context(nc.psum_tensor(name, list(shape), dt))
            return V(name, t[:])

        def mm(ps, lhsT, rhs, start=True, stop=True):
            P.I('pe', lambda e: e.matmul(ps.a, lhsT=lhsT.a, rhs=rhs.a, start=start, stop=stop),
                reads=[lhsT, rhs], writes=[ps])

        def tr(ps, in_, idn):
            P.I('pe', lambda e: e.transpose(ps.a, in_.a, idn.a), reads=[in_, idn], writes=[ps])

        def act(out, in_, func, bias=None, scale=1.0, eng='act'):
            rd = [in_]
            kw = {}
            if isinstance(bias, V):
                rd.append(bias)
                kw['bias'] = bias.a
            elif bias is not None:
                kw['bias'] = float(bias)
            P.I('act', lambda e: e.activation(out=out.a, in_=in_.a, func=func, scale=scale, **kw),
                reads=rd, writes=[out])

        def tt(out, a, b, op, eng='dve'):
            P.I(eng, lambda e: e.tensor_tensor(out=out.a, in0=a.a, in1=b.a, op=op), reads=[a, b], writes=[out])

        def ts(out, a, s1, op0, s2=None, op1=None, eng='dve'):
            rd = [a]
            v1 = s1
            if isinstance(s1, V):
                rd.append(s1)
                v1 = s1.a
            v2 = s2
            if isinstance(s2, V):
                rd.append(s2)
                v2 = s2.a
            if op1 is None:
                P.I(eng, lambda e: e.tensor_scalar(out=out.a, in0=a.a, scalar1=v1, scalar2=None, op0=op0),
                    reads=rd, writes=[out])
            else:
                P.I(eng, lambda e: e.tensor_scalar(out=out.a, in0=a.a, scalar1=v1, scalar2=v2, op0=op0, op1=op1),
                    reads=rd, writes=[out])

        def stt(out, a, s, b, op0, op1):
            rd = [a, b]
            sv = s
            if isinstance(s, V):
                rd.append(s)
                sv = s.a
            P.I('dve', lambda e: e.scalar_tensor_tensor(out=out.a, in0=a.a, scalar=sv, in1=b.a, op0=op0, op1=op1),
                reads=rd, writes=[out])

        def cp(out, in_, eng='dve'):
            if eng == 'act':
                act(out, in_, AF.Copy)
            else:
                P.I(eng, lambda e: e.tensor_copy(out=out.a, in_=in_.a), reads=[in_], writes=[out])

        def recip(out, in_):
            P.I('dve', lambda e: e.reciprocal(out=out.a, in_=in_.a), reads=[in_], writes=[out])

        def memset(t, val, eng='pool'):
            P.I(eng, lambda e: e.memset(t.a, val), writes=[t])

        def rsqrt_from(out, in_, scale, bias):
            act(out, in_, AF.Sqrt, bias=bias, scale=scale)
            recip(out, out)

        def ld(out, src, q='sp', **kw):
            P.dma(q, out.a, src, writes=[out], **kw)

        def st(dst, in_, q='sp', **kw):
            P.dma(q, dst, in_.a, reads=[in_], **kw)

        for n_ in BIGW:
            src = I_[n_]
            dst = WBF[n_]
            lead = list(src.shape[:-3])
            idxs = [()]
            for d_ in lead:
                idxs = [i + (j,) for i in idxs for j in range(d_)]
            for ix in idxs:
                sa, da = src, dst
                for j in ix:
                    sa = sa[j]
                    da = da[j]
                P.dma('pool', da, sa, writes=['wbf_' + n_], max_dma_last_dim=4096)
        P.barrier()

        hT = tile(es, 'hT', [128, 16, 512], BF16)
        postT = tile(es, 'postT', [128, 16, 512], F32)
        NWB = 8
        wbuf = [tile(es, 'wbuf%d' % i, [128, 2048], BF16) for i in range(NWB)]
        wi = [0]
        psA = [ptile('psA%d' % i, [128, 512]) for i in range(3)]
        pai = [0]
        psB = [ptile('psB%d' % i, [128, 512]) for i in range(5)]
        pbi = [0]

        def nextA():
            pai[0] = (pai[0] + 1) % 3
            return psA[pai[0]]

        def nextB():
            pbi[0] = (pbi[0] + 1) % 4
            return psB[pbi[0]]

        ident = tile(es, 'ident', [128, 128])
        ident_b = tile(es, 'ident_b', [128, 128], BF16)
        ones_b = tile(es, 'ones_b', [128, 128], BF16)
        half_b = tile(es, 'half_b', [128, 128], BF16)
        sel = tile(es, 'sel', [8, 8, 128])
        zero8 = tile(es, 'zero8', [8, 128])
        ld(ident, I_['c_ident'])
        cp(ident_b, ident)
        memset(ones_b, 1.0)
        memset(zero8, 0.0)
        P.dma('pool', half_b.a, I_['c_half'], writes=[half_b])
        ld(sel, I_['c_sel'])
        C = {}
        for g_ in ('p', 's'):
            C[g_] = {}
            for nm, shp in (('le', [128, 128]), ('lt', [128, 128]), ('gt', [128, 128]), ('dt', [128, 4, 128]),
                            ('gp', [128, 4, 128]), ('kd', [128, 4]), ('reset', [128, 128]), ('el', [128, 128])):
                C[g_][nm] = tile(es, 'c_%s_%s' % (nm, g_), shp)
                ld(C[g_][nm], I_['c_%s_%s' % (nm, g_)])
        segcol = tile(es, 'segcol', [128, 16])
        ld(segcol, I_['c_segcol'])
        gains = tile(es, 'gains', [128, 7, 16])
        rwp = tile(es, 'rwp', [128, 8, 8])
        negw0 = tile(es, 'negw0', [128, 8])
        mu = tile(es, 'mu', [128, 27])
        mlb = tile(es, 'mlb', [8, 2])
        carry = tile(es, 'carry', [128, 27])
        Trw = tile(es, 'Trw', [128, 8, 64])
        Cml = tile(es, 'Cml', [128, 8, 129])
        mml = tile(es, 'mml', [8, 1])
        Srt = tile(es, 'Srt', [128, 4, 2, 256])
        SrtB = tile(es, 'SrtB', [128, 4, 2, 256], BF16)
        CmlB = tile(es, 'CmlB', [128, 8, 128], BF16)
        TrwB = tile(es, 'TrwB', [128, 8, 64], BF16)
        NrepB = tile(es, 'NrepB', [128, 8, 128], BF16)

        def proj(Wd, blocks, KB, rhs_fn, rhs_keys, ntok, evac):
            for bi, blk in enumerate(blocks):
                ps = nextA()
                for q0 in range(0, KB, 16):
                    kq = min(16, KB - q0)
                    wb = wbuf[wi[0] % NWB]
                    wi[0] += 1
                    wv = V(wb.k, wb.a[:, 0:kq * 128].rearrange("p (k c) -> p k c", c=128))
                    P.dma('sp', wv.a, Wd[blk][:, q0:q0 + kq, :], writes=[wb])
                    for kb in range(kq):
                        P.I('pe', lambda e, kb=kb, q0=q0, wv=wv: e.matmul(
                            ps.a[:, 0:ntok], lhsT=wv.a[:, kb, :], rhs=rhs_fn(q0 + kb),
                            start=(q0 + kb == 0), stop=(q0 + kb == KB - 1)),
                            reads=[wb] + list(rhs_keys), writes=[ps])
                evac(bi, ps[:, 0:ntok])

        def fm_norm(src, gidx, TG, dst=None, resid=None, stk=None):
            sq = hT[:, :, 0:TG]
            act(sq, src, AF.Square)
            ps = nextB()
            for kb in range(16):
                mm(ps[:, 0:TG], ones_b, sq[:, kb, :], start=(kb == 0), stop=(kb == 15))
            rstd = tile(stk, 'rstd', [128, TG])
            rsqrt_from(rstd, ps[:, 0:TG], 1.0 / D, EPS)
            for kb in range(16):
                if resid is None:
                    stt(dst[:, kb, :], src[:, kb, :], gains[:, gidx, kb:kb + 1], rstd, ALU.mult, ALU.mult)
                else:
                    stt(src[:, kb, :], src[:, kb, :], gains[:, gidx, kb:kb + 1], rstd, ALU.mult, ALU.mult)
                    tt(resid[:, kb, :], resid[:, kb, :], src[:, kb, :], ALU.add, eng='pool')

        def sub_begin(xcur, t0, TG, gidx):
            with ExitStack() as stk:
                xT = tile(stk, 'xT', [128, 16, TG])
                ld(xT, xcur[:, :, t0:t0 + TG])
                fm_norm(xT, gidx, TG, dst=hT[:, :, 0:TG], stk=stk)
                P.barrier()

        def sub_end(xcur, xnext, t0, TG, gidx):
            with ExitStack() as stk:
                xT = tile(stk, 'xT', [128, 16, TG])
                ld(xT, xcur[:, :, t0:t0 + TG])
                fm_norm(postT[:, :, 0:TG], gidx, TG, resid=xT, stk=stk)
                st(xnext[:, :, t0:t0 + TG], xT)
                P.barrier()

        def scan(out, d0, d1, init, op0, op1):
            rd = [d0, d1]
            iv = init
            if isinstance(init, V):
                rd.append(init)
                iv = init.a
            P.I('dve', lambda e: e.tensor_tensor_scan(out=out.a, data0=d0.a, data1=d1.a, initial=iv, op0=op0, op1=op1),
                reads=rd, writes=[out])

        def mlstm(l, t0, TG, kind, NCH, nseg, slen, Cg, ysT, last_group):
            with ExitStack() as ms:
                gi = tile(ms, 'gi', [8, TG])
                gf = tile(ms, 'gf', [8, TG])

                def ev_i(bi, ps):
                    act(gi, ps[0:8, :], AF.Tanh, bias=mlb[:, 0:1], scale=1.0 / 15.0)

                def ev_f(bi, ps):
                    act(gf, ps[0:8, :], AF.Tanh, bias=mlb[:, 1:2], scale=1.0 / 15.0)
                proj(WBF['w_in'][l], [ML0 + 32], 16, lambda kb: hT.a[:, kb, 0:TG], [hT], TG, ev_i)
                proj(WBF['w_in'][l], [ML0 + 33], 16, lambda kb: hT.a[:, kb, 0:TG], [hT], TG, ev_f)
                ts(gi, gi, 15.0, ALU.mult)
                act(gf, gf, AF.Exp, scale=-15.0)
                act(gf, gf, AF.Ln, bias=1.0)
                ts(gf, gf, -1.0, ALU.mult)
                R_all = tile(ms, 'R_all', [8, NCH, 3, 128])
                gT_all = tile(ms, 'gT_all', [128, NCH, 8])
                bb = tile(ms, 'mbb', [8, 128])
                gg = tile(ms, 'mgg', [8, 128])
                cm = tile(ms, 'mcm', [8, 128])
                mm_ = tile(ms, 'mmm', [8, 128])
                m0tok = tile(ms, 'm0tok', [8, 128])
                if kind == 'p':
                    m0s = mml
                else:
                    m0s = tile(ms, 'm0s', [8, SB])
                    ld(m0s, I_['s_mlm'][l])
                for ch in range(NCH):
                    cs = slice(ch * 128, (ch + 1) * 128)
                    scan(bb, Cg['reset'][0:8, :], gf[:, cs], 0.0, ALU.mult, ALU.add)
                    tt(gg, gi[:, cs], bb, ALU.subtract)
                    for seg in range(nseg):
                        c0 = seg * slen
                        scan(cm[:, c0:c0 + slen], zero8[:, 0:slen], gg[:, c0:c0 + slen], m0s[:, seg:seg + 1],
                             ALU.add, ALU.max)
                        cp(m0tok[:, c0:c0 + slen], m0s[:, seg:seg + 1].bc([8, slen]))
                    tt(mm_, bb, cm, ALU.add)
                    ts(R_all[:, ch, 0, :], cm, -1.0, ALU.mult)
                    tt(m0tok, m0tok, cm, ALU.subtract)
                    act(R_all[:, ch, 1, :], m0tok, AF.Exp)
                    act(R_all[:, ch, 2, :], mm_, AF.Exp, scale=-1.0)
                    pst = nextB()
                    tr(pst[:, 0:8], gg, ident[0:8, 0:8])
                    cp(gT_all[:, ch, :], pst[:, 0:8])
                    for seg in range(nseg):
                        c0 = seg * slen
                        cp(m0s[:, seg:seg + 1], mm_[:, c0 + slen - 1:c0 + slen])
                if kind == 'p':
                    if last_group:
                        st(O_['p_mlm'][l], mml)
                else:
                    st(O_['o_mlm'][l], m0s)
                z4 = tile(ms, 'z4', [128, 4, TG])
                qb = tile(ms, 'mqb', [128, 128], BF16)
                kb_ = tile(ms, 'mkb', [128, 128], BF16)
                ktf = tile(ms, 'mktf', [128, 128])
                kw = tile(ms, 'mkw', [128, 128], BF16)
                vext = tile(ms, 'mvext', [128, 129], BF16)
                if kind == 's':
                    vm_all = tile(ms, 'mvm_all', [128, SB, 129], BF16)
                memset(vext, 1.0)
                bcs = tile(ms, 'mbcs', [128, 3, 128])
                ex = tile(ms, 'mex', [128, 128])
                Dx = tile(ms, 'mDx', [128, 128])
                w1 = tile(ms, 'mw1', [128, 128])
                wts = tile(ms, 'mwts', [128, 128], BF16)
                cd = tile(ms, 'mcd', [128, 2, 128])
                num = tile(ms, 'mnum', [128, 128])
                den = tile(ms, 'mden', [128, 128])
                sq = tile(ms, 'msq', [128, 128], BF16)
                rstd = tile(ms, 'mrstd', [128, 128])
                sgo = tile(ms, 'msgo', [128, 128])
                wend = tile(ms, 'mwend', [128, 1])
                if kind == 's':
                    Cxs = tile(ms, 'Cxs', [128, SB, 129])
                    CBs = tile(ms, 'CBs', [128, SB, 128], BF16)
                    NBs = tile(ms, 'NBs', [128, SB, 128], BF16)
                for h in range(8):
                    def ev(bi, ps):
                        cp(z4[:, bi, :], ps, eng='act')
                    proj(WBF['w_in'][l], [ML0 + h, ML0 + 8 + h, ML0 + 16 + h, ML0 + 24 + h], 16,
                         lambda kb: hT.a[:, kb, 0:TG], [hT], TG, ev)
                    if kind == 's':
                        ld(Cxs, I_['s_mlc'][l, :, h].rearrange("b p e -> p b e"))
                        cp(CBs, Cxs[:, :, 0:128], eng='pool')
                        cp(NBs, Cxs[:, :, 128:129].bc([128, SB, 128]), eng='pool')
                    for ch in range(NCH):
                        cs = slice(ch * 128, (ch + 1) * 128)
                        cp(qb, z4[:, 0, cs], eng='act')
                        ts(kb_, z4[:, 1, cs], 128.0 ** -0.5, ALU.mult, eng='pool')
                        pst = nextB()
                        tr(pst[:, 0:128], z4[:, 1, cs], ident)
                        ts(ktf, pst[:, 0:128], 128.0 ** -0.5, ALU.mult)
                        pst = nextB()
                        tr(pst[:, 0:128], z4[:, 2, cs], ident)
                        cp(vext[:, 0:128], pst[:, 0:128], eng='act')
                        psb = nextB()
                        mm(psb[:, 0:384], sel[:, h, :], V(R_all.k, R_all.a[:, ch].rearrange("p a b -> p (a b)")))
                        cp(bcs, V(psb.k, psb.a[:, 0:384].rearrange("p (a b) -> p a b", b=128)), eng='act')
                        pss = nextB()
                        mm(pss[:, 0:128], kb_, qb)
                        ts(ex, bcs[:, 0, :], gT_all[:, ch, h:h + 1], ALU.add, 0.0, ALU.min)
                        act(Dx, ex, AF.Exp)
                        tt(w1, Dx, Cg['le'], ALU.mult, eng='pool')
                        tt(wts, w1, pss[:, 0:128], ALU.mult)
                        psn = nextB()
                        mm(psn[:, 0:128], vext[:, 0:128], wts)
                        mm(psn[:, 128:256], ones_b, wts)
                        for seg in range(nseg):
                            c0 = seg * slen
                            CB = CmlB[:, h, :] if kind == 'p' else CBs[:, seg, :]
                            NB = NrepB[:, h, :] if kind == 'p' else NBs[:, seg, :]
                            mm(psn[:, 256 + c0:256 + c0 + slen], CB, qb[:, c0:c0 + slen])
                            mm(psn[:, 384 + c0:384 + c0 + slen], NB, qb[:, c0:c0 + slen])
                        cp(cd, V(psn.k, psn.a[:, 256:512].rearrange("p (a b) -> p a b", b=128)), eng='act')
                        tt(num, cd[:, 0, :], bcs[:, 1, :], ALU.mult)
                        tt(num, num, psn[:, 0:128], ALU.add)
                        tt(den, cd[:, 1, :], bcs[:, 1, :], ALU.mult)
                        tt(den, den, psn[:, 128:256], ALU.add)
                        stt(den, den, -1.0, den, ALU.mult, ALU.max)
                        tt(den, den, bcs[:, 2, :], ALU.max)
                        recip(den, den)
                        tt(num, num, den, ALU.mult)
                        act(sq, num, AF.Square)
                        psr = nextB()
                        mm(psr[:, 0:128], ones_b, sq)
                        rsqrt_from(rstd, psr[:, 0:128], 1.0 / 128.0, EPS)
                        act(sgo, z4[:, 3, cs], AF.Sigmoid)
                        stt(num, num, rwp[:, 7, h:h + 1], rstd, ALU.mult, ALU.mult)
                        tt(ysT[:, 8 + h, cs], num, sgo, ALU.mult)
                        tt(w1, Dx, Cg['el'], ALU.mult, eng='pool')
                        P.I('dve', lambda e: e.reduce_sum(out=wend.a, in_=w1.a, axis=AX.X), reads=[w1], writes=[wend])
                        ts(kw, ktf, wend[:, 0:1], ALU.mult)
                        if nseg > 1:
                            tt(vm_all, V(vext.k, vext.a.unsqueeze(1).to_broadcast([128, SB, 129])),
                               V(segcol.k, segcol.a.unsqueeze(2).to_broadcast([128, SB, 129])), ALU.mult)
                        for seg in range(nseg):
                            c0 = seg * slen
                            if nseg > 1:
                                v_ = vm_all[:, seg, :]
                            else:
                                v_ = vext
                            pd = nextB()
                            mm(pd[:, 0:129], kw, v_)
                            Cx = Cml[:, h, :] if kind == 'p' else Cxs[:, seg, :]
                            stt(Cx, Cx, bcs[:, 1, c0 + slen - 1:c0 + slen], pd[:, 0:129], ALU.mult, ALU.add)
                        if kind == 'p':
                            cp(CmlB[:, h, :], Cml[:, h, 0:128], eng='pool')
                            cp(NrepB[:, h, :], Cml[:, h, 128:129].bc([128, 128]), eng='pool')
                            if last_group and ch == NCH - 1:
                                st(O_['p_mlc'][l, h], Cml[:, h, :])
                    if kind == 's':
                        st(O_['o_mlc'][l, :, h].rearrange("b p e -> p b e"), Cxs)

        def rwkv(l, t0, TG, kind, NCH, nseg, slen, Cg, ysT, last_group):
            nlev = int(round(math.log2(slen))) - 1
            with ExitStack() as ws:
                pv = tile(ws, 'pv', [128, TG])
                if kind == 's':
                    sh0 = tile(ws, 'sh0', [128, 27, SB])
                    osh = tile(ws, 'osh', [128, 27, SB])
                    ld(sh0, I_['s_shift'][l])
                    segrow = tile(ws, 'segrow', [128, SB, 128])
                    ld(segrow, I_['c_segrow'])
                    atm_all = tile(ws, 'atm_all', [128, SB, 128], BF16)
                    Ts = tile(ws, 'Ts', [128, SB, 64])
                    TsB = tile(ws, 'TsB', [128, SB, 64], BF16)
                    ktm_all = tile(ws, 'ktm_all', [128, SB, 128], BF16)
                    btm_all = tile(ws, 'btm_all', [128, SB, 128], BF16)
                    segcol3 = V(segcol.k, segcol.a.unsqueeze(2).to_broadcast([128, SB, 128]))

                def tshift(u, blk):
                    if kind == 'p':
                        cp(pv[:, 0:1], carry[:, blk:blk + 1], eng='pool')
                        cp(pv[:, 1:TG], u[:, 0:TG - 1], eng='pool')
                        cp(carry[:, blk:blk + 1], u[:, TG - 1:TG], eng='pool')
                    else:
                        u3d = V(u.k, u.a.rearrange("p (b t) -> p b t", t=8))
                        p3d = V(pv.k, pv.a.rearrange("p (b t) -> p b t", t=8))
                        cp(p3d[:, :, 0:1], V(sh0.k, sh0.a[:, blk, :].unsqueeze(2)), eng='pool')
                        cp(p3d[:, :, 1:8], u3d[:, :, 0:7], eng='pool')
                        cp(V(osh.k, osh.a[:, blk, :].unsqueeze(2)), u3d[:, :, 7:8], eng='pool')
                    tt(pv, pv, u, ALU.subtract, eng='pool')
                    stt(u, pv, mu[:, blk:blk + 1], u, ALU.mult, ALU.add)

                u3 = tile(ws, 'u3', [128, 3, TG])

                def ev3(bi, ps):
                    cp(u3[:, bi, :], ps, eng='act')
                proj(WBF['w_in'][l], [RW0 + 24, RW0 + 25, RW0 + 26], 16, lambda kb: hT.a[:, kb, 0:TG], [hT], TG, ev3)
                for bi in range(3):
                    tshift(u3[:, bi, :], 24 + bi)
                tw = tile(ws, 'tw', [64, TG], BF16)
                ab = tile(ws, 'ab', [64, TG], BF16)
                sgd = tile(ws, 'sgd', [128, TG], BF16)
                act(tw, u3[0:64, 0, :], AF.Tanh)
                cp(ab, u3[0:64, 1, :])
                act(sgd, u3[:, 2, :], AF.Sigmoid)
                rkv = tile(ws, 'rkv', [128, 3, TG])
                e2 = tile(ws, 'e2', [128, TG])
                a_ = tile(ws, 'a_', [128, TG])
                g_ = tile(ws, 'g_', [128, TG])
                kk = tile(ws, 'kk', [128, TG])
                kkn = tile(ws, 'kkn', [128, TG])
                kmod = tile(ws, 'kmod', [128, TG])
                bvec = tile(ws, 'bvec', [128, TG])
                bon = tile(ws, 'bon', [128, TG])
                tmp = tile(ws, 'rtmp', [128, TG])
                tb = tile(ws, 'rtb', [128, TG], BF16)
                c = {}
                for nm in ('cwp', 'ecn', 'ecp', 'ktf', 'btf', 'dd', 'PTf', 'Pf', 'yf', 'cen', 'rs', 'tmpT'):
                    c[nm] = tile(ws, 'rw_' + nm, [128, 128])
                for nm in ('rt', 'kt', 'bt', 'at', 'kt_tok', 'bt_tok', 'v_tok', 'MT', 'Aqk', 'Aqb', 'yb', 'cb'):
                    c[nm] = tile(ws, 'rw_' + nm, [128, 128], BF16)
                for nm in ('X', 'Nn', 'X2', 'N2', 'Xb', 'Nb', 'PTb', 'Pb'):
                    c[nm] = tile(ws, 'rw_' + nm, [128, 128], F32)
                Gb = tile(ws, 'rw_Gb', [128, 64], F32)
                Uneg = tile(ws, 'rw_Un', [128, 64], BF16)
                for p in range(8):
                    pc = slice(p * 128, (p + 1) * 128)

                    def evr(bi, ps):
                        cp(rkv[:, bi, :], ps, eng='act')
                    proj(WBF['w_in'][l], [RW0 + p, RW0 + 8 + p, RW0 + 16 + p], 16,
                         lambda kb: hT.a[:, kb, 0:TG], [hT], TG, evr)
                    for bi in range(3):
                        tshift(rkv[:, bi, :], bi * 8 + p)
                    r = rkv[:, 0, :]
                    k = rkv[:, 1, :]
                    v = rkv[:, 2, :]
                    psw = nextB()
                    mm(psw[:, 0:TG], wup[:, pc], tw)
                    act(tmp, psw[:, 0:TG], AF.Exp, scale=-1.0, bias=negw0[:, p:p + 1])
                    act(tmp, tmp, AF.Ln, bias=1.0)
                    act(e2, tmp, AF.Exp, scale=-1.0, bias=-0.5)
                    psa = nextB()
                    mm(psa[:, 0:TG], aup[:, pc], ab)
                    act(a_, psa[:, 0:TG], AF.Sigmoid, bias=rwp[:, 1, p:p + 1])
                    psg = nextB()
                    mm(psg[:, 0:TG], gup[:, pc], sgd)
                    cp(g_, psg[:, 0:TG], eng='act')
                    ts(kk, k, rwp[:, 2, p:p + 1], ALU.mult)
                    act(tb, kk, AF.Square)
                    pss = nextB()
                    mm(pss[:, 0:TG], half_b, tb)
                    act(tmp, pss[:, 0:TG], AF.Sqrt)
                    ts(tmp, tmp, 1e-12, ALU.max)
                    recip(tmp, tmp)
                    tt(kkn, kk, tmp, ALU.mult)
                    ts(tmp, a_, -1.0, ALU.add, rwp[:, 3, p:p + 1], ALU.mult)
                    ts(tmp, tmp, 1.0, ALU.add)
                    tt(kmod, k, tmp, ALU.mult)
                    tt(bvec, kkn, a_, ALU.mult, eng='pool')
                    tt(tmp, r, kmod, ALU.mult)
                    ts(tb, tmp, rwp[:, 4, p:p + 1], ALU.mult)
                    psb = nextB()
                    mm(psb[:, 0:TG], half_b, tb)
                    tt(bon, psb[:, 0:TG], v, ALU.mult)
                    if kind == 's':
                        ld(Ts, I_['s_rw'][l, :, p].rearrange("b q v -> q b v"))
                        cp(TsB, Ts, eng='pool')
                    for ch in range(NCH):
                        cs = slice(ch * 128, (ch + 1) * 128)
                        scan(c['cwp'], Cg['reset'], e2[:, cs], 0.0, ALU.mult, ALU.add)
                        act(c['ecn'], c['cwp'], AF.Exp, scale=-1.0)
                        act(c['ecp'], c['cwp'], AF.Exp)
                        tt(c['rt'], r[:, cs], c['ecn'], ALU.mult)
                        tt(c['ktf'], kmod[:, cs], c['ecp'], ALU.mult)
                        cp(c['kt'], c['ktf'], eng='pool')
                        tt(c['btf'], bvec[:, cs], c['ecp'], ALU.mult)
                        cp(c['bt'], c['btf'], eng='pool')
                        tt(c['dd'], e2[:, cs], c['cwp'], ALU.subtract, eng='pool')
                        act(c['dd'], c['dd'], AF.Exp)
                        tt(c['at'], kkn[:, cs], c['dd'], ALU.mult)
                        for src, dst in ((c['ktf'], c['kt_tok']), (c['btf'], c['bt_tok']), (v[:, cs], c['v_tok'])):
                            pst = nextB()
                            tr(pst[:, 0:128], src, ident)
                            cp(dst, pst[:, 0:128], eng='act')
                        if kind == 's':
                            tt(atm_all, V(c['at'].k, c['at'].a.unsqueeze(1).to_broadcast([128, SB, 128])), segrow,
                               ALU.mult, eng='pool')
                            tt(ktm_all, V(c['kt_tok'].k, c['kt_tok'].a.unsqueeze(1).to_broadcast([128, SB, 128])),
                               segcol3, ALU.mult)
                            tt(btm_all, V(c['bt_tok'].k, c['bt_tok'].a.unsqueeze(1).to_broadcast([128, SB, 128])),
                               segcol3, ALU.mult)
                        psY = psB[4]
                        for j in range(2):
                            pr = slice(j * 64, (j + 1) * 64)
                            jc = slice(j * 64, (j + 1) * 64)
                            psN = nextB()
                            mm(psN[:, 0:128], c['bt'][pr, :], c['at'][pr, :])
                            mm(psN[:, 128:256], c['at'][pr, :], c['bt'][pr, :])
                            mm(psN[:, 256:384], c['kt'][pr, :], c['at'][pr, :])
                            tt(c['X'], psN[:, 0:128], Cg['lt'], ALU.mult)
                            tt(c['Nn'], psN[:, 128:256], Cg['gt'], ALU.mult)
                            tt(c['MT'], psN[:, 256:384], Cg['lt'], ALU.mult)
                            psQ = nextB()
                            mm(psQ[:, 0:128], c['kt'][pr, :], c['rt'][pr, :])
                            mm(psQ[:, 128:256], c['bt'][pr, :], c['rt'][pr, :])
                            tt(c['Aqk'], psQ[:, 0:128], Cg['le'], ALU.mult)
                            tt(c['Aqb'], psQ[:, 128:256], Cg['le'], ALU.mult)
                            psG = nextB()
                            mm(psG[:, 0:64], c['MT'], c['v_tok'][:, jc], start=True, stop=False)
                            for seg in range(nseg):
                                lhs = c['at'][pr, :] if kind == 'p' else atm_all[pr, seg, :]
                                T0b = TrwB[pr, p, :] if kind == 'p' else TsB[pr, seg, :]
                                mm(psG[:, 0:64], lhs, T0b, start=False, stop=(seg == nseg - 1))
                            cp(Gb, psG[:, 0:64], eng='act')
                            tt(c['PTf'], ident, c['X'], ALU.subtract, eng='pool')
                            tt(c['Pf'], ident, c['Nn'], ALU.subtract, eng='pool')
                            cp(c['PTb'], c['PTf'], eng='pool')
                            cp(c['Pb'], c['Pf'], eng='pool')
                            Xk, Nk = c['X'], c['Nn']
                            X2s = (c['X2'], c['Xb'])
                            N2s = (c['N2'], c['Nb'])
                            for lv in range(nlev):
                                X2 = X2s[lv % 2]
                                N2 = N2s[lv % 2]
                                psq = nextB()
                                mm(psq[:, 0:128], Nk, Xk)
                                mm(psq[:, 128:256], Xk, Nk)
                                cp(X2, psq[:, 0:128], eng='act')
                                cp(N2, psq[:, 128:256])
                                psp = nextB()
                                mm(psp[:, 0:128], c['Pb'], X2)
                                mm(psp[:, 128:256], X2, c['Pb'])
                                tt(c['PTf'], c['PTf'], psp[:, 0:128], ALU.add)
                                tt(c['Pf'], c['Pf'], psp[:, 128:256], ALU.add)
                                cp(c['PTb'], c['PTf'], eng='pool')
                                cp(c['Pb'], c['Pf'], eng='pool')
                                Xk, Nk = X2, N2
                            psU = nextB()
                            mm(psU[:, 0:64], c['PTb'], Gb)
                            ts(Uneg, psU[:, 0:64], -1.0, ALU.mult)
                            mm(psY[pr, 0:128], c['v_tok'][:, jc], c['Aqk'], start=True, stop=False)
                            mm(psY[pr, 0:128], Uneg, c['Aqb'], start=False, stop=False)
                            for seg in range(nseg):
                                c0 = seg * slen
                                T0b = TrwB[pr, p, :] if kind == 'p' else TsB[pr, seg, :]
                                mm(psY[pr, c0:c0 + slen], T0b, c['rt'][pr, c0:c0 + slen], start=False,
                                   stop=(seg == nseg - 1))
                            for seg in range(nseg):
                                c0 = seg * slen
                                if nseg > 1:
                                    k_, b_ = ktm_all[:, seg, :], btm_all[:, seg, :]
                                else:
                                    k_, b_ = c['kt_tok'], c['bt_tok']
                                psD = nextB()
                                mm(psD[pr, 0:64], k_[:, pr], c['v_tok'][:, jc], start=True, stop=False)
                                mm(psD[pr, 0:64], b_[:, pr], Uneg, start=False, stop=True)
                                Tf = Trw[pr, p, :] if kind == 'p' else Ts[pr, seg, :]
                                tt(c['tmpT'][pr, 0:64], Tf, psD[pr, 0:64], ALU.add)
                                ts(Tf, c['tmpT'][pr, 0:64], c['ecn'][pr, c0 + slen - 1:c0 + slen], ALU.mult)
                        if kind == 'p':
                            cp(TrwB[:, p, :], Trw[:, p, :], eng='pool')
                            if last_group and ch == NCH - 1:
                                st(O_['p_rw'][l, p], Trw[:, p, :])
                        cp(c['yf'], psY[:, 0:128], eng='act')
                        cp(c['yb'], c['yf'], eng='pool')
                        psm = nextB()
                        mm(psm[:, 0:128], half_b, c['yb'])
                        stt(c['cen'], psm[:, 0:128], -1.0 / 64.0, c['yf'], ALU.mult, ALU.add)
                        act(c['cb'], c['cen'], AF.Square)
                        psv = nextB()
                        mm(psv[:, 0:128], half_b, c['cb'])
                        rsqrt_from(c['rs'], psv[:, 0:128], 1.0 / 64.0, RW_GN_EPS)
                        tt(c['cen'], c['cen'], c['rs'], ALU.mult)
                        ts(c['cen'], c['cen'], rwp[:, 5, p:p + 1], ALU.mult, rwp[:, 6, p:p + 1], ALU.add)
                        tt(c['cen'], c['cen'], bon[:, cs], ALU.add)
                        tt(ysT[:, p, cs], c['cen'], g_[:, cs], ALU.mult)
                    if kind == 's':
                        st(O_['o_rw'][l, :, p].rearrange("b q v -> q b v"), Ts)
                if kind == 'p':
                    if last_group:
                        st(O_['p_shift'][l], carry)
                else:
                    st(O_['o_shift'][l], osh)

        def mixer(l, t0, TG, kind, NCH, nseg, slen, Cg, ysT, last_group):
            rwkv(l, t0, TG, kind, NCH, nseg, slen, Cg, ysT, last_group)
            P.barrier()
            mlstm(l, t0, TG, kind, NCH, nseg, slen, Cg, ysT, last_group)
            P.barrier()
            gl = make_consts(slen)['gl']
            with ExitStack() as rs:
                z = tile(rs, 'rtz', [128, 8, TG])
                cosT = tile(rs, 'cosT', [128, TG])
                sinT = tile(rs, 'sinT', [128, TG])
                ld(cosT, I_['c_cos'][:, t0:t0 + TG])
                ld(sinT, I_['c_sin'][:, t0:t0 + TG])
                t1 = tile(rs, 't1', [128, 128])
                t2 = tile(rs, 't2', [128, 128])
                t3 = tile(rs, 't3', [128, 128])
                t4 = tile(rs, 't4', [128, 128])
                qrf = tile(rs, 'qrf', [128, 2, 128])
                krf = tile(rs, 'krf', [128, 2, 128])
                qr = tile(rs, 'qr', [128, 2, 128], BF16)
                kr = tile(rs, 'kr', [128, 2, 128], BF16)
                qs = tile(rs, 'qs', [128, 2, 128], BF16)
                ktok = tile(rs, 'ktok', [128, 256], BF16)
                ktm = tile(rs, 'ktm', [128, 256], BF16)
                vtok = tile(rs, 'vtok', [128, 256], BF16)
                AT = tile(rs, 'AT', [128, 128], BF16)
                yc = tile(rs, 'yc', [128, 2, 128])
                ysum = tile(rs, 'ysum', [128, 2, 128])
                ysq = tile(rs, 'ysq', [128, 2, 128], BF16)
                rstd = tile(rs, 'rrstd', [128, 128])
                sg = tile(rs, 'rsg', [128, 2, 128])
                Ssm = [tile(rs, 'Ssm%d' % i, [128, 2, 256]) for i in range(2)]
                Sbs = [tile(rs, 'Sbs%d' % i, [128, 2, 256], BF16) for i in range(2)]
                for h in range(4):
                    blocks = [RT0 + 2 * h, RT0 + 2 * h + 1, RT0 + 8 + 2 * h, RT0 + 9 + 2 * h,
                              RT0 + 16 + 2 * h, RT0 + 17 + 2 * h, RT0 + 24 + 2 * h, RT0 + 25 + 2 * h]

                    def ev(bi, ps):
                        cp(z[:, bi, :], ps, eng='act')
                    proj(WBF['w_in'][l], blocks, 16, lambda kb: hT.a[:, kb, 0:TG], [hT], TG, ev)
                    for ch in range(NCH):
                        cs = slice(ch * 128, (ch + 1) * 128)
                        for s0, dstf, dstb in ((0, qrf, qr), (2, krf, kr)):
                            x1 = z[:, s0, cs]
                            x2 = z[:, s0 + 1, cs]
                            tt(t1, x1, cosT[:, cs], ALU.mult)
                            tt(t2, x2, sinT[:, cs], ALU.mult)
                            tt(dstf[:, 0, :], t1, t2, ALU.subtract)
                            tt(t3, x1, sinT[:, cs], ALU.mult, eng='pool')
                            tt(t4, x2, cosT[:, cs], ALU.mult, eng='pool')
                            tt(dstf[:, 1, :], t3, t4, ALU.add, eng='pool')
                            cp(dstb, dstf, eng='act')
                        for b in range(2):
                            tt(qs[:, b, :], qrf[:, b, :], Cg['gp'][:, h, :], ALU.mult)
                        for b in range(2):
                            pst = nextB()
                            tr(pst[:, 0:128], krf[:, b, :], ident)
                            ts(ktok[:, b * 128:(b + 1) * 128], pst[:, 0:128], Cg['kd'][:, h:h + 1], ALU.mult)
                            pst = nextB()
                            tr(pst[:, 0:128], z[:, 4 + b, cs], ident)
                            cp(vtok[:, b * 128:(b + 1) * 128], pst[:, 0:128], eng='act')
                        pss = nextB()
                        for b in range(2):
                            mm(pss[:, 0:128], kr[:, b, :], qr[:, b, :], start=(b == 0), stop=(b == 1))
                        tt(AT, pss[:, 0:128], Cg['dt'][:, h, :], ALU.mult)
                        psy = psB[4]
                        for eb in range(2):
                            mm(psy[:, eb * 128:(eb + 1) * 128], vtok[:, eb * 128:(eb + 1) * 128], AT)
                        for seg in range(nseg):
                            if kind == 'p':
                                S = Srt[:, h]
                                Sb = SrtB[:, h]
                            else:
                                S = Ssm[seg % 2]
                                Sb = Sbs[seg % 2]
                                ld(S, I_['s_rt'][l, seg, h])
                                cp(Sb, S, eng='pool')
                            c0 = seg * slen
                            for eb in range(2):
                                for db in range(2):
                                    mm(psy[:, 256 + eb * 128 + c0:256 + eb * 128 + c0 + slen],
                                       Sb[:, db, eb * 128:(eb + 1) * 128], qs[:, db, c0:c0 + slen],
                                       start=(db == 0), stop=(db == 1))
                            if nseg > 1:
                                ts(ktm, ktok, segcol[:, seg:seg + 1], ALU.mult)
                                kt_ = ktm
                            else:
                                kt_ = ktok
                            for db in range(2):
                                pd = nextB()
                                mm(pd[:, 0:256], kt_[:, db * 128:(db + 1) * 128], vtok)
                                stt(S[:, db, :], S[:, db, :], gl[h], pd[:, 0:256], ALU.mult, ALU.add)
                            if kind == 'p':
                                cp(Sb, S, eng='pool')
                                if last_group and ch == NCH - 1:
                                    st(O_['p_rt'][l, h], S)
                            else:
                                st(O_['o_rt'][l, seg, h], S)
                        cp(yc, V(psy.k, psy.a[:, 256:512].rearrange("p (e t) -> p e t", t=128)), eng='act')
                        tt(ysum, V(psy.k, psy.a[:, 0:256].rearrange("p (e t) -> p e t", t=128)), yc, ALU.add)
                        act(ysq, ysum, AF.Square)
                        pn = nextB()
                        for eb in range(2):
                            mm(pn[:, 0:128], ones_b, ysq[:, eb, :], start=(eb == 0), stop=(eb == 1))
                        rsqrt_from(rstd, pn[:, 0:128], 1.0 / 256.0, EPS)
                        act(sg, z[:, 6:8, cs], AF.Sigmoid)
                        tt(sg, sg, z[:, 6:8, cs], ALU.mult, eng='pool')
                        for eb in range(2):
                            tt(ysum[:, eb, :], ysum[:, eb, :], rstd, ALU.mult)
                            tt(ysT[:, 16 + 2 * h + eb, cs], ysum[:, eb, :], sg[:, eb, :], ALU.mult)

        for l in range(DEPTH):
            xcur = I_['xT'] if l == 0 else xs[0]
            xmid = xs[0] if l == 0 else xs[0]
            ld(gains, I_['gains'][l])
            ld(rwp, I_['rwp'][l])
            ld(mu, I_['rw_mu'][l])
            ld(mlb, I_['ml_b'][l])
            ts(negw0, rwp[:, 0, :], -1.0, ALU.mult)
            ts(mlb, mlb, 1.0 / 15.0, ALU.mult)
            memset(carry, 0.0)
            memset(Trw, 0.0)
            memset(Cml, 0.0)
            memset(mml, 0.0)
            memset(Srt, 0.0)
            memset(SrtB, 0.0)
            memset(CmlB, 0.0)
            memset(TrwB, 0.0)
            memset(NrepB, 0.0)
            with ExitStack() as lst:
                wup = tile(lst, 'wup', [64, 1024], BF16)
                aup = tile(lst, 'aup', [64, 1024], BF16)
                gup = tile(lst, 'gup', [128, 1024], BF16)
                P.dma('pool', wup.a, I_['rw_wup'][l], writes=[wup])
                P.dma('pool', aup.a, I_['rw_aup'][l], writes=[aup])
                P.dma('pool', gup.a, I_['rw_gup'][l], writes=[gup])
                KTp = tile(lst, 'KTp', [128, 4, 256], BF16)
                Vp = tile(lst, 'Vp', [128, 2, 512], BF16)
                with ExitStack() as stk:
                    mT = tile(stk, 'mT', [128, 16, 256])
                    ld(mT, I_['memT'])
                    fm_norm(mT, 6, 256, dst=hT[:, :, 0:256], stk=stk)
                    kf = tile(stk, 'kf', [128, 4, 256])

                    def ev_k(bi, ps):
                        cp(kf[:, bi, :], ps, eng='act')
                        cp(KTp[:, bi, :], ps)
                    proj(WBF['x_wkv'][l], [0, 1, 2, 3], 16, lambda kb: hT.a[:, kb, 0:256], [hT], 256, ev_k)
                    st(O_['p_kT'][l].rearrange("h p m -> p h m"), kf)
                    vf = tile(stk, 'vf', [128, 2, 512])
                    for bi in range(4):
                        wb = wbuf[wi[0] % NWB]
                        wi[0] += 1
                        wv = V(wb.k, wb.a[:, 0:2048].rearrange("p (k c) -> p k c", c=128))
                        P.dma('sp', wv.a, WBF['x_wkv'][l][4 + bi], writes=[wb])
                        for mb in range(2):
                            ps = nextA()
                            for kb in range(16):
                                mm(ps[:, 0:128], hT[:, kb, mb * 128:(mb + 1) * 128], wv[:, kb, :],
                                   start=(kb == 0), stop=(kb == 15))
                            cp(vf[:, mb, bi * 128:(bi + 1) * 128], ps[:, 0:128], eng='act')
                            cp(Vp[:, mb, bi * 128:(bi + 1) * 128], ps[:, 0:128])
                    st(O_['p_v'][l], vf)
                    P.barrier()

                for gi, (t0, TG, kind) in enumerate(groups):
                    NCH = TG // 128
                    nseg = 1 if kind == 'p' else SB
                    slen = 128 // nseg
                    Cg = C[kind]
                    last_group = (kind == 'p' and t0 + TG == NPT) or kind == 's'
                    sub_begin(xcur, t0, TG, 0)
                    with ExitStack() as mst:
                        ysT = tile(mst, 'ysT', [128, 24, TG], BF16)
                        mixer(l, t0, TG, kind, NCH, nseg, slen, Cg, ysT, last_group)
                        P.barrier()
                        import os as _os
                        for _z in _os.environ.get('MK_ZERO', ''):
                            memset(ysT[:, int(_z) * 8:int(_z) * 8 + 8, :], 0.0, eng='dve')
                        mergedT = tile(mst, 'mergedT', [128, 16, TG], BF16)
                        acc = tile(mst, 'acc', [128, TG])
                        sg = tile(mst, 'sg', [128, TG])
                        for cb in range(16):
                            for c in range(3):
                                def ev_g(bi, ps):
                                    act(sg, ps, AF.Sigmoid)
                                proj(WBF['w_in'][l], [GT0 + c * 16 + cb], 16, lambda kb: hT.a[:, kb, 0:TG], [hT], TG, ev_g)

                                def ev_p(bi, ps, c=c):
                                    if c == 0:
                                        tt(acc, sg, ps, ALU.mult)
                                    else:
                                        tt(sg, sg, ps, ALU.mult)
                                        tt(acc, acc, sg, ALU.add, eng='pool')
                                proj(WBF['w_br'][l][c], [cb], 8, lambda kb, c=c: ysT.a[:, c * 8 + kb, :], [ysT], TG, ev_p)
                            cp(mergedT[:, cb, :], acc, eng='act')

                        def ev_o(bi, ps):
                            cp(postT[:, bi, 0:TG], ps, eng='act')
                        proj(WBF['w_out'][l], list(range(16)), 16, lambda kb: mergedT.a[:, kb, :], [mergedT], TG, ev_o)
                        P.barrier()
                    sub_end(xcur, xs[0], t0, TG, 1)
                    sub_begin(xs[0], t0, TG, 2)
                    with ExitStack() as ast:
                        qT = tile(ast, 'qT', [128, 4, TG], BF16)

                        def ev_q(bi, ps):
                            act(qT[:, bi, :], ps, AF.Copy, scale=128.0 ** -0.5)
                        proj(WBF['x_wq'][l], [0, 1, 2, 3], 16, lambda kb: hT.a[:, kb, 0:TG], [hT], TG, ev_q)
                        oT = tile(ast, 'oT', [128, 4, TG], BF16)
                        if kind == 's':
                            KTs = tile(ast, 'KTs', [128, SB, 4, 256], BF16)
                            Vs = tile(ast, 'Vs', [128, SB, 2, 512], BF16)
                            for b in range(SB):
                                P.dma('pool', KTs.a[:, b], I_['s_kT'][l, b].rearrange("h p m -> p h m"), writes=[KTs])
                                P.dma('pool', Vs.a[:, b], I_['s_v'][l, b], writes=[Vs])
                        eT = tile(ast, 'eT', [128, 2, 128], BF16)
                        rec = tile(ast, 'rec', [128, 128])
                        for ch in range(NCH):
                            cs = slice(ch * 128, (ch + 1) * 128)
                            for h in range(4):
                                pss = nextB()
                                pso = nextB()
                                if kind == 'p':
                                    for mb in range(2):
                                        mm(pss[:, mb * 128:(mb + 1) * 128], KTp[:, h, mb * 128:(mb + 1) * 128], qT[:, h, cs])
                                else:
                                    for b in range(SB):
                                        for mb in range(2):
                                            mm(pss[:, mb * 128 + b * 8:mb * 128 + b * 8 + 8],
                                               KTs[:, b, h, mb * 128:(mb + 1) * 128], qT[:, h, b * 8:b * 8 + 8])
                                act(eT, V(pss.k, pss.a[:, 0:256].rearrange("p (m t) -> p m t", t=128)), AF.Exp)
                                for mb in range(2):
                                    mm(pso[:, 128:256], ones_b, eT[:, mb, :], start=(mb == 0), stop=(mb == 1))
                                if kind == 'p':
                                    for mb in range(2):
                                        mm(pso[:, 0:128], Vp[:, mb, h * 128:(h + 1) * 128], eT[:, mb, :],
                                           start=(mb == 0), stop=(mb == 1))
                                else:
                                    for b in range(SB):
                                        for mb in range(2):
                                            mm(pso[:, b * 8:b * 8 + 8], Vs[:, b, mb, h * 128:(h + 1) * 128],
                                               eT[:, mb, b * 8:b * 8 + 8], start=(mb == 0), stop=(mb == 1))
                                recip(rec, pso[:, 128:256])
                                tt(oT[:, h, cs], pso[:, 0:128], rec, ALU.mult)

                        def ev_xo(bi, ps):
                            cp(postT[:, bi, 0:TG], ps, eng='act')
                        proj(WBF['x_wo'][l], list(range(16)), 4, lambda kb: oT.a[:, kb, :], [oT], TG, ev_xo)
                        P.barrier()
                    sub_end(xs[0], xs[0], t0, TG, 3)
                    sub_begin(xs[0], t0, TG, 4)
                    with ExitStack() as fst:
                        aT = tile(fst, 'aT', [128, 64, TG], BF16)
                        rl = [tile(fst, 'rl%d' % i, [128, TG]) for i in range(2)]

                        def ev_1(bi, ps):
                            r_ = rl[bi % 2]
                            act(r_, ps, AF.Relu)
                            tt(aT[:, bi, :], r_, r_, ALU.mult)
                        proj(WBF['ff_w1'][l], list(range(64)), 16, lambda kb: hT.a[:, kb, 0:TG], [hT], TG, ev_1)

                        def ev_2(bi, ps):
                            cp(postT[:, bi, 0:TG], ps, eng='act')
                        proj(WBF['ff_w2'][l], list(range(16)), 64, lambda kb: aT.a[:, kb, :], [aT], TG, ev_2)
                        P.barrier()
                    if l == DEPTH - 1:
                        sub_end(xs[0], O_['yT'], t0, TG, 5)
                    else:
                        sub_end(xs[0], xs[0], t0, TG, 5)
                P.barrier()
        P.finish()
    return nc


def _blk(w, kb):
    K, N = w.shape
    return np.ascontiguousarray(w.reshape(kb, 128, N // 128, 128).transpose(2, 1, 0, 3))


def _fm(v, nb):
    return np.ascontiguousarray(v.reshape(nb, 128).T)


def _pad_cols(w, blocks):
    out = np.zeros((w.shape[0], 128 * len(blocks)), w.dtype)
    for i, (s, n) in enumerate(blocks):
        out[:, i * 128:i * 128 + n] = w[:, s:s + n]
    return out


def _win_blocks():
    b = []
    o = 0
    for i in range(24):
        b.append((o + i * 128, 128))
    o = 3072
    b += [(o, 64), (o + 64, 64), (o + 128, 128)]
    o = 3328
    for i in range(32):
        b.append((o + i * 128, 128))
    o = 3328 + 4096
    b += [(o, 8), (o + 8, 8)]
    o = 3328 + 4112
    for i in range(32):
        b.append((o + i * 128, 128))
    o = 3328 + 4112 + 4096
    for i in range(48):
        b.append((o + i * 128, 128))
    assert len(b) == NBLK_IN
    return b


_NC_CACHE = {}


def kernel(**inp):
    inp = {k: np.asarray(v) for k, v in inp.items()}
    seq = inp['x_prompt'].shape[1]
    NPT = seq
    NT = NPT + 128
    if seq not in _NC_CACHE:
        _NC_CACHE[seq] = build(seq)
    nc = _NC_CACHE[seq]
    f32 = np.float32
    wb = _win_blocks()
    shared = {}
    shared['w_in'] = np.stack([_blk(_pad_cols(inp['w_in'][l], wb), 16) for l in range(DEPTH)])
    shared['w_br'] = np.stack([np.stack([_blk(inp['w_br'][l, c], 8) for c in range(3)]) for l in range(DEPTH)])
    shared['w_out'] = np.stack([_blk(inp['w_out'][l], 16) for l in range(DEPTH)])
    shared['x_wq'] = np.stack([_blk(inp['x_wq'][l], 16) for l in range(DEPTH)])
    shared['x_wkv'] = np.stack([_blk(inp['x_wkv'][l], 16) for l in range(DEPTH)])
    shared['x_wo'] = np.stack([_blk(inp['x_wo'][l], 4) for l in range(DEPTH)])
    shared['ff_w1'] = np.stack([_blk(inp['ff_w1'][l], 16) for l in range(DEPTH)])
    shared['ff_w2'] = np.stack([_blk(inp['ff_w2'][l], 64) for l in range(DEPTH)])
    gn = ['g_pre_mix', 'g_post_mix', 'g_pre_x', 'g_post_x', 'g_pre_ff', 'g_post_ff', 'g_mem']
    shared['gains'] = np.stack([np.stack([_fm(inp[n][l], 16) for n in gn], axis=1) for l in range(DEPTH)])
    rn = ['rw_w0', 'rw_a0', 'rw_k_k', 'rw_k_a', 'rw_r_k', 'rw_gn_g', 'rw_gn_b', 'ml_norm_g']
    shared['rwp'] = np.stack([np.stack([_fm(inp[n][l].reshape(-1), 8) for n in rn], axis=1) for l in range(DEPTH)])
    mu_pad = np.zeros((DEPTH, 27 * 128), f32)
    for l in range(DEPTH):
        mu_pad[l] = _pad_cols(inp['rw_mu'][l][None, :], wb[:27])[0]
    shared['rw_mu'] = np.stack([_fm(mu_pad[l], 27) for l in range(DEPTH)])
    shared['rw_wup'] = inp['rw_w_up']
    shared['rw_aup'] = inp['rw_a_up']
    shared['rw_gup'] = inp['rw_g_up']
    shared['ml_b'] = np.stack([inp['ml_i_b'], inp['ml_f_b']], axis=-1)
    for g_, sl in (('p', 128), ('s', 8)):
        c = make_consts(sl)
        shared['c_le_' + g_] = c['le']
        shared['c_lt_' + g_] = c['lt']
        shared['c_gt_' + g_] = c['gt']
        shared['c_dt_' + g_] = np.ascontiguousarray(c['dt'].transpose(1, 0, 2))
        shared['c_gp_' + g_] = np.ascontiguousarray(c['gp'].transpose(1, 0, 2))
        shared['c_kd_' + g_] = c['kd']
        shared['c_reset_' + g_] = c['reset']
        shared['c_el_' + g_] = c['elast']
        if g_ == 's':
            shared['c_segcol'] = c['segcol']
            shared['c_segrow'] = np.ascontiguousarray(np.broadcast_to(c['segcol'].T[None, :, :], (128, 16, 128)))
    pos = np.concatenate([np.arange(NPT), PAST_LEN + (np.arange(128) % 8)]).astype(f32)
    shared['c_cos'], shared['c_sin'] = rope_tables(pos)
    shared['c_ident'] = np.eye(128, dtype=f32)
    hb = np.zeros((128, 128), f32)
    hb[:64, :64] = 1
    hb[64:, 64:] = 1
    shared['c_half'] = hb
    selm = np.zeros((8, 8, 128), f32)
    for h in range(8):
        selm[h, h, :] = 1
    shared['c_sel'] = selm

    def fmT(x):
        return np.ascontiguousarray(x.T.reshape(16, 128, x.shape[0]).transpose(1, 0, 2))

    in_maps = []
    for c in range(NCORE):
        m = dict(shared)
        bp = c // 2
        bs = slice(c * SB, (c + 1) * SB)
        xtok = np.concatenate([inp['x_prompt'][bp], inp['x_sample'][bs].reshape(SB * DEC_T, D)], axis=0)
        m['xT'] = fmT(xtok)
        m['memT'] = fmT(inp['mem_prompt'][bp])
        sh = np.zeros((DEPTH, SB, 27 * 128), f32)
        for l in range(DEPTH):
            sh[l] = _pad_cols(inp['state_rwkv_shift'][l, bs], wb[:27])
        m['s_shift'] = np.ascontiguousarray(sh.reshape(DEPTH, SB, 27, 128).transpose(0, 3, 2, 1))
        srw = inp['state_rwkv'][:, bs]
        m['s_rw'] = np.ascontiguousarray(srw.transpose(0, 1, 2, 4, 3).reshape(DEPTH, SB, 8, 128, 64))
        m['s_mlc'] = np.ascontiguousarray(np.concatenate(
            [inp['state_mlstm_c'][:, bs], inp['state_mlstm_n'][:, bs][..., None]], axis=-1))
        m['s_mlm'] = np.ascontiguousarray(inp['state_mlstm_m'][:, bs].transpose(0, 2, 1))
        srt = inp['state_ret'][:, bs]
        m['s_rt'] = np.ascontiguousarray(srt.reshape(DEPTH, SB, 4, 2, 128, 256).transpose(0, 1, 2, 4, 3, 5))
        ck = inp['cache_mem_k'][:, bs]
        m['s_kT'] = np.ascontiguousarray(ck.transpose(0, 1, 3, 4, 2))
        cv = inp['cache_mem_v'][:, bs]
        m['s_v'] = np.ascontiguousarray(cv.reshape(DEPTH, SB, 2, 128, 512).transpose(0, 1, 3, 2, 4))
        in_maps.append(m)
    res = run_bass_kernel_spmd(nc, in_maps, core_ids=list(range(NCORE)))
    R = res.results

    def tokT(y):
        return y.transpose(1, 0, 2).reshape(D, -1).T

    nb = inp['x_prompt'].shape[0]
    y_p = np.stack([tokT(R[2 * b]['yT'][:, :, :NPT]) for b in range(nb)]).astype(f32)
    y_s = np.concatenate([tokT(R[c]['yT'][:, :, NPT:]).reshape(SB, DEC_T, D) for c in range(NCORE)]).astype(f32)

    def unpad_shift(a):
        return np.concatenate([a[..., 0:3072], a[..., 3072:3136], a[..., 3200:3264], a[..., 3328:3456]], axis=-1)

    pc = [2 * b for b in range(nb)]
    p_shift = np.stack([unpad_shift(R[c]['p_shift'].transpose(0, 2, 1).reshape(DEPTH, 27 * 128)) for c in pc], axis=1)
    p_rw = np.stack([R[c]['p_rw'].reshape(DEPTH, 16, 64, 64).transpose(0, 1, 3, 2) for c in pc], axis=1)
    p_mlc = np.stack([R[c]['p_mlc'][..., :128] for c in pc], axis=1)
    p_mln = np.stack([R[c]['p_mlc'][..., 128] for c in pc], axis=1)
    p_mlm = np.stack([R[c]['p_mlm'][..., 0] for c in pc], axis=1)
    p_rt = np.stack([R[c]['p_rt'].transpose(0, 1, 3, 2, 4).reshape(DEPTH, 4, 256, 256) for c in pc], axis=1)
    p_mk = np.stack([R[c]['p_kT'].transpose(0, 3, 1, 2) for c in pc], axis=1)
    p_mv = np.stack([R[c]['p_v'].transpose(0, 2, 1, 3).reshape(DEPTH, 256, 4, 128) for c in pc], axis=1)
    s_shift = np.concatenate([unpad_shift(R[c]['o_shift'].transpose(0, 3, 2, 1).reshape(DEPTH, SB, 27 * 128))
                              for c in range(NCORE)], axis=1)
    s_rw = np.concatenate([R[c]['o_rw'].reshape(DEPTH, SB, 16, 64, 64).transpose(0, 1, 2, 4, 3)
                           for c in range(NCORE)], axis=1)
    s_mlc = np.concatenate([R[c]['o_mlc'][..., :128] for c in range(NCORE)], axis=1)
    s_mln = np.concatenate([R[c]['o_mlc'][..., 128] for c in range(NCORE)], axis=1)
    s_mlm = np.concatenate([R[c]['o_mlm'].transpose(0, 2, 1) for c in range(NCORE)], axis=1)
    s_rt = np.concatenate([R[c]['o_rt'].transpose(0, 1, 2, 4, 3, 5).reshape(DEPTH, SB, 4, 256, 256)
                           for c in range(NCORE)], axis=1)
    outs = (y_p, y_s, p_shift, p_rw, p_mlc, p_mln, p_mlm, p_rt, p_mk, p_mv,
            s_shift, s_rw, s_mlc, s_mln, s_mlm, s_rt)
    return tuple(np.ascontiguousarray(o, dtype=f32) for o in outs)
```

```python
import math
import numpy as np
from contextlib import ExitStack
import concourse.bass as bass
import concourse.mybir as mybir
from concourse.bass_utils import run_bass_kernel_spmd

F32 = mybir.dt.float32
BF16 = mybir.dt.bfloat16
ALU = mybir.AluOpType
AF = mybir.ActivationFunctionType
AX = mybir.AxisListType

D = 2048
DEPTH = 2
NCORE = 8
BATCH = 4
SEQ = 2048
DEC_B = 128
DEC_T = 8
SB = DEC_B // NCORE
PAST_LEN = 16384
N_MEM = 256
MIXW = 1024
EPS = 1e-6
RW_GN_EPS = 64e-5
RW0 = 0
ML0 = 27
RT0 = 61
GT0 = 93
NBLK_IN = 141


class V:
    def __init__(s, k, a):
        s.k = k
        s.a = a

    def __getitem__(s, i):
        return V(s.k, s.a[i])

    def bc(s, shape):
        return V(s.k, s.a.to_broadcast(shape))


class Prog:
    ENGS = ('pe', 'act', 'dve', 'pool', 'sp')

    def __init__(self, nc, es, ndma=16):
        self.nc = nc
        self.e = {'pe': nc.tensor, 'act': nc.scalar, 'dve': nc.vector, 'pool': nc.gpsimd, 'sp': nc.sync}
        self.es = es
        self.epoch = {k: 0 for k in self.ENGS}
        self.sem = {(k, 0): es.enter_context(nc.semaphore('s_' + k + '0')) for k in self.ENGS}
        self.cnt = {k: 0 for k in self.ENGS}
        import os as _o
        self.EPOCH_MAX = int(_o.environ.get("MK_EPOCH", "30000"))
        self.dsem = [es.enter_context(nc.semaphore('d%d' % i)) for i in range(ndma)]
        self.dcnt = [0] * ndma
        self.dnext = 0
        self.dnext_pool = ndma // 2
        self.known = {k: {} for k in self.ENGS}
        self.lastw = {}
        self.readers = {}
        self.ninst = 0
        self.stopped = False
        self.nbar = 0
        import os
        self.stop_after = int(os.environ['MK_STOP']) if 'MK_STOP' in os.environ else None
        self.stop_i = int(os.environ['MK_STOPI']) if 'MK_STOPI' in os.environ else None

    def _wait(self, eng, ev):
        kind, who, val = ev
        if kind == 'c' and who[0] == eng and eng == 'pe':
            return
        key = (kind, who)
        if self.known[eng].get(key, 0) >= val:
            return
        if kind == 'c':
            for (kk, ww), vv in self.known[eng].items():
                if kk == 'c' and ww[0] == who[0] and ww[1] > who[1] and vv > 0:
                    return
        sem = self.sem[who] if kind == 'c' else self.dsem[who]
        self.e[eng].wait_ge(sem, val)
        self.known[eng][key] = val
        self.ninst += 1

    def _deps(self, eng, reads, writes):
        for r in reads:
            ev = self.lastw.get(r)
            if ev is not None:
                self._wait(eng, ev)
        for w in writes:
            ev = self.lastw.get(w)
            if ev is not None:
                self._wait(eng, ev)
            for ev in self.readers.get(w, ()):
                self._wait(eng, ev)

    def _commit(self, ev, reads, writes):
        for w in writes:
            self.lastw[w] = ev
            self.readers[w] = []
        for r in reads:
            if r in writes:
                continue
            l = self.readers.setdefault(r, [])
            l.append(ev)
            if len(l) > 40:
                d = {}
                for x in l:
                    d[(x[0], x[1])] = x
                self.readers[r] = list(d.values())

    def I(self, eng, fn, reads=(), writes=()):
        if self.stop_i is not None and self.ninst >= self.stop_i:
            self.stopped = True
        if self.stopped:
            return
        reads = [r.k if isinstance(r, V) else r for r in reads]
        writes = [w.k if isinstance(w, V) else w for w in writes]
        pr = [r for r in reads if r.startswith('ps')]
        if pr:
            writes = writes + [r for r in pr if r not in writes]
            reads = [r for r in reads if not r.startswith('ps')]
        self._deps(eng, reads, writes)
        ins = fn(self.e[eng])
        if self.cnt[eng] >= self.EPOCH_MAX:
            self.epoch[eng] += 1
            self.cnt[eng] = 0
            self.sem[(eng, self.epoch[eng])] = self.es.enter_context(
                self.nc.semaphore('s_%s%d' % (eng, self.epoch[eng])))
        self.cnt[eng] += 1
        ins.then_inc(self.sem[(eng, self.epoch[eng])], 1)
        self._commit(('c', (eng, self.epoch[eng]), self.cnt[eng]), reads, writes)
        self.ninst += 1

    def dma(self, q, out, in_, reads=(), writes=(), **kw):
        if self.stop_i is not None and self.ninst >= self.stop_i:
            self.stopped = True
        if self.stopped:
            return
        reads = [r.k if isinstance(r, V) else r for r in reads]
        writes = [w.k if isinstance(w, V) else w for w in writes]
        half = len(self.dsem) // 2
        if q == 'pool':
            i = self.dnext_pool
            self.dnext_pool = half + (self.dnext_pool - half + 1) % (len(self.dsem) - half)
        else:
            i = self.dnext
            self.dnext = (self.dnext + 1) % half
        if self.dcnt[i] > 0:
            self._wait(q, ('d', i, self.dcnt[i]))
        self._deps(q, reads, writes)
        ins = self.e[q].dma_start(out=out, in_=in_, **kw)
        self.dcnt[i] += 16
        ins.then_inc(self.dsem[i], 16)
        self._commit(('d', i, self.dcnt[i]), reads, writes)
        self.ninst += 1

    def barrier(self):
        if self.stopped:
            return
        self.nbar += 1
        if self.stop_after is not None and self.nbar >= self.stop_after:
            self.stopped = True
        for eng in self.ENGS:
            for i, v in enumerate(self.dcnt):
                if v:
                    self._wait(eng, ('d', i, v))
            for k in self.ENGS:
                if self.cnt[k]:
                    self._wait(eng, ('c', (k, self.epoch[k]), self.cnt[k]))
        self.lastw = {}
        self.readers = {}

    def finish(self):
        for i, v in enumerate(self.dcnt):
            if v:
                self._wait('sp', ('d', i, v))
        for k in self.ENGS:
            if k != 'sp' and self.cnt[k]:
                self._wait('sp', ('c', (k, self.epoch[k]), self.cnt[k]))


def _ret_gamma():
    lg = np.log(1.0 - np.exp(np.linspace(math.log(1.0 / 32), math.log(1.0 / 512), 4)))
    return np.exp(lg).astype(np.float64)


def make_consts(seglen):
    t = np.arange(128)
    seg = t // seglen
    tau = t % seglen
    same = seg[:, None] == seg[None, :]
    c = {}
    le = (same & (t[:, None] <= t[None, :])).astype(np.float32)
    lt = (same & (t[:, None] < t[None, :])).astype(np.float32)
    c['le'] = le
    c['lt'] = lt
    c['gt'] = lt.T.copy()
    g = _ret_gamma()
    dt = np.zeros((4, 128, 128), np.float32)
    gp = np.zeros((4, 128, 128), np.float32)
    kd = np.zeros((128, 4), np.float32)
    for h in range(4):
        diff = (t[None, :] - t[:, None]).clip(0)
        dt[h] = (g[h] ** diff) * le * (256.0 ** -0.5)
        gp[h] = np.broadcast_to((g[h] ** (tau + 1.0))[None, :], (128, 128))
        kd[:, h] = (g[h] ** (seglen - 1.0 - tau)) * (256.0 ** -0.5)
    c['dt'] = dt
    c['gp'] = gp
    c['kd'] = kd
    c['gl'] = [float(g[h] ** seglen) for h in range(4)]
    reset = np.ones((128, 128), np.float32)
    reset[:, tau == 0] = 0.0
    c['reset'] = reset
    nseg = 128 // seglen
    sm = np.zeros((128, nseg), np.float32)
    sm[t, seg] = 1.0
    c['segcol'] = sm
    el = np.zeros((128, 128), np.float32)
    el[t, seg * seglen + seglen - 1] = 1.0
    c['elast'] = el
    return c


def rope_tables(pos):
    inv = 10000.0 ** (-np.arange(128, dtype=np.float32) / 128.0)
    ang = pos[None, :].astype(np.float32) * inv[:, None].astype(np.float32)
    ang = ang.astype(np.float32)
    return np.cos(ang).astype(np.float32), np.sin(ang).astype(np.float32)


def build(seq=SEQ):
    NPT = seq
    NT = NPT + 128
    groups = []
    t0 = 0
    while t0 < NPT:
        tg = min(512, NPT - t0)
        groups.append((t0, tg, 'p'))
        t0 += tg
    groups.append((NPT, 128, 's'))
    NTILE = NT // 128

    nc = bass.Bass("TRN2", target_bir_lowering=False)
    dt_in = lambda n, s: nc.dram_tensor(n, list(s), F32, kind="ExternalInput").ap()
    dt_out = lambda n, s: nc.dram_tensor(n, list(s), F32, kind="ExternalOutput").ap()
    dt_scr = lambda n, s: nc.dram_tensor(n, list(s), F32, kind="Internal").ap()

    I_ = {}
    I_['xT'] = dt_in('xT', [128, 16, NT])
    I_['memT'] = dt_in('memT', [128, 16, 256])
    I_['w_in'] = dt_in('w_in', [DEPTH, NBLK_IN, 128, 16, 128])
    I_['w_br'] = dt_in('w_br', [DEPTH, 3, 16, 128, 8, 128])
    I_['w_out'] = dt_in('w_out', [DEPTH, 16, 128, 16, 128])
    I_['x_wq'] = dt_in('x_wq', [DEPTH, 4, 128, 16, 128])
    I_['x_wkv'] = dt_in('x_wkv', [DEPTH, 8, 128, 16, 128])
    I_['x_wo'] = dt_in('x_wo', [DEPTH, 16, 128, 4, 128])
    I_['ff_w1'] = dt_in('ff_w1', [DEPTH, 64, 128, 16, 128])
    I_['ff_w2'] = dt_in('ff_w2', [DEPTH, 16, 128, 64, 128])
    I_['gains'] = dt_in('gains', [DEPTH, 128, 7, 16])
    I_['rwp'] = dt_in('rwp', [DEPTH, 128, 8, 8])
    I_['rw_mu'] = dt_in('rw_mu', [DEPTH, 128, 27])
    I_['rw_wup'] = dt_in('rw_wup', [DEPTH, 64, 1024])
    I_['rw_aup'] = dt_in('rw_aup', [DEPTH, 64, 1024])
    I_['rw_gup'] = dt_in('rw_gup', [DEPTH, 128, 1024])
    I_['ml_b'] = dt_in('ml_b', [DEPTH, 8, 2])
    I_['s_shift'] = dt_in('s_shift', [DEPTH, 128, 27, SB])
    I_['s_rw'] = dt_in('s_rw', [DEPTH, SB, 8, 128, 64])
    I_['s_mlc'] = dt_in('s_mlc', [DEPTH, SB, 8, 128, 129])
    I_['s_mlm'] = dt_in('s_mlm', [DEPTH, 8, SB])
    I_['s_rt'] = dt_in('s_rt', [DEPTH, SB, 4, 128, 2, 256])
    I_['s_kT'] = dt_in('s_kT', [DEPTH, SB, 4, 128, 256])
    I_['s_v'] = dt_in('s_v', [DEPTH, SB, 128, 2, 512])
    for g_, sl in (('p', 128), ('s', 8)):
        I_['c_le_' + g_] = dt_in('c_le_' + g_, [128, 128])
        I_['c_lt_' + g_] = dt_in('c_lt_' + g_, [128, 128])
        I_['c_gt_' + g_] = dt_in('c_gt_' + g_, [128, 128])
        I_['c_dt_' + g_] = dt_in('c_dt_' + g_, [128, 4, 128])
        I_['c_gp_' + g_] = dt_in('c_gp_' + g_, [128, 4, 128])
        I_['c_kd_' + g_] = dt_in('c_kd_' + g_, [128, 4])
        I_['c_reset_' + g_] = dt_in('c_reset_' + g_, [128, 128])
        I_['c_el_' + g_] = dt_in('c_el_' + g_, [128, 128])
    I_['c_segcol'] = dt_in('c_segcol', [128, 16])
    I_['c_segrow'] = dt_in('c_segrow', [128, 16, 128])
    I_['c_cos'] = dt_in('c_cos', [128, NT])
    I_['c_sin'] = dt_in('c_sin', [128, NT])
    I_['c_ident'] = dt_in('c_ident', [128, 128])
    I_['c_half'] = dt_in('c_half', [128, 128])
    I_['c_sel'] = dt_in('c_sel', [8, 8, 128])

    O_ = {}
    O_['yT'] = dt_out('yT', [128, 16, NT])
    O_['p_shift'] = dt_out('p_shift', [DEPTH, 128, 27])
    O_['p_rw'] = dt_out('p_rw', [DEPTH, 8, 128, 64])
    O_['p_mlc'] = dt_out('p_mlc', [DEPTH, 8, 128, 129])
    O_['p_mlm'] = dt_out('p_mlm', [DEPTH, 8, 1])
    O_['p_rt'] = dt_out('p_rt', [DEPTH, 4, 128, 2, 256])
    O_['p_kT'] = dt_out('p_kT', [DEPTH, 4, 128, 256])
    O_['p_v'] = dt_out('p_v', [DEPTH, 128, 2, 512])
    O_['o_shift'] = dt_out('o_shift', [DEPTH, 128, 27, SB])
    O_['o_rw'] = dt_out('o_rw', [DEPTH, SB, 8, 128, 64])
    O_['o_mlc'] = dt_out('o_mlc', [DEPTH, SB, 8, 128, 129])
    O_['o_mlm'] = dt_out('o_mlm', [DEPTH, 8, SB])
    O_['o_rt'] = dt_out('o_rt', [DEPTH, SB, 4, 128, 2, 256])
    xs = [dt_scr('xs0', [128, 16, NT])]
    BIGW = ['w_in', 'w_br', 'w_out', 'x_wq', 'x_wkv', 'x_wo', 'ff_w1', 'ff_w2']
    WBF = {n: nc.dram_tensor('bf_' + n, list(I_[n].shape), BF16, kind="Internal").ap() for n in BIGW}

    with ExitStack() as es:
        P = Prog(nc, es)
        uid = [0]

        def tile(stk, name, shape, dt=F32):
            uid[0] += 1
            nm = '%s_%d' % (name, uid[0])
            t = stk.enter_context(nc.sbuf_tensor(nm, list(shape), dt))
            return V(nm, t[:])

        def ptile(name, shape, dt=F32):
            t = es.enter_context(nc.psum_tensor(name, list(shape), dt))
            return V(name, t[:])

        def mm(ps, lhsT, rhs, start=True, stop=True):
            P.I('pe', lambda e: e.matmul(ps.a, lhsT=lhsT.a, rhs=rhs.a, start=start, stop=stop),
                reads=[lhsT, rhs], writes=[ps])

        def tr(ps, in_, idn):
            P.I('pe', lambda e: e.transpose(ps.a, in_.a, idn.a), reads=[in_, idn], writes=[ps])

        def act(out, in_, func, bias=None, scale=1.0, eng='act'):
            rd = [in_]
            kw = {}
            if isinstance(bias, V):
                rd.append(bias)
                kw['bias'] = bias.a
            elif bias is not None:
                kw['bias'] = float(bias)
            P.I('act', lambda e: e.activation(out=out.a, in_=in_.a, func=func, scale=scale, **kw),
                reads=rd, writes=[out])

        def tt(out, a, b, op, eng='dve'):
            P.I(eng, lambda e: e.tensor_tensor(out=out.a, in0=a.a, in1=b.a, op=op), reads=[a, b], writes=[out])

        def ts(out, a, s1, op0, s2=None, op1=None, eng='dve'):
            rd = [a]
            v1 = s1
            if isinstance(s1, V):
                rd.append(s1)
                v1 = s1.a
            v2 = s2
            if isinstance(s2, V):
                rd.append(s2)
                v2 = s2.a
            if op1 is None:
                P.I(eng, lambda e: e.tensor_scalar(out=out.a, in0=a.a, scalar1=v1, scalar2=None, op0=op0),
                    reads=rd, writes=[out])
            else:
                P.I(eng, lambda e: e.tensor_scalar(out=out.a, in0=a.a, scalar1=v1, scalar2=v2, op0=op0, op1=op1),
                    reads=rd, writes=[out])

        def stt(out, a, s, b, op0, op1):
            rd = [a, b]
            sv = s
            if isinstance(s, V):
                rd.append(s)
                sv = s.a
            P.I('dve', lambda e: e.scalar_tensor_tensor(out=out.a, in0=a.a, scalar=sv, in1=b.a, op0=op0, op1=op1),
                reads=rd, writes=[out])

        def cp(out, in_, eng='dve'):
            if eng == 'act':
                act(out, in_, AF.Copy)
            else:
                P.I(eng, lambda e: e.tensor_copy(out=out.a, in_=in_.a), reads=[in_], writes=[out])

        def recip(out, in_):
            P.I('dve', lambda e: e.reciprocal(out=out.a, in_=in_.a), reads=[in_], writes=[out])

        def memset(t, val, eng='pool'):
            P.I(eng, lambda e: e.memset(t.a, val), writes=[t])

        def rsqrt_from(out, in_, scale, bias):
            act(out, in_, AF.Sqrt, bias=bias, scale=scale)
            recip(out, out)

        def ld(out, src, q='sp', **kw):
            P.dma(q, out.a, src, writes=[out], **kw)

        def st(dst, in_, q='sp', **kw):
            P.dma(q, dst, in_.a, reads=[in_], **kw)

        for n_ in BIGW:
            src = I_[n_]
            dst = WBF[n_]
            lead = list(src.shape[:-3])
            idxs = [()]
            for d_ in lead:
                idxs = [i + (j,) for i in idxs for j in range(d_)]
            for ix in idxs:
                sa, da = src, dst
                for j in ix:
                    sa = sa[j]
                    da = da[j]
                P.dma('pool', da, sa, writes=['wbf_' + n_], max_dma_last_dim=4096)
        P.barrier()

        hT = tile(es, 'hT', [128, 16, 512], BF16)
        postT = tile(es, 'postT', [128, 16, 512], F32)
        NWB = 8
        wbuf = [tile(es, 'wbuf%d' % i, [128, 2048], BF16) for i in range(NWB)]
        wi = [0]
        psA = [ptile('psA%d' % i, [128, 512]) for i in range(3)]
        pai = [0]
        psB = [ptile('psB%d' % i, [128, 512]) for i in range(5)]
        pbi = [0]

        def nextA():
            pai[0] = (pai[0] + 1) % 3
            return psA[pai[0]]

        def nextB():
            pbi[0] = (pbi[0] + 1) % 4
            return psB[pbi[0]]

        ident = tile(es, 'ident', [128, 128])
        ident_b = tile(es, 'ident_b', [128, 128], BF16)
        ones_b = tile(es, 'ones_b', [128, 128], BF16)
        half_b = tile(es, 'half_b', [128, 128], BF16)
        sel = tile(es, 'sel', [8, 8, 128])
        zero8 = tile(es, 'zero8', [8, 128])
        ld(ident, I_['c_ident'])
        cp(ident_b, ident)
        memset(ones_b, 1.0)
        memset(zero8, 0.0)
        P.dma('pool', half_b.a, I_['c_half'], writes=[half_b])
        ld(sel, I_['c_sel'])
        C = {}
        for g_ in ('p', 's'):
            C[g_] = {}
            for nm, shp in (('le', [128, 128]), ('lt', [128, 128]), ('gt', [128, 128]), ('dt', [128, 4, 128]),
                            ('gp', [128, 4, 128]), ('kd', [128, 4]), ('reset', [128, 128]), ('el', [128, 128])):
                C[g_][nm] = tile(es, 'c_%s_%s' % (nm, g_), shp)
                ld(C[g_][nm], I_['c_%s_%s' % (nm, g_)])
        segcol = tile(es, 'segcol', [128, 16])
        ld(segcol, I_['c_segcol'])
        gains = tile(es, 'gains', [128, 7, 16])
        rwp = tile(es, 'rwp', [128, 8, 8])
        negw0 = tile(es, 'negw0', [128, 8])
        mu = tile(es, 'mu', [128, 27])
        mlb = tile(es, 'mlb', [8, 2])
        carry = tile(es, 'carry', [128, 27])
        Trw = tile(es, 'Trw', [128, 8, 64])
        Cml = tile(es, 'Cml', [128, 8, 129])
        mml = tile(es, 'mml', [8, 1])
        Srt = tile(es, 'Srt', [128, 4, 2, 256])
        SrtB = tile(es, 'SrtB', [128, 4, 2, 256], BF16)
        CmlB = tile(es, 'CmlB', [128, 8, 128], BF16)
        TrwB = tile(es, 'TrwB', [128, 8, 64], BF16)
        NrepB = tile(es, 'NrepB', [128, 8, 128], BF16)

        def proj(Wd, blocks, KB, rhs_fn, rhs_keys, ntok, evac):
            for bi, blk in enumerate(blocks):
                ps = nextA()
                for q0 in range(0, KB, 16):
                    kq = min(16, KB - q0)
                    wb = wbuf[wi[0] % NWB]
                    wi[0] += 1
                    wv = V(wb.k, wb.a[:, 0:kq * 128].rearrange("p (k c) -> p k c", c=128))
                    P.dma('sp', wv.a, Wd[blk][:, q0:q0 + kq, :], writes=[wb])
                    for kb in range(kq):
                        P.I('pe', lambda e, kb=kb, q0=q0, wv=wv: e.matmul(
                            ps.a[:, 0:ntok], lhsT=wv.a[:, kb, :], rhs=rhs_fn(q0 + kb),
                            start=(q0 + kb == 0), stop=(q0 + kb == KB - 1)),
                            reads=[wb] + list(rhs_keys), writes=[ps])
                evac(bi, ps[:, 0:ntok])

        def fm_norm(src, gidx, TG, dst=None, resid=None, stk=None):
            sq = hT[:, :, 0:TG]
            act(sq, src, AF.Square)
            ps = nextB()
            for kb in range(16):
                mm(ps[:, 0:TG], ones_b, sq[:, kb, :], start=(kb == 0), stop=(kb == 15))
            rstd = tile(stk, 'rstd', [128, TG])
            rsqrt_from(rstd, ps[:, 0:TG], 1.0 / D, EPS)
            for kb in range(16):
                if resid is None:
                    stt(dst[:, kb, :], src[:, kb, :], gains[:, gidx, kb:kb + 1], rstd, ALU.mult, ALU.mult)
                else:
                    stt(src[:, kb, :], src[:, kb, :], gains[:, gidx, kb:kb + 1], rstd, ALU.mult, ALU.mult)
                    tt(resid[:, kb, :], resid[:, kb, :], src[:, kb, :], ALU.add, eng='pool')

        def sub_begin(xcur, t0, TG, gidx):
            with ExitStack() as stk:
                xT = tile(stk, 'xT', [128, 16, TG])
                ld(xT, xcur[:, :, t0:t0 + TG])
                fm_norm(xT, gidx, TG, dst=hT[:, :, 0:TG], stk=stk)
                P.barrier()

        def sub_end(xcur, xnext, t0, TG, gidx):
            with ExitStack() as stk:
                xT = tile(stk, 'xT', [128, 16, TG])
                ld(xT, xcur[:, :, t0:t0 + TG])
                fm_norm(postT[:, :, 0:TG], gidx, TG, resid=xT, stk=stk)
                st(xnext[:, :, t0:t0 + TG], xT)
                P.barrier()

        def scan(out, d0, d1, init, op0, op1):
            rd = [d0, d1]
            iv = init
            if isinstance(init, V):
                rd.append(init)
                iv = init.a
            P.I('dve', lambda e: e.tensor_tensor_scan(out=out.a, data0=d0.a, data1=d1.a, initial=iv, op0=op0, op1=op1),
                reads=rd, writes=[out])

        def mlstm(l, t0, TG, kind, NCH, nseg, slen, Cg, ysT, last_group):
            with ExitStack() as ms:
                gi = tile(ms, 'gi', [8, TG])
                gf = tile(ms, 'gf', [8, TG])

                def ev_i(bi, ps):
                    act(gi, ps[0:8, :], AF.Tanh, bias=mlb[:, 0:1], scale=1.0 / 15.0)

                def ev_f(bi, ps):
                    act(gf, ps[0:8, :], AF.Tanh, bias=mlb[:, 1:2], scale=1.0 / 15.0)
                proj(WBF['w_in'][l], [ML0 + 32], 16, lambda kb: hT.a[:, kb, 0:TG], [hT], TG, ev_i)
                proj(WBF['w_in'][l], [ML0 + 33], 16, lambda kb: hT.a[:, kb, 0:TG], [hT], TG, ev_f)
                ts(gi, gi, 15.0, ALU.mult)
                act(gf, gf, AF.Exp, scale=-15.0)
                act(gf, gf, AF.Ln, bias=1.0)
                ts(gf, gf, -1.0, ALU.mult)
                R_all = tile(ms, 'R_all', [8, NCH, 3, 128])
                gT_all = tile(ms, 'gT_all', [128, NCH, 8])
                bb = tile(ms, 'mbb', [8, 128])
                gg = tile(ms, 'mgg', [8, 128])
                cm = tile(ms, 'mcm', [8, 128])
                mm_ = tile(ms, 'mmm', [8, 128])
                m0tok = tile(ms, 'm0tok', [8, 128])
                if kind == 'p':
                    m0s = mml
                else:
                    m0s = tile(ms, 'm0s', [8, SB])
                    ld(m0s, I_['s_mlm'][l])
                for ch in range(NCH):
                    cs = slice(ch * 128, (ch + 1) * 128)
                    scan(bb, Cg['reset'][0:8, :], gf[:, cs], 0.0, ALU.mult, ALU.add)
                    tt(gg, gi[:, cs], bb, ALU.subtract)
                    for seg in range(nseg):
                        c0 = seg * slen
                        scan(cm[:, c0:c0 + slen], zero8[:, 0:slen], gg[:, c0:c0 + slen], m0s[:, seg:seg + 1],
                             ALU.add, ALU.max)
                        cp(m0tok[:, c0:c0 + slen], m0s[:, seg:seg + 1].bc([8, slen]))
                    tt(mm_, bb, cm, ALU.add)
                    ts(R_all[:, ch, 0, :], cm, -1.0, ALU.mult)
                    tt(m0tok, m0tok, cm, ALU.subtract)
                    act(R_all[:, ch, 1, :], m0tok, AF.Exp)
                    act(R_all[:, ch, 2, :], mm_, AF.Exp, scale=-1.0)
                    pst = nextB()
                    tr(pst[:, 0:8], gg, ident[0:8, 0:8])
                    cp(gT_all[:, ch, :], pst[:, 0:8])
                    for seg in range(nseg):
                        c0 = seg * slen
                        cp(m0s[:, seg:seg + 1], mm_[:, c0 + slen - 1:c0 + slen])
                if kind == 'p':
                    if last_group:
                        st(O_['p_mlm'][l], mml)
                else:
                    st(O_['o_mlm'][l], m0s)
                z4 = tile(ms, 'z4', [128, 4, TG])
                qb = tile(ms, 'mqb', [128, 128], BF16)
                kb_ = tile(ms, 'mkb', [128, 128], BF16)
                ktf = tile(ms, 'mktf', [128, 128])
                kw = tile(ms, 'mkw', [128, 128], BF16)
                vext = tile(ms, 'mvext', [128, 129], BF16)
                if kind == 's':
                    vm_all = tile(ms, 'mvm_all', [128, SB, 129], BF16)
                memset(vext, 1.0)
                bcs = tile(ms, 'mbcs', [128, 3, 128])
                ex = tile(ms, 'mex', [128, 128])
                Dx = tile(ms, 'mDx', [128, 128])
                w1 = tile(ms, 'mw1', [128, 128])
                wts = tile(ms, 'mwts', [128, 128], BF16)
                cd = tile(ms, 'mcd', [128, 2, 128])
                num = tile(ms, 'mnum', [128, 128])
                den = tile(ms, 'mden', [128, 128])
                sq = tile(ms, 'msq', [128, 128], BF16)
                rstd = tile(ms, 'mrstd', [128, 128])
                sgo = tile(ms, 'msgo', [128, 128])
                wend = tile(ms, 'mwend', [128, 1])
                if kind == 's':
                    Cxs = tile(ms, 'Cxs', [128, SB, 129])
                    CBs = tile(ms, 'CBs', [128, SB, 128], BF16)
                    NBs = tile(ms, 'NBs', [128, SB, 128], BF16)
                for h in range(8):
                    def ev(bi, ps):
                        cp(z4[:, bi, :], ps, eng='act')
                    proj(WBF['w_in'][l], [ML0 + h, ML0 + 8 + h, ML0 + 16 + h, ML0 + 24 + h], 16,
                         lambda kb: hT.a[:, kb, 0:TG], [hT], TG, ev)
                    if kind == 's':
                        ld(Cxs, I_['s_mlc'][l, :, h].rearrange("b p e -> p b e"))
                        cp(CBs, Cxs[:, :, 0:128], eng='pool')
                        cp(NBs, Cxs[:, :, 128:129].bc([128, SB, 128]), eng='pool')
                    for ch in range(NCH):
                        cs = slice(ch * 128, (ch + 1) * 128)
                        cp(qb, z4[:, 0, cs], eng='act')
                        ts(kb_, z4[:, 1, cs], 128.0 ** -0.5, ALU.mult, eng='pool')
                        pst = nextB()
                        tr(pst[:, 0:128], z4[:, 1, cs], ident)
                        ts(ktf, pst[:, 0:128], 128.0 ** -0.5, ALU.mult)
                        pst = nextB()
                        tr(pst[:, 0:128], z4[:, 2, cs], ident)
                        cp(vext[:, 0:128], pst[:, 0:128], eng='act')
                        psb = nextB()
                        mm(psb[:, 0:384], sel[:, h, :], V(R_all.k, R_all.a[:, ch].rearrange("p a b -> p (a b)")))
                        cp(bcs, V(psb.k, psb.a[:, 0:384].rearrange("p (a b) -> p a b", b=128)), eng='act')
                        pss = nextB()
                        mm(pss[:, 0:128], kb_, qb)
                        ts(ex, bcs[:, 0, :], gT_all[:, ch, h:h + 1], ALU.add, 0.0, ALU.min)
                        act(Dx, ex, AF.Exp)
                        tt(w1, Dx, Cg['le'], ALU.mult, eng='pool')
                        tt(wts, w1, pss[:, 0:128], ALU.mult)
                        psn = nextB()
                        mm(psn[:, 0:128], vext[:, 0:128], wts)
                        mm(psn[:, 128:256], ones_b, wts)
                        for seg in range(nseg):
                            c0 = seg * slen
                            CB = CmlB[:, h, :] if kind == 'p' else CBs[:, seg, :]
                            NB = NrepB[:, h, :] if kind == 'p' else NBs[:, seg, :]
                            mm(psn[:, 256 + c0:256 + c0 + slen], CB, qb[:, c0:c0 + slen])
                            mm(psn[:, 384 + c0:384 + c0 + slen], NB, qb[:, c0:c0 + slen])
                        cp(cd, V(psn.k, psn.a[:, 256:512].rearrange("p (a b) -> p a b", b=128)), eng='act')
                        tt(num, cd[:, 0, :], bcs[:, 1, :], ALU.mult)
                        tt(num, num, psn[:, 0:128], ALU.add)
                        tt(den, cd[:, 1, :], bcs[:, 1, :], ALU.mult)
                        tt(den, den, psn[:, 128:256], ALU.add)
                        stt(den, den, -1.0, den, ALU.mult, ALU.max)
                        tt(den, den, bcs[:, 2, :], ALU.max)
                        recip(den, den)
                        tt(num, num, den, ALU.mult)
                        act(sq, num, AF.Square)
                        psr = nextB()
                        mm(psr[:, 0:128], ones_b, sq)
                        rsqrt_from(rstd, psr[:, 0:128], 1.0 / 128.0, EPS)
                        act(sgo, z4[:, 3, cs], AF.Sigmoid)
                        stt(num, num, rwp[:, 7, h:h + 1], rstd, ALU.mult, ALU.mult)
                        tt(ysT[:, 8 + h, cs], num, sgo, ALU.mult)
                        tt(w1, Dx, Cg['el'], ALU.mult, eng='pool')
                        P.I('dve', lambda e: e.reduce_sum(out=wend.a, in_=w1.a, axis=AX.X), reads=[w1], writes=[wend])
                        ts(kw, ktf, wend[:, 0:1], ALU.mult)
                        if nseg > 1:
                            tt(vm_all, V(vext.k, vext.a.unsqueeze(1).to_broadcast([128, SB, 129])),
                               V(segcol.k, segcol.a.unsqueeze(2).to_broadcast([128, SB, 129])), ALU.mult)
                        for seg in range(nseg):
                            c0 = seg * slen
                            if nseg > 1:
                                v_ = vm_all[:, seg, :]
                            else:
                                v_ = vext
                            pd = nextB()
                            mm(pd[:, 0:129], kw, v_)
                            Cx = Cml[:, h, :] if kind == 'p' else Cxs[:, seg, :]
                            stt(Cx, Cx, bcs[:, 1, c0 + slen - 1:c0 + slen], pd[:, 0:129], ALU.mult, ALU.add)
                        if kind == 'p':
                            cp(CmlB[:, h, :], Cml[:, h, 0:128], eng='pool')
                            cp(NrepB[:, h, :], Cml[:, h, 128:129].bc([128, 128]), eng='pool')
                            if last_group and ch == NCH - 1:
                                st(O_['p_mlc'][l, h], Cml[:, h, :])
                    if kind == 's':
                        st(O_['o_mlc'][l, :, h].rearrange("b p e -> p b e"), Cxs)

        def rwkv(l, t0, TG, kind, NCH, nseg, slen, Cg, ysT, last_group):
            nlev = int(round(math.log2(slen))) - 1
            with ExitStack() as ws:
                pv = tile(ws, 'pv', [128, TG])
                if kind == 's':
                    sh0 = tile(ws, 'sh0', [128, 27, SB])
                    osh = tile(ws, 'osh', [128, 27, SB])
                    ld(sh0, I_['s_shift'][l])
                    segrow = tile(ws, 'segrow', [128, SB, 128])
                    ld(segrow, I_['c_segrow'])
                    atm_all = tile(ws, 'atm_all', [128, SB, 128], BF16)
                    Ts = tile(ws, 'Ts', [128, SB, 64])
                    TsB = tile(ws, 'TsB', [128, SB, 64], BF16)
                    ktm_all = tile(ws, 'ktm_all', [128, SB, 128], BF16)
                    btm_all = tile(ws, 'btm_all', [128, SB, 128], BF16)
                    segcol3 = V(segcol.k, segcol.a.unsqueeze(2).to_broadcast([128, SB, 128]))

                def tshift(u, blk):
                    if kind == 'p':
                        cp(pv[:, 0:1], carry[:, blk:blk + 1], eng='pool')
                        cp(pv[:, 1:TG], u[:, 0:TG - 1], eng='pool')
                        cp(carry[:, blk:blk + 1], u[:, TG - 1:TG], eng='pool')
                    else:
                        u3d = V(u.k, u.a.rearrange("p (b t) -> p b t", t=8))
                        p3d = V(pv.k, pv.a.rearrange("p (b t) -> p b t", t=8))
                        cp(p3d[:, :, 0:1], V(sh0.k, sh0.a[:, blk, :].unsqueeze(2)), eng='pool')
                        cp(p3d[:, :, 1:8], u3d[:, :, 0:7], eng='pool')
                        cp(V(osh.k, osh.a[:, blk, :].unsqueeze(2)), u3d[:, :, 7:8], eng='pool')
                    tt(pv, pv, u, ALU.subtract, eng='pool')
                    stt(u, pv, mu[:, blk:blk + 1], u, ALU.mult, ALU.add)

                tw = tile(ws, 'tw', [64, TG], BF16)
                ab = tile(ws, 'ab', [64, TG], BF16)
                sgd = tile(ws, 'sgd', [128, TG], BF16)
                with ExitStack() as us:
                    u3 = tile(us, 'u3', [128, 3, TG])

                    def ev3(bi, ps):
                        cp(u3[:, bi, :], ps, eng='act')
                    proj(WBF['w_in'][l], [RW0 + 24, RW0 + 25, RW0 + 26], 16, lambda kb: hT.a[:, kb, 0:TG], [hT], TG,
                         ev3)
                    for bi in range(3):
                        tshift(u3[:, bi, :], 24 + bi)
                    act(tw, u3[0:64, 0, :], AF.Tanh)
                    cp(ab, u3[0:64, 1, :])
                    act(sgd, u3[:, 2, :], AF.Sigmoid)
                    P.barrier()
                rkv = tile(ws, 'rkv', [128, 3, TG])
                e2 = tile(ws, 'e2', [128, TG])
                a_ = tile(ws, 'a_', [128, TG])
                g_ = tile(ws, 'g_', [128, TG])
                kk = tile(ws, 'kk', [128, TG])
                kkn = tile(ws, 'kkn', [128, TG])
                kmod = tile(ws, 'kmod', [128, TG])
                bvec = tile(ws, 'bvec', [128, TG])
                bon = tile(ws, 'bon', [128, TG])
                tmp = tile(ws, 'rtmp', [128, TG])
                tb = tile(ws, 'rtb', [128, TG], BF16)
                c = {}
                for nm in ('cwp', 'ecn', 'ecp', 'ktf', 'btf', 'dd', 'yf', 'cen', 'rs'):
                    c[nm] = tile(ws, 'rw_' + nm, [128, 128])
                for nm in ('rt', 'kt', 'bt', 'at', 'kt_tok', 'bt_tok', 'v_tok', 'yb', 'cb'):
                    c[nm] = tile(ws, 'rw_' + nm, [128, 128], BF16)
                Uneg = tile(ws, 'rw_Un', [128, 64], BF16)
                cjs = []
                for j_ in range(2):
                    cj_ = {}
                    for nm in ('X', 'Nn', 'X2', 'N2', 'Xb', 'Nb', 'PTf', 'Pf', 'tmpT'):
                        cj_[nm] = tile(ws, 'rwj%d_%s' % (j_, nm), [128, 128], F32)
                    for nm in ('MT', 'Aqk', 'Aqb'):
                        cj_[nm] = tile(ws, 'rwj%d_%s' % (j_, nm), [128, 128], BF16)
                    cj_['Gb'] = tile(ws, 'rwj%d_Gb' % j_, [128, 64], F32)
                    cj_['Uneg'] = tile(ws, 'rwj%d_Un' % j_, [128, 64], BF16)
                    cjs.append(cj_)
                for p in range(8):
                    pc = slice(p * 128, (p + 1) * 128)

                    def evr(bi, ps):
                        cp(rkv[:, bi, :], ps, eng='act')
                    proj(WBF['w_in'][l], [RW0 + p, RW0 + 8 + p, RW0 + 16 + p], 16,
                         lambda kb: hT.a[:, kb, 0:TG], [hT], TG, evr)
                    for bi in range(3):
                        tshift(rkv[:, bi, :], bi * 8 + p)
                    r = rkv[:, 0, :]
                    k = rkv[:, 1, :]
                    v = rkv[:, 2, :]
                    psw = nextB()
                    mm(psw[:, 0:TG], wup[:, pc], tw)
                    act(tmp, psw[:, 0:TG], AF.Exp, scale=-1.0, bias=negw0[:, p:p + 1])
                    act(tmp, tmp, AF.Ln, bias=1.0)
                    act(e2, tmp, AF.Exp, scale=-1.0, bias=-0.5)
                    psa = nextB()
                    mm(psa[:, 0:TG], aup[:, pc], ab)
                    act(a_, psa[:, 0:TG], AF.Sigmoid, bias=rwp[:, 1, p:p + 1])
                    psg = nextB()
                    mm(psg[:, 0:TG], gup[:, pc], sgd)
                    cp(g_, psg[:, 0:TG], eng='act')
                    ts(kk, k, rwp[:, 2, p:p + 1], ALU.mult)
                    act(tb, kk, AF.Square)
                    pss = nextB()
                    mm(pss[:, 0:TG], half_b, tb)
                    act(tmp, pss[:, 0:TG], AF.Sqrt)
                    ts(tmp, tmp, 1e-12, ALU.max)
                    recip(tmp, tmp)
                    tt(kkn, kk, tmp, ALU.mult)
                    ts(tmp, a_, -1.0, ALU.add, rwp[:, 3, p:p + 1], ALU.mult)
                    ts(tmp, tmp, 1.0, ALU.add)
                    tt(kmod, k, tmp, ALU.mult)
                    tt(bvec, kkn, a_, ALU.mult, eng='pool')
                    tt(tmp, r, kmod, ALU.mult)
                    ts(tb, tmp, rwp[:, 4, p:p + 1], ALU.mult)
                    psb = nextB()
                    mm(psb[:, 0:TG], half_b, tb)
                    tt(bon, psb[:, 0:TG], v, ALU.mult)
                    if kind == 's':
                        ld(Ts, I_['s_rw'][l, :, p].rearrange("b q v -> q b v"))
                        cp(TsB, Ts, eng='pool')
                    for ch in range(NCH):
                        cs = slice(ch * 128, (ch + 1) * 128)
                        scan(c['cwp'], Cg['reset'], e2[:, cs], 0.0, ALU.mult, ALU.add)
                        act(c['ecn'], c['cwp'], AF.Exp, scale=-1.0)
                        act(c['ecp'], c['cwp'], AF.Exp)
                        tt(c['rt'], r[:, cs], c['ecn'], ALU.mult)
                        tt(c['ktf'], kmod[:, cs], c['ecp'], ALU.mult)
                        cp(c['kt'], c['ktf'], eng='pool')
                        tt(c['btf'], bvec[:, cs], c['ecp'], ALU.mult)
                        cp(c['bt'], c['btf'], eng='pool')
                        tt(c['dd'], e2[:, cs], c['cwp'], ALU.subtract, eng='pool')
                        act(c['dd'], c['dd'], AF.Exp)
                        tt(c['at'], kkn[:, cs], c['dd'], ALU.mult)
                        for src, dst in ((c['ktf'], c['kt_tok']), (c['btf'], c['bt_tok']), (v[:, cs], c['v_tok'])):
                            pst = nextB()
                            tr(pst[:, 0:128], src, ident)
                            cp(dst, pst[:, 0:128], eng='act')
                        if kind == 's':
                            tt(atm_all, V(c['at'].k, c['at'].a.unsqueeze(1).to_broadcast([128, SB, 128])), segrow,
                               ALU.mult, eng='pool')
                            tt(ktm_all, V(c['kt_tok'].k, c['kt_tok'].a.unsqueeze(1).to_broadcast([128, SB, 128])),
                               segcol3, ALU.mult)
                            tt(btm_all, V(c['bt_tok'].k, c['bt_tok'].a.unsqueeze(1).to_broadcast([128, SB, 128])),
                               segcol3, ALU.mult)
                        psY = psB[4]
                        for j in range(2):
                            cj = cjs[j]
                            pr = slice(j * 64, (j + 1) * 64)
                            jc = slice(j * 64, (j + 1) * 64)
                            psN = nextB()
                            mm(psN[:, 0:128], c['bt'][pr, :], c['at'][pr, :])
                            mm(psN[:, 128:256], c['at'][pr, :], c['bt'][pr, :])
                            mm(psN[:, 256:384], c['kt'][pr, :], c['at'][pr, :])
                            tt(cj['X'], psN[:, 0:128], Cg['lt'], ALU.mult)
                            tt(cj['Nn'], psN[:, 128:256], Cg['gt'], ALU.mult)
                            tt(cj['MT'], psN[:, 256:384], Cg['lt'], ALU.mult)
                            psQ = nextB()
                            mm(psQ[:, 0:128], c['kt'][pr, :], c['rt'][pr, :])
                            mm(psQ[:, 128:256], c['bt'][pr, :], c['rt'][pr, :])
                            tt(cj['Aqk'], psQ[:, 0:128], Cg['le'], ALU.mult)
                            tt(cj['Aqb'], psQ[:, 128:256], Cg['le'], ALU.mult)
                            psG = nextB()
                            mm(psG[:, 0:64], cj['MT'], c['v_tok'][:, jc], start=True, stop=False)
                            for seg in range(nseg):
                                lhs = c['at'][pr, :] if kind == 'p' else atm_all[pr, seg, :]
                                T0b = TrwB[pr, p, :] if kind == 'p' else TsB[pr, seg, :]
                                mm(psG[:, 0:64], lhs, T0b, start=False, stop=(seg == nseg - 1))
                            cp(cj['Gb'], psG[:, 0:64], eng='act')
                            tt(cj['PTf'], ident, cj['X'], ALU.subtract, eng='pool')
                            tt(cj['Pf'], ident, cj['Nn'], ALU.subtract, eng='pool')
                        cur = [(cjs[j]['X'], cjs[j]['Nn']) for j in range(2)]
                        for lv in range(nlev):
                            for j in range(2):
                                cj = cjs[j]
                                Xk, Nk = cur[j]
                                X2 = cj['X2'] if lv % 2 == 0 else cj['Xb']
                                N2 = cj['N2'] if lv % 2 == 0 else cj['Nb']
                                psq = nextB()
                                mm(psq[:, 0:128], Nk, Xk)
                                mm(psq[:, 128:256], Xk, Nk)
                                cp(X2, psq[:, 0:128], eng='act')
                                cp(N2, psq[:, 128:256])
                                psp = nextB()
                                mm(psp[:, 0:128], cj['Pf'], X2)
                                mm(psp[:, 128:256], X2, cj['Pf'])
                                tt(cj['PTf'], cj['PTf'], psp[:, 0:128], ALU.add)
                                tt(cj['Pf'], cj['Pf'], psp[:, 128:256], ALU.add)
                                cur[j] = (X2, N2)
                        for j in range(2):
                            cj = cjs[j]
                            pr = slice(j * 64, (j + 1) * 64)
                            jc = slice(j * 64, (j + 1) * 64)
                            Uneg = cj['Uneg']
                            psU = nextB()
                            mm(psU[:, 0:64], cj['PTf'], cj['Gb'])
                            ts(Uneg, psU[:, 0:64], -1.0, ALU.mult)
                            mm(psY[pr, 0:128], c['v_tok'][:, jc], cj['Aqk'], start=True, stop=False)
                            mm(psY[pr, 0:128], Uneg, cj['Aqb'], start=False, stop=False)
                            for seg in range(nseg):
                                c0 = seg * slen
                                T0b = TrwB[pr, p, :] if kind == 'p' else TsB[pr, seg, :]
                                mm(psY[pr, c0:c0 + slen], T0b, c['rt'][pr, c0:c0 + slen], start=False,
                                   stop=(seg == nseg - 1))
                            for seg in range(nseg):
                                c0 = seg * slen
                                if nseg > 1:
                                    k_, b_ = ktm_all[:, seg, :], btm_all[:, seg, :]
                                else:
                                    k_, b_ = c['kt_tok'], c['bt_tok']
                                psD = nextB()
                                mm(psD[pr, 0:64], k_[:, pr], c['v_tok'][:, jc], start=True, stop=False)
                                mm(psD[pr, 0:64], b_[:, pr], Uneg, start=False, stop=True)
                                Tf = Trw[pr, p, :] if kind == 'p' else Ts[pr, seg, :]
                                tt(cj['tmpT'][pr, 0:64], Tf, psD[pr, 0:64], ALU.add)
                                ts(Tf, cj['tmpT'][pr, 0:64], c['ecn'][pr, c0 + slen - 1:c0 + slen], ALU.mult)
                        if kind == 'p':
                            cp(TrwB[:, p, :], Trw[:, p, :], eng='pool')
                            if last_group and ch == NCH - 1:
                                st(O_['p_rw'][l, p], Trw[:, p, :])
                        cp(c['yf'], psY[:, 0:128], eng='act')
                        cp(c['yb'], c['yf'], eng='pool')
                        psm = nextB()
                        mm(psm[:, 0:128], half_b, c['yb'])
                        stt(c['cen'], psm[:, 0:128], -1.0 / 64.0, c['yf'], ALU.mult, ALU.add)
                        act(c['cb'], c['cen'], AF.Square)
                        psv = nextB()
                        mm(psv[:, 0:128], half_b, c['cb'])
                        rsqrt_from(c['rs'], psv[:, 0:128], 1.0 / 64.0, RW_GN_EPS)
                        tt(c['cen'], c['cen'], c['rs'], ALU.mult)
                        ts(c['cen'], c['cen'], rwp[:, 5, p:p + 1], ALU.mult, rwp[:, 6, p:p + 1], ALU.add)
                        tt(c['cen'], c['cen'], bon[:, cs], ALU.add)
                        tt(ysT[:, p, cs], c['cen'], g_[:, cs], ALU.mult)
                    if kind == 's':
                        st(O_['o_rw'][l, :, p].rearrange("b q v -> q b v"), Ts)
                if kind == 'p':
                    if last_group:
                        st(O_['p_shift'][l], carry)
                else:
                    st(O_['o_shift'][l], osh)

        def mixer(l, t0, TG, kind, NCH, nseg, slen, Cg, ysT, last_group):
            rwkv(l, t0, TG, kind, NCH, nseg, slen, Cg, ysT, last_group)
            P.barrier()
            mlstm(l, t0, TG, kind, NCH, nseg, slen, Cg, ysT, last_group)
            P.barrier()
            gl = make_consts(slen)['gl']
            with ExitStack() as rs:
                z = tile(rs, 'rtz', [128, 8, TG])
                cosT = tile(rs, 'cosT', [128, TG])
                sinT = tile(rs, 'sinT', [128, TG])
                ld(cosT, I_['c_cos'][:, t0:t0 + TG])
                ld(sinT, I_['c_sin'][:, t0:t0 + TG])
                t1 = tile(rs, 't1', [128, 128])
                t2 = tile(rs, 't2', [128, 128])
                t3 = tile(rs, 't3', [128, 128])
                t4 = tile(rs, 't4', [128, 128])
                qrf = tile(rs, 'qrf', [128, 2, 128])
                krf = tile(rs, 'krf', [128, 2, 128])
                qr = tile(rs, 'qr', [128, 2, 128], BF16)
                kr = tile(rs, 'kr', [128, 2, 128], BF16)
                qs = tile(rs, 'qs', [128, 2, 128], BF16)
                ktok = tile(rs, 'ktok', [128, 256], BF16)
                ktm = tile(rs, 'ktm', [128, 256], BF16)
                vtok = tile(rs, 'vtok', [128, 256], BF16)
                AT = tile(rs, 'AT', [128, 128], BF16)
                yc = tile(rs, 'yc', [128, 2, 128])
                ysum = tile(rs, 'ysum', [128, 2, 128])
                ysq = tile(rs, 'ysq', [128, 2, 128], BF16)
                rstd = tile(rs, 'rrstd', [128, 128])
                sg = tile(rs, 'rsg', [128, 2, 128])
                Ssm = [tile(rs, 'Ssm%d' % i, [128, 2, 256]) for i in range(2)]
                Sbs = [tile(rs, 'Sbs%d' % i, [128, 2, 256], BF16) for i in range(2)]
                for h in range(4):
                    blocks = [RT0 + 2 * h, RT0 + 2 * h + 1, RT0 + 8 + 2 * h, RT0 + 9 + 2 * h,
                              RT0 + 16 + 2 * h, RT0 + 17 + 2 * h, RT0 + 24 + 2 * h, RT0 + 25 + 2 * h]

                    def ev(bi, ps):
                        cp(z[:, bi, :], ps, eng='act')
                    proj(WBF['w_in'][l], blocks, 16, lambda kb: hT.a[:, kb, 0:TG], [hT], TG, ev)
                    for ch in range(NCH):
                        cs = slice(ch * 128, (ch + 1) * 128)
                        for s0, dstf, dstb in ((0, qrf, qr), (2, krf, kr)):
                            x1 = z[:, s0, cs]
                            x2 = z[:, s0 + 1, cs]
                            tt(t1, x1, cosT[:, cs], ALU.mult)
                            tt(t2, x2, sinT[:, cs], ALU.mult)
                            tt(dstf[:, 0, :], t1, t2, ALU.subtract)
                            tt(t3, x1, sinT[:, cs], ALU.mult, eng='pool')
                            tt(t4, x2, cosT[:, cs], ALU.mult, eng='pool')
                            tt(dstf[:, 1, :], t3, t4, ALU.add, eng='pool')
                            cp(dstb, dstf, eng='act')
                        for b in range(2):
                            tt(qs[:, b, :], qrf[:, b, :], Cg['gp'][:, h, :], ALU.mult)
                        for b in range(2):
                            pst = nextB()
                            tr(pst[:, 0:128], krf[:, b, :], ident)
                            ts(ktok[:, b * 128:(b + 1) * 128], pst[:, 0:128], Cg['kd'][:, h:h + 1], ALU.mult)
                            pst = nextB()
                            tr(pst[:, 0:128], z[:, 4 + b, cs], ident)
                            cp(vtok[:, b * 128:(b + 1) * 128], pst[:, 0:128], eng='act')
                        pss = nextB()
                        for b in range(2):
                            mm(pss[:, 0:128], kr[:, b, :], qr[:, b, :], start=(b == 0), stop=(b == 1))
                        tt(AT, pss[:, 0:128], Cg['dt'][:, h, :], ALU.mult)
                        psy = psB[4]
                        for eb in range(2):
                            mm(psy[:, eb * 128:(eb + 1) * 128], vtok[:, eb * 128:(eb + 1) * 128], AT)
                        for seg in range(nseg):
                            if kind == 'p':
                                S = Srt[:, h]
                                Sb = SrtB[:, h]
                            else:
                                S = Ssm[seg % 2]
                                Sb = Sbs[seg % 2]
                                ld(S, I_['s_rt'][l, seg, h])
                                cp(Sb, S, eng='pool')
                            c0 = seg * slen
                            for eb in range(2):
                                for db in range(2):
                                    mm(psy[:, 256 + eb * 128 + c0:256 + eb * 128 + c0 + slen],
                                       Sb[:, db, eb * 128:(eb + 1) * 128], qs[:, db, c0:c0 + slen],
                                       start=(db == 0), stop=(db == 1))
                            if nseg > 1:
                                ts(ktm, ktok, segcol[:, seg:seg + 1], ALU.mult)
                                kt_ = ktm
                            else:
                                kt_ = ktok
                            for db in range(2):
                                pd = nextB()
                                mm(pd[:, 0:256], kt_[:, db * 128:(db + 1) * 128], vtok)
                                stt(S[:, db, :], S[:, db, :], gl[h], pd[:, 0:256], ALU.mult, ALU.add)
                            if kind == 'p':
                                cp(Sb, S, eng='pool')
                                if last_group and ch == NCH - 1:
                                    st(O_['p_rt'][l, h], S)
                            else:
                                st(O_['o_rt'][l, seg, h], S)
                        cp(yc, V(psy.k, psy.a[:, 256:512].rearrange("p (e t) -> p e t", t=128)), eng='act')
                        tt(ysum, V(psy.k, psy.a[:, 0:256].rearrange("p (e t) -> p e t", t=128)), yc, ALU.add)
                        act(ysq, ysum, AF.Square)
                        pn = nextB()
                        for eb in range(2):
                            mm(pn[:, 0:128], ones_b, ysq[:, eb, :], start=(eb == 0), stop=(eb == 1))
                        rsqrt_from(rstd, pn[:, 0:128], 1.0 / 256.0, EPS)
                        act(sg, z[:, 6:8, cs], AF.Sigmoid)
                        tt(sg, sg, z[:, 6:8, cs], ALU.mult, eng='pool')
                        for eb in range(2):
                            tt(ysum[:, eb, :], ysum[:, eb, :], rstd, ALU.mult)
                            tt(ysT[:, 16 + 2 * h + eb, cs], ysum[:, eb, :], sg[:, eb, :], ALU.mult)

        for l in range(DEPTH):
            xcur = I_['xT'] if l == 0 else xs[0]
            xmid = xs[0] if l == 0 else xs[0]
            ld(gains, I_['gains'][l])
            ld(rwp, I_['rwp'][l])
            ld(mu, I_['rw_mu'][l])
            ld(mlb, I_['ml_b'][l])
            ts(negw0, rwp[:, 0, :], -1.0, ALU.mult)
            ts(mlb, mlb, 1.0 / 15.0, ALU.mult)
            memset(carry, 0.0)
            memset(Trw, 0.0)
            memset(Cml, 0.0)
            memset(mml, 0.0)
            memset(Srt, 0.0)
            memset(SrtB, 0.0)
            memset(CmlB, 0.0)
            memset(TrwB, 0.0)
            memset(NrepB, 0.0)
            with ExitStack() as lst:
                wup = tile(lst, 'wup', [64, 1024], BF16)
                aup = tile(lst, 'aup', [64, 1024], BF16)
                gup = tile(lst, 'gup', [128, 1024], BF16)
                P.dma('pool', wup.a, I_['rw_wup'][l], writes=[wup])
                P.dma('pool', aup.a, I_['rw_aup'][l], writes=[aup])
                P.dma('pool', gup.a, I_['rw_gup'][l], writes=[gup])
                KTp = tile(lst, 'KTp', [128, 4, 256], BF16)
                Vp = tile(lst, 'Vp', [128, 2, 512], BF16)
                with ExitStack() as stk:
                    mT = tile(stk, 'mT', [128, 16, 256])
                    ld(mT, I_['memT'])
                    fm_norm(mT, 6, 256, dst=hT[:, :, 0:256], stk=stk)
                    kf = tile(stk, 'kf', [128, 4, 256])

                    def ev_k(bi, ps):
                        cp(kf[:, bi, :], ps, eng='act')
                        cp(KTp[:, bi, :], ps)
                    proj(WBF['x_wkv'][l], [0, 1, 2, 3], 16, lambda kb: hT.a[:, kb, 0:256], [hT], 256, ev_k)
                    st(O_['p_kT'][l].rearrange("h p m -> p h m"), kf)
                    vf = tile(stk, 'vf', [128, 2, 512])
                    for bi in range(4):
                        wb = wbuf[wi[0] % NWB]
                        wi[0] += 1
                        wv = V(wb.k, wb.a[:, 0:2048].rearrange("p (k c) -> p k c", c=128))
                        P.dma('sp', wv.a, WBF['x_wkv'][l][4 + bi], writes=[wb])
                        for mb in range(2):
                            ps = nextA()
                            for kb in range(16):
                                mm(ps[:, 0:128], hT[:, kb, mb * 128:(mb + 1) * 128], wv[:, kb, :],
                                   start=(kb == 0), stop=(kb == 15))
                            cp(vf[:, mb, bi * 128:(bi + 1) * 128], ps[:, 0:128], eng='act')
                            cp(Vp[:, mb, bi * 128:(bi + 1) * 128], ps[:, 0:128])
                    st(O_['p_v'][l], vf)
                    P.barrier()

                for gi, (t0, TG, kind) in enumerate(groups):
                    NCH = TG // 128
                    nseg = 1 if kind == 'p' else SB
                    slen = 128 // nseg
                    Cg = C[kind]
                    last_group = (kind == 'p' and t0 + TG == NPT) or kind == 's'
                    sub_begin(xcur, t0, TG, 0)
                    with ExitStack() as mst:
                        ysT = tile(mst, 'ysT', [128, 24, TG], BF16)
                        mixer(l, t0, TG, kind, NCH, nseg, slen, Cg, ysT, last_group)
                        P.barrier()
                        import os as _os
                        for _z in _os.environ.get('MK_ZERO', ''):
                            memset(ysT[:, int(_z) * 8:int(_z) * 8 + 8, :], 0.0, eng='dve')
                        mergedT = tile(mst, 'mergedT', [128, 16, TG], BF16)
                        acc = tile(mst, 'acc', [128, TG])
                        sg = tile(mst, 'sg', [128, TG])
                        for cb in range(16):
                            for c in range(3):
                                def ev_g(bi, ps):
                                    act(sg, ps, AF.Sigmoid)
                                proj(WBF['w_in'][l], [GT0 + c * 16 + cb], 16, lambda kb: hT.a[:, kb, 0:TG], [hT], TG, ev_g)

                                def ev_p(bi, ps, c=c):
                                    if c == 0:
                                        tt(acc, sg, ps, ALU.mult)
                                    else:
                                        tt(sg, sg, ps, ALU.mult)
                                        tt(acc, acc, sg, ALU.add, eng='pool')
                                proj(WBF['w_br'][l][c], [cb], 8, lambda kb, c=c: ysT.a[:, c * 8 + kb, :], [ysT], TG, ev_p)
                            cp(mergedT[:, cb, :], acc, eng='act')

                        def ev_o(bi, ps):
                            cp(postT[:, bi, 0:TG], ps, eng='act')
                        proj(WBF['w_out'][l], list(range(16)), 16, lambda kb: mergedT.a[:, kb, :], [mergedT], TG, ev_o)
                        P.barrier()
                    sub_end(xcur, xs[0], t0, TG, 1)
                    sub_begin(xs[0], t0, TG, 2)
                    with ExitStack() as ast:
                        qT = tile(ast, 'qT', [128, 4, TG], BF16)

                        def ev_q(bi, ps):
                            act(qT[:, bi, :], ps, AF.Copy, scale=128.0 ** -0.5)
                        proj(WBF['x_wq'][l], [0, 1, 2, 3], 16, lambda kb: hT.a[:, kb, 0:TG], [hT], TG, ev_q)
                        oT = tile(ast, 'oT', [128, 4, TG], BF16)
                        if kind == 's':
                            KTs = tile(ast, 'KTs', [128, SB, 4, 256], BF16)
                            Vs = tile(ast, 'Vs', [128, SB, 2, 512], BF16)
                            for b in range(SB):
                                P.dma('pool', KTs.a[:, b], I_['s_kT'][l, b].rearrange("h p m -> p h m"), writes=[KTs])
                                P.dma('pool', Vs.a[:, b], I_['s_v'][l, b], writes=[Vs])
                        eT = tile(ast, 'eT', [128, 2, 128], BF16)
                        rec = tile(ast, 'rec', [128, 128])
                        for ch in range(NCH):
                            cs = slice(ch * 128, (ch + 1) * 128)
                            for h in range(4):
                                pss = nextB()
                                pso = nextB()
                                if kind == 'p':
                                    for mb in range(2):
                                        mm(pss[:, mb * 128:(mb + 1) * 128], KTp[:, h, mb * 128:(mb + 1) * 128], qT[:, h, cs])
                                else:
                                    for b in range(SB):
                                        for mb in range(2):
                                            mm(pss[:, mb * 128 + b * 8:mb * 128 + b * 8 + 8],
                                               KTs[:, b, h, mb * 128:(mb + 1) * 128], qT[:, h, b * 8:b * 8 + 8])
                                act(eT, V(pss.k, pss.a[:, 0:256].rearrange("p (m t) -> p m t", t=128)), AF.Exp)
                                for mb in range(2):
                                    mm(pso[:, 128:256], ones_b, eT[:, mb, :], start=(mb == 0), stop=(mb == 1))
                                if kind == 'p':
                                    for mb in range(2):
                                        mm(pso[:, 0:128], Vp[:, mb, h * 128:(h + 1) * 128], eT[:, mb, :],
                                           start=(mb == 0), stop=(mb == 1))
                                else:
                                    for b in range(SB):
                                        for mb in range(2):
                                            mm(pso[:, b * 8:b * 8 + 8], Vs[:, b, mb, h * 128:(h + 1) * 128],
                                               eT[:, mb, b * 8:b * 8 + 8], start=(mb == 0), stop=(mb == 1))
                                recip(rec, pso[:, 128:256])
                                tt(oT[:, h, cs], pso[:, 0:128], rec, ALU.mult)

                        def ev_xo(bi, ps):
                            cp(postT[:, bi, 0:TG], ps, eng='act')
                        proj(WBF['x_wo'][l], list(range(16)), 4, lambda kb: oT.a[:, kb, :], [oT], TG, ev_xo)
                        P.barrier()
                    sub_end(xs[0], xs[0], t0, TG, 3)
                    sub_begin(xs[0], t0, TG, 4)
                    with ExitStack() as fst:
                        aT = tile(fst, 'aT', [128, 64, TG], BF16)
                        rl = [tile(fst, 'rl%d' % i, [128, TG]) for i in range(2)]

                        def ev_1(bi, ps):
                            r_ = rl[bi % 2]
                            act(r_, ps, AF.Relu)
                            tt(aT[:, bi, :], r_, r_, ALU.mult)
                        proj(WBF['ff_w1'][l], list(range(64)), 16, lambda kb: hT.a[:, kb, 0:TG], [hT], TG, ev_1)

                        def ev_2(bi, ps):
                            cp(postT[:, bi, 0:TG], ps, eng='act')
                        proj(WBF['ff_w2'][l], list(range(16)), 64, lambda kb: aT.a[:, kb, :], [aT], TG, ev_2)
                        P.barrier()
                    if l == DEPTH - 1:
                        sub_end(xs[0], O_['yT'], t0, TG, 5)
                    else:
                        sub_end(xs[0], xs[0], t0, TG, 5)
                P.barrier()
        P.finish()
    return nc


def _blk(w, kb):
    K, N = w.shape
    return np.ascontiguousarray(w.reshape(kb, 128, N // 128, 128).transpose(2, 1, 0, 3))


def _fm(v, nb):
    return np.ascontiguousarray(v.reshape(nb, 128).T)


def _pad_cols(w, blocks):
    out = np.zeros((w.shape[0], 128 * len(blocks)), w.dtype)
    for i, (s, n) in enumerate(blocks):
        out[:, i * 128:i * 128 + n] = w[:, s:s + n]
    return out


def _win_blocks():
    b = []
    o = 0
    for i in range(24):
        b.append((o + i * 128, 128))
    o = 3072
    b += [(o, 64), (o + 64, 64), (o + 128, 128)]
    o = 3328
    for i in range(32):
        b.append((o + i * 128, 128))
    o = 3328 + 4096
    b += [(o, 8), (o + 8, 8)]
    o = 3328 + 4112
    for i in range(32):
        b.append((o + i * 128, 128))
    o = 3328 + 4112 + 4096
    for i in range(48):
        b.append((o + i * 128, 128))
    assert len(b) == NBLK_IN
    return b


_NC_CACHE = {}


def kernel(**inp):
    inp = {k: np.asarray(v) for k, v in inp.items()}
    seq = inp['x_prompt'].shape[1]
    NPT = seq
    NT = NPT + 128
    if seq not in _NC_CACHE:
        _NC_CACHE[seq] = build(seq)
    nc = _NC_CACHE[seq]
    f32 = np.float32
    wb = _win_blocks()
    shared = {}
    shared['w_in'] = np.stack([_blk(_pad_cols(inp['w_in'][l], wb), 16) for l in range(DEPTH)])
    shared['w_br'] = np.stack([np.stack([_blk(inp['w_br'][l, c], 8) for c in range(3)]) for l in range(DEPTH)])
    shared['w_out'] = np.stack([_blk(inp['w_out'][l], 16) for l in range(DEPTH)])
    shared['x_wq'] = np.stack([_blk(inp['x_wq'][l], 16) for l in range(DEPTH)])
    shared['x_wkv'] = np.stack([_blk(inp['x_wkv'][l], 16) for l in range(DEPTH)])
    shared['x_wo'] = np.stack([_blk(inp['x_wo'][l], 4) for l in range(DEPTH)])
    shared['ff_w1'] = np.stack([_blk(inp['ff_w1'][l], 16) for l in range(DEPTH)])
    shared['ff_w2'] = np.stack([_blk(inp['ff_w2'][l], 64) for l in range(DEPTH)])
    gn = ['g_pre_mix', 'g_post_mix', 'g_pre_x', 'g_post_x', 'g_pre_ff', 'g_post_ff', 'g_mem']
    shared['gains'] = np.stack([np.stack([_fm(inp[n][l], 16) for n in gn], axis=1) for l in range(DEPTH)])
    rn = ['rw_w0', 'rw_a0', 'rw_k_k', 'rw_k_a', 'rw_r_k', 'rw_gn_g', 'rw_gn_b', 'ml_norm_g']
    shared['rwp'] = np.stack([np.stack([_fm(inp[n][l].reshape(-1), 8) for n in rn], axis=1) for l in range(DEPTH)])
    mu_pad = np.zeros((DEPTH, 27 * 128), f32)
    for l in range(DEPTH):
        mu_pad[l] = _pad_cols(inp['rw_mu'][l][None, :], wb[:27])[0]
    shared['rw_mu'] = np.stack([_fm(mu_pad[l], 27) for l in range(DEPTH)])
    shared['rw_wup'] = inp['rw_w_up']
    shared['rw_aup'] = inp['rw_a_up']
    shared['rw_gup'] = inp['rw_g_up']
    shared['ml_b'] = np.stack([inp['ml_i_b'], inp['ml_f_b']], axis=-1)
    for g_, sl in (('p', 128), ('s', 8)):
        c = make_consts(sl)
        shared['c_le_' + g_] = c['le']
        shared['c_lt_' + g_] = c['lt']
        shared['c_gt_' + g_] = c['gt']
        shared['c_dt_' + g_] = np.ascontiguousarray(c['dt'].transpose(1, 0, 2))
        shared['c_gp_' + g_] = np.ascontiguousarray(c['gp'].transpose(1, 0, 2))
        shared['c_kd_' + g_] = c['kd']
        shared['c_reset_' + g_] = c['reset']
        shared['c_el_' + g_] = c['elast']
        if g_ == 's':
            shared['c_segcol'] = c['segcol']
            shared['c_segrow'] = np.ascontiguousarray(np.broadcast_to(c['segcol'].T[None, :, :], (128, 16, 128)))
    pos = np.concatenate([np.arange(NPT), PAST_LEN + (np.arange(128) % 8)]).astype(f32)
    shared['c_cos'], shared['c_sin'] = rope_tables(pos)
    shared['c_ident'] = np.eye(128, dtype=f32)
    hb = np.zeros((128, 128), f32)
    hb[:64, :64] = 1
    hb[64:, 64:] = 1
    shared['c_half'] = hb
    selm = np.zeros((8, 8, 128), f32)
    for h in range(8):
        selm[h, h, :] = 1
    shared['c_sel'] = selm

    def fmT(x):
        return np.ascontiguousarray(x.T.reshape(16, 128, x.shape[0]).transpose(1, 0, 2))

    in_maps = []
    for c in range(NCORE):
        m = dict(shared)
        bp = c // 2
        bs = slice(c * SB, (c + 1) * SB)
        xtok = np.concatenate([inp['x_prompt'][bp], inp['x_sample'][bs].reshape(SB * DEC_T, D)], axis=0)
        m['xT'] = fmT(xtok)
        m['memT'] = fmT(inp['mem_prompt'][bp])
        sh = np.zeros((DEPTH, SB, 27 * 128), f32)
        for l in range(DEPTH):
            sh[l] = _pad_cols(inp['state_rwkv_shift'][l, bs], wb[:27])
        m['s_shift'] = np.ascontiguousarray(sh.reshape(DEPTH, SB, 27, 128).transpose(0, 3, 2, 1))
        srw = inp['state_rwkv'][:, bs]
        m['s_rw'] = np.ascontiguousarray(srw.transpose(0, 1, 2, 4, 3).reshape(DEPTH, SB, 8, 128, 64))
        m['s_mlc'] = np.ascontiguousarray(np.concatenate(
            [inp['state_mlstm_c'][:, bs], inp['state_mlstm_n'][:, bs][..., None]], axis=-1))
        m['s_mlm'] = np.ascontiguousarray(inp['state_mlstm_m'][:, bs].transpose(0, 2, 1))
        srt = inp['state_ret'][:, bs]
        m['s_rt'] = np.ascontiguousarray(srt.reshape(DEPTH, SB, 4, 2, 128, 256).transpose(0, 1, 2, 4, 3, 5))
        ck = inp['cache_mem_k'][:, bs]
        m['s_kT'] = np.ascontiguousarray(ck.transpose(0, 1, 3, 4, 2))
        cv = inp['cache_mem_v'][:, bs]
        m['s_v'] = np.ascontiguousarray(cv.reshape(DEPTH, SB, 2, 128, 512).transpose(0, 1, 3, 2, 4))
        in_maps.append(m)
    res = run_bass_kernel_spmd(nc, in_maps, core_ids=list(range(NCORE)))
    R = res.results

    def tokT(y):
        return y.transpose(1, 0, 2).reshape(D, -1).T

    nb = inp['x_prompt'].shape[0]
    y_p = np.stack([tokT(R[2 * b]['yT'][:, :, :NPT]) for b in range(nb)]).astype(f32)
    y_s = np.concatenate([tokT(R[c]['yT'][:, :, NPT:]).reshape(SB, DEC_T, D) for c in range(NCORE)]).astype(f32)

    def unpad_shift(a):
        return np.concatenate([a[..., 0:3072], a[..., 3072:3136], a[..., 3200:3264], a[..., 3328:3456]], axis=-1)

    pc = [2 * b for b in range(nb)]
    p_shift = np.stack([unpad_shift(R[c]['p_shift'].transpose(0, 2, 1).reshape(DEPTH, 27 * 128)) for c in pc], axis=1)
    p_rw = np.stack([R[c]['p_rw'].reshape(DEPTH, 16, 64, 64).transpose(0, 1, 3, 2) for c in pc], axis=1)
    p_mlc = np.stack([R[c]['p_mlc'][..., :128] for c in pc], axis=1)
    p_mln = np.stack([R[c]['p_mlc'][..., 128] for c in pc], axis=1)
    p_mlm = np.stack([R[c]['p_mlm'][..., 0] for c in pc], axis=1)
    p_rt = np.stack([R[c]['p_rt'].transpose(0, 1, 3, 2, 4).reshape(DEPTH, 4, 256, 256) for c in pc], axis=1)
    p_mk = np.stack([R[c]['p_kT'].transpose(0, 3, 1, 2) for c in pc], axis=1)
    p_mv = np.stack([R[c]['p_v'].transpose(0, 2, 1, 3).reshape(DEPTH, 256, 4, 128) for c in pc], axis=1)
    s_shift = np.concatenate([unpad_shift(R[c]['o_shift'].transpose(0, 3, 2, 1).reshape(DEPTH, SB, 27 * 128))
                              for c in range(NCORE)], axis=1)
    s_rw = np.concatenate([R[c]['o_rw'].reshape(DEPTH, SB, 16, 64, 64).transpose(0, 1, 2, 4, 3)
                           for c in range(NCORE)], axis=1)
    s_mlc = np.concatenate([R[c]['o_mlc'][..., :128] for c in range(NCORE)], axis=1)
    s_mln = np.concatenate([R[c]['o_mlc'][..., 128] for c in range(NCORE)], axis=1)
    s_mlm = np.concatenate([R[c]['o_mlm'].transpose(0, 2, 1) for c in range(NCORE)], axis=1)
    s_rt = np.concatenate([R[c]['o_rt'].transpose(0, 1, 2, 4, 3, 5).reshape(DEPTH, SB, 4, 256, 256)
                           for c in range(NCORE)], axis=1)
    outs = (y_p, y_s, p_shift, p_rw, p_mlc, p_mln, p_mlm, p_rt, p_mk, p_mv,
            s_shift, s_rw, s_mlc, s_mln, s_mlm, s_rt)
    return tuple(np.ascontiguousarray(o, dtype=f32) for o in outs)
```

```python
import math
import numpy as np
from contextlib import ExitStack
import concourse.bass as bass
import concourse.mybir as mybir
from concourse.bass_utils import run_bass_kernel_spmd

F32 = mybir.dt.float32
BF16 = mybir.dt.bfloat16
ALU = mybir.AluOpType
AF = mybir.ActivationFunctionType
AX = mybir.AxisListType

D = 2048
DEPTH = 2
NCORE = 8
BATCH = 4
SEQ = 2048
DEC_B = 128
DEC_T = 8
SB = DEC_B // NCORE
PAST_LEN = 16384
N_MEM = 256
MIXW = 1024
EPS = 1e-6
RW_GN_EPS = 64e-5
RW0 = 0
ML0 = 27
RT0 = 61
GT0 = 93
NBLK_IN = 141


class V:
    def __init__(s, k, a):
        s.k = k
        s.a = a

    def __getitem__(s, i):
        return V(s.k, s.a[i])

    def bc(s, shape):
        return V(s.k, s.a.to_broadcast(shape))


class Prog:
    ENGS = ('pe', 'act', 'dve', 'pool', 'sp')

    def __init__(self, nc, es, ndma=16):
        self.nc = nc
        self.e = {'pe': nc.tensor, 'act': nc.scalar, 'dve': nc.vector, 'pool': nc.gpsimd, 'sp': nc.sync}
        self.es = es
        self.epoch = {k: 0 for k in self.ENGS}
        self.sem = {(k, 0): es.enter_context(nc.semaphore('s_' + k + '0')) for k in self.ENGS}
        self.cnt = {k: 0 for k in self.ENGS}
        import os as _o
        self.EPOCH_MAX = int(_o.environ.get("MK_EPOCH", "30000"))
        self.dsem = [es.enter_context(nc.semaphore('d%d' % i)) for i in range(ndma)]
        self.dcnt = [0] * ndma
        self.dnext = 0
        self.dnext_pool = ndma // 2
        self.known = {k: {} for k in self.ENGS}
        self.lastw = {}
        self.readers = {}
        self.ninst = 0
        self.stopped = False
        self.nbar = 0
        import os
        self.stop_after = int(os.environ['MK_STOP']) if 'MK_STOP' in os.environ else None
        self.stop_i = int(os.environ['MK_STOPI']) if 'MK_STOPI' in os.environ else None

    def _wait(self, eng, ev):
        kind, who, val = ev
        if kind == 'c' and who[0] == eng and eng == 'pe':
            return
        key = (kind, who)
        if self.known[eng].get(key, 0) >= val:
            return
        if kind == 'c':
            for (kk, ww), vv in self.known[eng].items():
                if kk == 'c' and ww[0] == who[0] and ww[1] > who[1] and vv > 0:
                    return
        sem = self.sem[who] if kind == 'c' else self.dsem[who]
        self.e[eng].wait_ge(sem, val)
        self.known[eng][key] = val
        self.ninst += 1

    def _deps(self, eng, reads, writes):
        for r in reads:
            ev = self.lastw.get(r)
            if ev is not None:
                self._wait(eng, ev)
        for w in writes:
            ev = self.lastw.get(w)
            if ev is not None:
                self._wait(eng, ev)
            for ev in self.readers.get(w, ()):
                self._wait(eng, ev)

    def _commit(self, ev, reads, writes):
        for w in writes:
            self.lastw[w] = ev
            self.readers[w] = []
        for r in reads:
            if r in writes:
                continue
            l = self.readers.setdefault(r, [])
            l.append(ev)
            if len(l) > 40:
                d = {}
                for x in l:
                    d[(x[0], x[1])] = x
                self.readers[r] = list(d.values())

    def I(self, eng, fn, reads=(), writes=()):
        if self.stop_i is not None and self.ninst >= self.stop_i:
            self.stopped = True
        if self.stopped:
            return
        reads = [r.k if isinstance(r, V) else r for r in reads]
        writes = [w.k if isinstance(w, V) else w for w in writes]
        pr = [r for r in reads if r.startswith('ps')]
        if pr:
            writes = writes + [r for r in pr if r not in writes]
            reads = [r for r in reads if not r.startswith('ps')]
        self._deps(eng, reads, writes)
        ins = fn(self.e[eng])
        if self.cnt[eng] >= self.EPOCH_MAX:
            self.epoch[eng] += 1
            self.cnt[eng] = 0
            self.sem[(eng, self.epoch[eng])] = self.es.enter_context(
                self.nc.semaphore('s_%s%d' % (eng, self.epoch[eng])))
        self.cnt[eng] += 1
        ins.then_inc(self.sem[(eng, self.epoch[eng])], 1)
        self._commit(('c', (eng, self.epoch[eng]), self.cnt[eng]), reads, writes)
        self.ninst += 1

    def dma(self, q, out, in_, reads=(), writes=(), **kw):
        if self.stop_i is not None and self.ninst >= self.stop_i:
            self.stopped = True
        if self.stopped:
            return
        reads = [r.k if isinstance(r, V) else r for r in reads]
        writes = [w.k if isinstance(w, V) else w for w in writes]
        half = len(self.dsem) // 2
        if q == 'pool':
            i = self.dnext_pool
            self.dnext_pool = half + (self.dnext_pool - half + 1) % (len(self.dsem) - half)
        else:
            i = self.dnext
            self.dnext = (self.dnext + 1) % half
        if self.dcnt[i] > 0:
            self._wait(q, ('d', i, self.dcnt[i]))
        self._deps(q, reads, writes)
        ins = self.e[q].dma_start(out=out, in_=in_, **kw)
        self.dcnt[i] += 16
        ins.then_inc(self.dsem[i], 16)
        self._commit(('d', i, self.dcnt[i]), reads, writes)
        self.ninst += 1

    def barrier(self):
        if self.stopped:
            return
        self.nbar += 1
        if self.stop_after is not None and self.nbar >= self.stop_after:
            self.stopped = True
        for eng in self.ENGS:
            for i, v in enumerate(self.dcnt):
                if v:
                    self._wait(eng, ('d', i, v))
            for k in self.ENGS:
                if self.cnt[k]:
                    self._wait(eng, ('c', (k, self.epoch[k]), self.cnt[k]))
        self.lastw = {}
        self.readers = {}

    def finish(self):
        for i, v in enumerate(self.dcnt):
            if v:
                self._wait('sp', ('d', i, v))
        for k in self.ENGS:
            if k != 'sp' and self.cnt[k]:
                self._wait('sp', ('c', (k, self.epoch[k]), self.cnt[k]))


def _ret_gamma():
    lg = np.log(1.0 - np.exp(np.linspace(math.log(1.0 / 32), math.log(1.0 / 512), 4)))
    return np.exp(lg).astype(np.float64)


def make_consts(seglen):
    t = np.arange(128)
    seg = t // seglen
    tau = t % seglen
    same = seg[:, None] == seg[None, :]
    c = {}
    le = (same & (t[:, None] <= t[None, :])).astype(np.float32)
    lt = (same & (t[:, None] < t[None, :])).astype(np.float32)
    c['le'] = le
    c['lt'] = lt
    c['gt'] = lt.T.copy()
    g = _ret_gamma()
    dt = np.zeros((4, 128, 128), np.float32)
    gp = np.zeros((4, 128, 128), np.float32)
    kd = np.zeros((128, 4), np.float32)
    for h in range(4):
        diff = (t[None, :] - t[:, None]).clip(0)
        dt[h] = (g[h] ** diff) * le * (256.0 ** -0.5)
        gp[h] = np.broadcast_to((g[h] ** (tau + 1.0))[None, :], (128, 128))
        kd[:, h] = (g[h] ** (seglen - 1.0 - tau)) * (256.0 ** -0.5)
    c['dt'] = dt
    c['gp'] = gp
    c['kd'] = kd
    c['gl'] = [float(g[h] ** seglen) for h in range(4)]
    reset = np.ones((128, 128), np.float32)
    reset[:, tau == 0] = 0.0
    c['reset'] = reset
    nseg = 128 // seglen
    sm = np.zeros((128, nseg), np.float32)
    sm[t, seg] = 1.0
    c['segcol'] = sm
    el = np.zeros((128, 128), np.float32)
    el[t, seg * seglen + seglen - 1] = 1.0
    c['elast'] = el
    return c


def rope_tables(pos):
    inv = 10000.0 ** (-np.arange(128, dtype=np.float32) / 128.0)
    ang = pos[None, :].astype(np.float32) * inv[:, None].astype(np.float32)
    ang = ang.astype(np.float32)
    return np.cos(ang).astype(np.float32), np.sin(ang).astype(np.float32)


def build(seq=SEQ):
    NPT = seq
    NT = NPT + 128
    groups = []
    t0 = 0
    while t0 < NPT:
        tg = min(512, NPT - t0)
        groups.append((t0, tg, 'p'))
        t0 += tg
    groups.append((NPT, 128, 's'))
    NTILE = NT // 128

    nc = bass.Bass("TRN2", target_bir_lowering=False)
    dt_in = lambda n, s: nc.dram_tensor(n, list(s), F32, kind="ExternalInput").ap()
    dt_out = lambda n, s: nc.dram_tensor(n, list(s), F32, kind="ExternalOutput").ap()
    dt_scr = lambda n, s: nc.dram_tensor(n, list(s), F32, kind="Internal").ap()

    I_ = {}
    I_['xT'] = dt_in('xT', [128, 16, NT])
    I_['memT'] = dt_in('memT', [128, 16, 256])
    I_['w_in'] = dt_in('w_in', [DEPTH, NBLK_IN, 128, 16, 128])
    I_['w_br'] = dt_in('w_br', [DEPTH, 3, 16, 128, 8, 128])
    I_['w_out'] = dt_in('w_out', [DEPTH, 16, 128, 16, 128])
    I_['x_wq'] = dt_in('x_wq', [DEPTH, 4, 128, 16, 128])
    I_['x_wkv'] = dt_in('x_wkv', [DEPTH, 8, 128, 16, 128])
    I_['x_wo'] = dt_in('x_wo', [DEPTH, 16, 128, 4, 128])
    I_['ff_w1'] = dt_in('ff_w1', [DEPTH, 64, 128, 16, 128])
    I_['ff_w2'] = dt_in('ff_w2', [DEPTH, 16, 128, 64, 128])
    I_['gains'] = dt_in('gains', [DEPTH, 128, 7, 16])
    I_['rwp'] = dt_in('rwp', [DEPTH, 128, 8, 8])
    I_['rw_mu'] = dt_in('rw_mu', [DEPTH, 128, 27])
    I_['rw_wup'] = dt_in('rw_wup', [DEPTH, 64, 1024])
    I_['rw_aup'] = dt_in('rw_aup', [DEPTH, 64, 1024])
    I_['rw_gup'] = dt_in('rw_gup', [DEPTH, 128, 1024])
    I_['ml_b'] = dt_in('ml_b', [DEPTH, 8, 2])
    I_['s_shift'] = dt_in('s_shift', [DEPTH, 128, 27, SB])
    I_['s_rw'] = dt_in('s_rw', [DEPTH, SB, 8, 128, 64])
    I_['s_mlc'] = dt_in('s_mlc', [DEPTH, SB, 8, 128, 129])
    I_['s_mlm'] = dt_in('s_mlm', [DEPTH, 8, SB])
    I_['s_rt'] = dt_in('s_rt', [DEPTH, SB, 4, 128, 2, 256])
    I_['s_kT'] = dt_in('s_kT', [DEPTH, SB, 4, 128, 256])
    I_['s_v'] = dt_in('s_v', [DEPTH, SB, 128, 2, 512])
    for g_, sl in (('p', 128), ('s', 8)):
        I_['c_le_' + g_] = dt_in('c_le_' + g_, [128, 128])
        I_['c_lt_' + g_] = dt_in('c_lt_' + g_, [128, 128])
        I_['c_gt_' + g_] = dt_in('c_gt_' + g_, [128, 128])
        I_['c_dt_' + g_] = dt_in('c_dt_' + g_, [128, 4, 128])
        I_['c_gp_' + g_] = dt_in('c_gp_' + g_, [128, 4, 128])
        I_['c_kd_' + g_] = dt_in('c_kd_' + g_, [128, 4])
        I_['c_reset_' + g_] = dt_in('c_reset_' + g_, [128, 128])
        I_['c_el_' + g_] = dt_in('c_el_' + g_, [128, 128])
    I_['c_segcol'] = dt_in('c_segcol', [128, 16])
    I_['c_segrow'] = dt_in('c_segrow', [128, 16, 128])
    I_['c_cos'] = dt_in('c_cos', [128, NT])
    I_['c_sin'] = dt_in('c_sin', [128, NT])
    I_['c_ident'] = dt_in('c_ident', [128, 128])
    I_['c_half'] = dt_in('c_half', [128, 128])
    I_['c_sel'] = dt_in('c_sel', [8, 8, 128])

    O_ = {}
    O_['yT'] = dt_out('yT', [128, 16, NT])
    O_['p_shift'] = dt_out('p_shift', [DEPTH, 128, 27])
    O_['p_rw'] = dt_out('p_rw', [DEPTH, 8, 128, 64])
    O_['p_mlc'] = dt_out('p_mlc', [DEPTH, 8, 128, 129])
    O_['p_mlm'] = dt_out('p_mlm', [DEPTH, 8, 1])
    O_['p_rt'] = dt_out('p_rt', [DEPTH, 4, 128, 2, 256])
    O_['p_kT'] = dt_out('p_kT', [DEPTH, 4, 128, 256])
    O_['p_v'] = dt_out('p_v', [DEPTH, 128, 2, 512])
    O_['o_shift'] = dt_out('o_shift', [DEPTH, 128, 27, SB])
    O_['o_rw'] = dt_out('o_rw', [DEPTH, SB, 8, 128, 64])
    O_['o_mlc'] = dt_out('o_mlc', [DEPTH, SB, 8, 128, 129])
    O_['o_mlm'] = dt_out('o_mlm', [DEPTH, 8, SB])
    O_['o_rt'] = dt_out('o_rt', [DEPTH, SB, 4, 128, 2, 256])
    xs = [dt_scr('xs0', [128, 16, NT])]
    BIGW = ['w_in', 'w_br', 'w_out', 'x_wq', 'x_wkv', 'x_wo', 'ff_w1', 'ff_w2']
    WBF = {n: nc.dram_tensor('bf_' + n, list(I_[n].shape), BF16, kind="Internal").ap() for n in BIGW}

    with ExitStack() as es:
        P = Prog(nc, es)
        uid = [0]

        def tile(stk, name, shape, dt=F32):
            uid[0] += 1
            nm = '%s_%d' % (name, uid[0])
            t = stk.enter_context(nc.sbuf_tensor(nm, list(shape), dt))
            return V(nm, t[:])

        def ptile(name, shape, dt=F32):
            t = es.enter_context(nc.psum_tensor(name, list(shape), dt))
            return V(name, t[:])

        def mm(ps, lhsT, rhs, start=True, stop=True):
            P.I('pe', lambda e: e.matmul(ps.a, lhsT=lhsT.a, rhs=rhs.a, start=start, stop=stop),
                reads=[lhsT, rhs], writes=[ps])

        def tr(ps, in_, idn):
            P.I('pe', lambda e: e.transpose(ps.a, in_.a, idn.a), reads=[in_, idn], writes=[ps])

        def act(out, in_, func, bias=None, scale=1.0, eng='act'):
            rd = [in_]
            kw = {}
            if isinstance(bias, V):
                rd.append(bias)
                kw['bias'] = bias.a
            elif bias is not None:
                kw['bias'] = float(bias)
            P.I('act', lambda e: e.activation(out=out.a, in_=in_.a, func=func, scale=scale, **kw),
                reads=rd, writes=[out])

        def tt(out, a, b, op, eng='dve'):
            P.I(eng, lambda e: e.tensor_tensor(out=out.a, in0=a.a, in1=b.a, op=op), reads=[a, b], writes=[out])

        def ts(out, a, s1, op0, s2=None, op1=None, eng='dve'):
            rd = [a]
            v1 = s1
            if isinstance(s1, V):
                rd.append(s1)
                v1 = s1.a
            v2 = s2
            if isinstance(s2, V):
                rd.append(s2)
                v2 = s2.a
            if op1 is None:
                P.I(eng, lambda e: e.tensor_scalar(out=out.a, in0=a.a, scalar1=v1, scalar2=None, op0=op0),
                    reads=rd, writes=[out])
            else:
                P.I(eng, lambda e: e.tensor_scalar(out=out.a, in0=a.a, scalar1=v1, scalar2=v2, op0=op0, op1=op1),
                    reads=rd, writes=[out])

        def stt(out, a, s, b, op0, op1):
            rd = [a, b]
            sv = s
            if isinstance(s, V):
                rd.append(s)
                sv = s.a
            P.I('dve', lambda e: e.scalar_tensor_tensor(out=out.a, in0=a.a, scalar=sv, in1=b.a, op0=op0, op1=op1),
                reads=rd, writes=[out])

        def cp(out, in_, eng='dve'):
            if eng == 'act':
                act(out, in_, AF.Copy)
            else:
                P.I(eng, lambda e: e.tensor_copy(out=out.a, in_=in_.a), reads=[in_], writes=[out])

        def recip(out, in_):
            P.I('dve', lambda e: e.reciprocal(out=out.a, in_=in_.a), reads=[in_], writes=[out])

        def memset(t, val, eng='pool'):
            P.I(eng, lambda e: e.memset(t.a, val), writes=[t])

        def rsqrt_from(out, in_, scale, bias):
            act(out, in_, AF.Sqrt, bias=bias, scale=scale)
            recip(out, out)

        def ld(out, src, q='sp', **kw):
            P.dma(q, out.a, src, writes=[out], **kw)

        def st(dst, in_, q='sp', **kw):
            P.dma(q, dst, in_.a, reads=[in_], **kw)

        for n_ in BIGW:
            src = I_[n_]
            dst = WBF[n_]
            lead = list(src.shape[:-3])
            idxs = [()]
            for d_ in lead:
                idxs = [i + (j,) for i in idxs for j in range(d_)]
            for ix in idxs:
                sa, da = src, dst
                for j in ix:
                    sa = sa[j]
                    da = da[j]
                P.dma('pool', da, sa, writes=['wbf_' + n_], max_dma_last_dim=4096)
        P.barrier()

        hT = tile(es, 'hT', [128, 16, 512], BF16)
        postT = tile(es, 'postT', [128, 16, 512], F32)
        NWB = 8
        wbuf = [tile(es, 'wbuf%d' % i, [128, 2048], BF16) for i in range(NWB)]
        wi = [0]
        psA = [ptile('psA%d' % i, [128, 512]) for i in range(3)]
        pai = [0]
        psB = [ptile('psB%d' % i, [128, 512]) for i in range(5)]
        pbi = [0]

        def nextA():
            pai[0] = (pai[0] + 1) % 3
            return psA[pai[0]]

        def nextB():
            pbi[0] = (pbi[0] + 1) % 4
            return psB[pbi[0]]

        ident = tile(es, 'ident', [128, 128])
        ident_b = tile(es, 'ident_b', [128, 128], BF16)
        ones_b = tile(es, 'ones_b', [128, 128], BF16)
        half_b = tile(es, 'half_b', [128, 128], BF16)
        sel = tile(es, 'sel', [8, 8, 128])
        zero8 = tile(es, 'zero8', [8, 128])
        ld(ident, I_['c_ident'])
        cp(ident_b, ident)
        memset(ones_b, 1.0)
        memset(zero8, 0.0)
        P.dma('pool', half_b.a, I_['c_half'], writes=[half_b])
        ld(sel, I_['c_sel'])
        C = {}
        for g_ in ('p', 's'):
            C[g_] = {}
            for nm, shp in (('le', [128, 128]), ('lt', [128, 128]), ('gt', [128, 128]), ('dt', [128, 4, 128]),
                            ('gp', [128, 4, 128]), ('kd', [128, 4]), ('reset', [128, 128]), ('el', [128, 128])):
                C[g_][nm] = tile(es, 'c_%s_%s' % (nm, g_), shp)
                ld(C[g_][nm], I_['c_%s_%s' % (nm, g_)])
        segcol = tile(es, 'segcol', [128, 16])
        ld(segcol, I_['c_segcol'])
        gains = tile(es, 'gains', [128, 7, 16])
        rwp = tile(es, 'rwp', [128, 8, 8])
        negw0 = tile(es, 'negw0', [128, 8])
        mu = tile(es, 'mu', [128, 27])
        mlb = tile(es, 'mlb', [8, 2])
        carry = tile(es, 'carry', [128, 27])
        Trw = tile(es, 'Trw', [128, 8, 64])
        Cml = tile(es, 'Cml', [128, 8, 129])
        mml = tile(es, 'mml', [8, 1])
        Srt = tile(es, 'Srt', [128, 4, 2, 256])
        SrtB = tile(es, 'SrtB', [128, 4, 2, 256], BF16)
        CmlB = tile(es, 'CmlB', [128, 8, 128], BF16)
        TrwB = tile(es, 'TrwB', [128, 8, 64], BF16)
        NrepB = tile(es, 'NrepB', [128, 8, 128], BF16)

        def proj(Wd, blocks, KB, rhs_fn, rhs_keys, ntok, evac):
            for bi, blk in enumerate(blocks):
                ps = nextA()
                for q0 in range(0, KB, 16):
                    kq = min(16, KB - q0)
                    wb = wbuf[wi[0] % NWB]
                    wi[0] += 1
                    wv = V(wb.k, wb.a[:, 0:kq * 128].rearrange("p (k c) -> p k c", c=128))
                    P.dma('sp', wv.a, Wd[blk][:, q0:q0 + kq, :], writes=[wb])
                    for kb in range(kq):
                        P.I('pe', lambda e, kb=kb, q0=q0, wv=wv: e.matmul(
                            ps.a[:, 0:ntok], lhsT=wv.a[:, kb, :], rhs=rhs_fn(q0 + kb),
                            start=(q0 + kb == 0), stop=(q0 + kb == KB - 1)),
                            reads=[wb] + list(rhs_keys), writes=[ps])
                evac(bi, ps[:, 0:ntok])

        def fm_norm(src, gidx, TG, dst=None, resid=None, stk=None):
            sq = hT[:, :, 0:TG]
            act(sq, src, AF.Square)
            ps = nextB()
            for kb in range(16):
                mm(ps[:, 0:TG], ones_b, sq[:, kb, :], start=(kb == 0), stop=(kb == 15))
            rstd = tile(stk, 'rstd', [128, TG])
            rsqrt_from(rstd, ps[:, 0:TG], 1.0 / D, EPS)
            for kb in range(16):
                if resid is None:
                    stt(dst[:, kb, :], src[:, kb, :], gains[:, gidx, kb:kb + 1], rstd, ALU.mult, ALU.mult)
                else:
                    stt(src[:, kb, :], src[:, kb, :], gains[:, gidx, kb:kb + 1], rstd, ALU.mult, ALU.mult)
                    tt(resid[:, kb, :], resid[:, kb, :], src[:, kb, :], ALU.add, eng='pool')

        def sub_begin(xcur, t0, TG, gidx):
            with ExitStack() as stk:
                xT = tile(stk, 'xT', [128, 16, TG])
                ld(xT, xcur[:, :, t0:t0 + TG])
                fm_norm(xT, gidx, TG, dst=hT[:, :, 0:TG], stk=stk)
                P.barrier()

        def sub_end(xcur, xnext, t0, TG, gidx):
            with ExitStack() as stk:
                xT = tile(stk, 'xT', [128, 16, TG])
                ld(xT, xcur[:, :, t0:t0 + TG])
                fm_norm(postT[:, :, 0:TG], gidx, TG, resid=xT, stk=stk)
                st(xnext[:, :, t0:t0 + TG], xT)
                P.barrier()

        def scan(out, d0, d1, init, op0, op1):
            rd = [d0, d1]
            iv = init
            if isinstance(init, V):
                rd.append(init)
                iv = init.a
            P.I('dve', lambda e: e.tensor_tensor_scan(out=out.a, data0=d0.a, data1=d1.a, initial=iv, op0=op0, op1=op1),
                reads=rd, writes=[out])

        def mlstm(l, t0, TG, kind, NCH, nseg, slen, Cg, ysT, last_group):
            with ExitStack() as ms:
                gi = tile(ms, 'gi', [8, TG])
                gf = tile(ms, 'gf', [8, TG])

                def ev_i(bi, ps):
                    act(gi, ps[0:8, :], AF.Tanh, bias=mlb[:, 0:1], scale=1.0 / 15.0)

                def ev_f(bi, ps):
                    act(gf, ps[0:8, :], AF.Tanh, bias=mlb[:, 1:2], scale=1.0 / 15.0)
                proj(WBF['w_in'][l], [ML0 + 32], 16, lambda kb: hT.a[:, kb, 0:TG], [hT], TG, ev_i)
                proj(WBF['w_in'][l], [ML0 + 33], 16, lambda kb: hT.a[:, kb, 0:TG], [hT], TG, ev_f)
                ts(gi, gi, 15.0, ALU.mult)
                act(gf, gf, AF.Exp, scale=-15.0)
                act(gf, gf, AF.Ln, bias=1.0)
                ts(gf, gf, -1.0, ALU.mult)
                R_all = tile(ms, 'R_all', [8, NCH, 3, 128])
                gT_all = tile(ms, 'gT_all', [128, NCH, 8])
                bb = tile(ms, 'mbb', [8, 128])
                gg = tile(ms, 'mgg', [8, 128])
                cm = tile(ms, 'mcm', [8, 128])
                mm_ = tile(ms, 'mmm', [8, 128])
                m0tok = tile(ms, 'm0tok', [8, 128])
                if kind == 'p':
                    m0s = mml
                else:
                    m0s = tile(ms, 'm0s', [8, SB])
                    ld(m0s, I_['s_mlm'][l])
                for ch in range(NCH):
                    cs = slice(ch * 128, (ch + 1) * 128)
                    scan(bb, Cg['reset'][0:8, :], gf[:, cs], 0.0, ALU.mult, ALU.add)
                    tt(gg, gi[:, cs], bb, ALU.subtract)
                    for seg in range(nseg):
                        c0 = seg * slen
                        scan(cm[:, c0:c0 + slen], zero8[:, 0:slen], gg[:, c0:c0 + slen], m0s[:, seg:seg + 1],
                             ALU.add, ALU.max)
                        cp(m0tok[:, c0:c0 + slen], m0s[:, seg:seg + 1].bc([8, slen]))
                    tt(mm_, bb, cm, ALU.add)
                    ts(R_all[:, ch, 0, :], cm, -1.0, ALU.mult)
                    tt(m0tok, m0tok, cm, ALU.subtract)
                    act(R_all[:, ch, 1, :], m0tok, AF.Exp)
                    act(R_all[:, ch, 2, :], mm_, AF.Exp, scale=-1.0)
                    pst = nextB()
                    tr(pst[:, 0:8], gg, ident[0:8, 0:8])
                    cp(gT_all[:, ch, :], pst[:, 0:8])
                    for seg in range(nseg):
                        c0 = seg * slen
                        cp(m0s[:, seg:seg + 1], mm_[:, c0 + slen - 1:c0 + slen])
                if kind == 'p':
                    if last_group:
                        st(O_['p_mlm'][l], mml)
                else:
                    st(O_['o_mlm'][l], m0s)
                z4 = tile(ms, 'z4', [128, 4, TG])
                qb = tile(ms, 'mqb', [128, 128], BF16)
                kb_ = tile(ms, 'mkb', [128, 128], BF16)
                ktf = tile(ms, 'mktf', [128, 128])
                kw = tile(ms, 'mkw', [128, 128], BF16)
                vext = tile(ms, 'mvext', [128, 129], BF16)
                if kind == 's':
                    vm_all = tile(ms, 'mvm_all', [128, SB, 129], BF16)
                memset(vext, 1.0)
                bcs = tile(ms, 'mbcs', [128, 3, 128])
                ex = tile(ms, 'mex', [128, 128])
                Dx = tile(ms, 'mDx', [128, 128])
                w1 = tile(ms, 'mw1', [128, 128])
                wts = tile(ms, 'mwts', [128, 128], BF16)
                cd = tile(ms, 'mcd', [128, 2, 128])
                num = tile(ms, 'mnum', [128, 128])
                den = tile(ms, 'mden', [128, 128])
                sq = tile(ms, 'msq', [128, 128], BF16)
                rstd = tile(ms, 'mrstd', [128, 128])
                sgo = tile(ms, 'msgo', [128, 128])
                wend = tile(ms, 'mwend', [128, 1])
                if kind == 's':
                    Cxs = tile(ms, 'Cxs', [128, SB, 129])
                    CBs = tile(ms, 'CBs', [128, SB, 128], BF16)
                    NBs = tile(ms, 'NBs', [128, SB, 128], BF16)
                for h in range(8):
                    def ev(bi, ps):
                        cp(z4[:, bi, :], ps, eng='act')
                    proj(WBF['w_in'][l], [ML0 + h, ML0 + 8 + h, ML0 + 16 + h, ML0 + 24 + h], 16,
                         lambda kb: hT.a[:, kb, 0:TG], [hT], TG, ev)
                    if kind == 's':
                        ld(Cxs, I_['s_mlc'][l, :, h].rearrange("b p e -> p b e"))
                        cp(CBs, Cxs[:, :, 0:128], eng='pool')
                        cp(NBs, Cxs[:, :, 128:129].bc([128, SB, 128]), eng='pool')
                    for ch in range(NCH):
                        cs = slice(ch * 128, (ch + 1) * 128)
                        cp(qb, z4[:, 0, cs], eng='act')
                        ts(kb_, z4[:, 1, cs], 128.0 ** -0.5, ALU.mult, eng='pool')
                        pst = nextB()
                        tr(pst[:, 0:128], z4[:, 1, cs], ident)
                        ts(ktf, pst[:, 0:128], 128.0 ** -0.5, ALU.mult)
                        pst = nextB()
                        tr(pst[:, 0:128], z4[:, 2, cs], ident)
                        cp(vext[:, 0:128], pst[:, 0:128], eng='act')
                        psb = nextB()
                        mm(psb[:, 0:384], sel[:, h, :], V(R_all.k, R_all.a[:, ch].rearrange("p a b -> p (a b)")))
                        cp(bcs, V(psb.k, psb.a[:, 0:384].rearrange("p (a b) -> p a b", b=128)), eng='act')
                        pss = nextB()
                        mm(pss[:, 0:128], kb_, qb)
                        ts(ex, bcs[:, 0, :], gT_all[:, ch, h:h + 1], ALU.add, 0.0, ALU.min)
                        act(Dx, ex, AF.Exp)
                        tt(w1, Dx, Cg['le'], ALU.mult, eng='pool')
                        tt(wts, w1, pss[:, 0:128], ALU.mult)
                        psn = nextB()
                        mm(psn[:, 0:128], vext[:, 0:128], wts)
                        mm(psn[:, 128:256], ones_b, wts)
                        for seg in range(nseg):
                            c0 = seg * slen
                            CB = CmlB[:, h, :] if kind == 'p' else CBs[:, seg, :]
                            NB = NrepB[:, h, :] if kind == 'p' else NBs[:, seg, :]
                            mm(psn[:, 256 + c0:256 + c0 + slen], CB, qb[:, c0:c0 + slen])
                            mm(psn[:, 384 + c0:384 + c0 + slen], NB, qb[:, c0:c0 + slen])
                        cp(cd, V(psn.k, psn.a[:, 256:512].rearrange("p (a b) -> p a b", b=128)), eng='act')
                        tt(num, cd[:, 0, :], bcs[:, 1, :], ALU.mult)
                        tt(num, num, psn[:, 0:128], ALU.add)
                        tt(den, cd[:, 1, :], bcs[:, 1, :], ALU.mult)
                        tt(den, den, psn[:, 128:256], ALU.add)
                        stt(den, den, -1.0, den, ALU.mult, ALU.max)
                        tt(den, den, bcs[:, 2, :], ALU.max)
                        recip(den, den)
                        tt(num, num, den, ALU.mult)
                        act(sq, num, AF.Square)
                        psr = nextB()
                        mm(psr[:, 0:128], ones_b, sq)
                        rsqrt_from(rstd, psr[:, 0:128], 1.0 / 128.0, EPS)
                        act(sgo, z4[:, 3, cs], AF.Sigmoid)
                        stt(num, num, rwp[:, 7, h:h + 1], rstd, ALU.mult, ALU.mult)
                        tt(ysT[:, 8 + h, cs], num, sgo, ALU.mult)
                        tt(w1, Dx, Cg['el'], ALU.mult, eng='pool')
                        P.I('dve', lambda e: e.reduce_sum(out=wend.a, in_=w1.a, axis=AX.X), reads=[w1], writes=[wend])
                        ts(kw, ktf, wend[:, 0:1], ALU.mult)
                        if nseg > 1:
                            tt(vm_all, V(vext.k, vext.a.unsqueeze(1).to_broadcast([128, SB, 129])),
                               V(segcol.k, segcol.a.unsqueeze(2).to_broadcast([128, SB, 129])), ALU.mult)
                        for seg in range(nseg):
                            c0 = seg * slen
                            if nseg > 1:
                                v_ = vm_all[:, seg, :]
                            else:
                                v_ = vext
                            pd = nextB()
                            mm(pd[:, 0:129], kw, v_)
                            Cx = Cml[:, h, :] if kind == 'p' else Cxs[:, seg, :]
                            stt(Cx, Cx, bcs[:, 1, c0 + slen - 1:c0 + slen], pd[:, 0:129], ALU.mult, ALU.add)
                        if kind == 'p':
                            cp(CmlB[:, h, :], Cml[:, h, 0:128], eng='pool')
                            cp(NrepB[:, h, :], Cml[:, h, 128:129].bc([128, 128]), eng='pool')
                            if last_group and ch == NCH - 1:
                                st(O_['p_mlc'][l, h], Cml[:, h, :])
                    if kind == 's':
                        st(O_['o_mlc'][l, :, h].rearrange("b p e -> p b e"), Cxs)

        def rwkv(l, t0, TG, kind, NCH, nseg, slen, Cg, ysT, last_group):
            nlev = int(round(math.log2(slen))) - 1
            with ExitStack() as ws:
                pv = tile(ws, 'pv', [128, TG])
                if kind == 's':
                    sh0 = tile(ws, 'sh0', [128, 27, SB])
                    osh = tile(ws, 'osh', [128, 27, SB])
                    ld(sh0, I_['s_shift'][l])
                    segrow = tile(ws, 'segrow', [128, SB, 128])
                    ld(segrow, I_['c_segrow'])
                    atm_all = tile(ws, 'atm_all', [128, SB, 128], BF16)
                    Ts = tile(ws, 'Ts', [128, SB, 64])
                    TsB = tile(ws, 'TsB', [128, SB, 64], BF16)
                    ktm_all = tile(ws, 'ktm_all', [128, SB, 128], BF16)
                    btm_all = tile(ws, 'btm_all', [128, SB, 128], BF16)
                    segcol3 = V(segcol.k, segcol.a.unsqueeze(2).to_broadcast([128, SB, 128]))

                def tshift(u, blk):
                    if kind == 'p':
                        cp(pv[:, 0:1], carry[:, blk:blk + 1], eng='pool')
                        cp(pv[:, 1:TG], u[:, 0:TG - 1], eng='pool')
                        cp(carry[:, blk:blk + 1], u[:, TG - 1:TG], eng='pool')
                    else:
                        u3d = V(u.k, u.a.rearrange("p (b t) -> p b t", t=8))
                        p3d = V(pv.k, pv.a.rearrange("p (b t) -> p b t", t=8))
                        cp(p3d[:, :, 0:1], V(sh0.k, sh0.a[:, blk, :].unsqueeze(2)), eng='pool')
                        cp(p3d[:, :, 1:8], u3d[:, :, 0:7], eng='pool')
                        cp(V(osh.k, osh.a[:, blk, :].unsqueeze(2)), u3d[:, :, 7:8], eng='pool')
                    tt(pv, pv, u, ALU.subtract, eng='pool')
                    stt(u, pv, mu[:, blk:blk + 1], u, ALU.mult, ALU.add)

                tw = tile(ws, 'tw', [64, TG], BF16)
                ab = tile(ws, 'ab', [64, TG], BF16)
                sgd = tile(ws, 'sgd', [128, TG], BF16)
                with ExitStack() as us:
                    u3 = tile(us, 'u3', [128, 3, TG])

                    def ev3(bi, ps):
                        cp(u3[:, bi, :], ps, eng='act')
                    proj(WBF['w_in'][l], [RW0 + 24, RW0 + 25, RW0 + 26], 16, lambda kb: hT.a[:, kb, 0:TG], [hT], TG,
                         ev3)
                    for bi in range(3):
                        tshift(u3[:, bi, :], 24 + bi)
                    act(tw, u3[0:64, 0, :], AF.Tanh)
                    cp(ab, u3[0:64, 1, :])
                    act(sgd, u3[:, 2, :], AF.Sigmoid)
                    P.barrier()
                rkv = tile(ws, 'rkv', [128, 3, TG])
                e2 = tile(ws, 'e2', [128, TG])
                a_ = tile(ws, 'a_', [128, TG])
                g_ = tile(ws, 'g_', [128, TG])
                kk = tile(ws, 'kk', [128, TG])
                kkn = tile(ws, 'kkn', [128, TG])
                kmod = tile(ws, 'kmod', [128, TG])
                bvec = tile(ws, 'bvec', [128, TG])
                bon = tile(ws, 'bon', [128, TG])
                tmp = tile(ws, 'rtmp', [128, TG])
                tb = tile(ws, 'rtb', [128, TG], BF16)
                c = {}
                for nm in ('cwp', 'ecn', 'ecp', 'ktf', 'btf', 'dd', 'yf', 'cen', 'rs'):
                    c[nm] = tile(ws, 'rw_' + nm, [128, 128])
                for nm in ('rt', 'kt', 'bt', 'at', 'kt_tok', 'bt_tok', 'v_tok', 'yb', 'cb'):
                    c[nm] = tile(ws, 'rw_' + nm, [128, 128], BF16)
                Uneg = tile(ws, 'rw_Un', [128, 64], BF16)
                cjs = []
                for j_ in range(2):
                    cj_ = {}
                    for nm in ('X', 'Nn', 'X2', 'N2', 'Xb', 'Nb', 'PTf', 'Pf', 'tmpT'):
                        cj_[nm] = tile(ws, 'rwj%d_%s' % (j_, nm), [128, 128], F32)
                    for nm in ('MT', 'Aqk', 'Aqb'):
                        cj_[nm] = tile(ws, 'rwj%d_%s' % (j_, nm), [128, 128], BF16)
                    cj_['Gb'] = tile(ws, 'rwj%d_Gb' % j_, [128, 64], F32)
                    cj_['Uneg'] = tile(ws, 'rwj%d_Un' % j_, [128, 64], BF16)
                    cjs.append(cj_)
                for p in range(8):
                    pc = slice(p * 128, (p + 1) * 128)

                    def evr(bi, ps):
                        cp(rkv[:, bi, :], ps, eng='act')
                    proj(WBF['w_in'][l], [RW0 + p, RW0 + 8 + p, RW0 + 16 + p], 16,
                         lambda kb: hT.a[:, kb, 0:TG], [hT], TG, evr)
                    for bi in range(3):
                        tshift(rkv[:, bi, :], bi * 8 + p)
                    r = rkv[:, 0, :]
                    k = rkv[:, 1, :]
                    v = rkv[:, 2, :]
                    psw = nextB()
                    mm(psw[:, 0:TG], wup[:, pc], tw)
                    act(tmp, psw[:, 0:TG], AF.Exp, scale=-1.0, bias=negw0[:, p:p + 1])
                    act(tmp, tmp, AF.Ln, bias=1.0)
                    act(e2, tmp, AF.Exp, scale=-1.0, bias=-0.5)
                    psa = nextB()
                    mm(psa[:, 0:TG], aup[:, pc], ab)
                    act(a_, psa[:, 0:TG], AF.Sigmoid, bias=rwp[:, 1, p:p + 1])
                    psg = nextB()
                    mm(psg[:, 0:TG], gup[:, pc], sgd)
                    cp(g_, psg[:, 0:TG], eng='act')
                    ts(kk, k, rwp[:, 2, p:p + 1], ALU.mult)
                    act(tb, kk, AF.Square)
                    pss = nextB()
                    mm(pss[:, 0:TG], half_b, tb)
                    act(tmp, pss[:, 0:TG], AF.Sqrt)
                    ts(tmp, tmp, 1e-12, ALU.max)
                    recip(tmp, tmp)
                    tt(kkn, kk, tmp, ALU.mult)
                    ts(tmp, a_, -1.0, ALU.add, rwp[:, 3, p:p + 1], ALU.mult)
                    ts(tmp, tmp, 1.0, ALU.add)
                    tt(kmod, k, tmp, ALU.mult)
                    tt(bvec, kkn, a_, ALU.mult)
                    tt(tmp, r, kmod, ALU.mult)
                    ts(tb, tmp, rwp[:, 4, p:p + 1], ALU.mult)
                    psb = nextB()
                    mm(psb[:, 0:TG], half_b, tb)
                    tt(bon, psb[:, 0:TG], v, ALU.mult)
                    if kind == 's':
                        ld(Ts, I_['s_rw'][l, :, p].rearrange("b q v -> q b v"))
                        cp(TsB, Ts, eng='pool')
                    for ch in range(NCH):
                        cs = slice(ch * 128, (ch + 1) * 128)
                        scan(c['cwp'], Cg['reset'], e2[:, cs], 0.0, ALU.mult, ALU.add)
                        act(c['ecn'], c['cwp'], AF.Exp, scale=-1.0)
                        act(c['ecp'], c['cwp'], AF.Exp)
                        tt(c['rt'], r[:, cs], c['ecn'], ALU.mult)
                        tt(c['ktf'], kmod[:, cs], c['ecp'], ALU.mult)
                        cp(c['kt'], c['ktf'])
                        tt(c['btf'], bvec[:, cs], c['ecp'], ALU.mult)
                        cp(c['bt'], c['btf'])
                        tt(c['dd'], e2[:, cs], c['cwp'], ALU.subtract)
                        act(c['dd'], c['dd'], AF.Exp)
                        tt(c['at'], kkn[:, cs], c['dd'], ALU.mult)
                        for src, dst in ((c['ktf'], c['kt_tok']), (c['btf'], c['bt_tok']), (v[:, cs], c['v_tok'])):
                            pst = nextB()
                            tr(pst[:, 0:128], src, ident)
                            cp(dst, pst[:, 0:128], eng='act')
                        if kind == 's':
                            tt(atm_all, V(c['at'].k, c['at'].a.unsqueeze(1).to_broadcast([128, SB, 128])), segrow,
                               ALU.mult, eng='pool')
                            tt(ktm_all, V(c['kt_tok'].k, c['kt_tok'].a.unsqueeze(1).to_broadcast([128, SB, 128])),
                               segcol3, ALU.mult)
                            tt(btm_all, V(c['bt_tok'].k, c['bt_tok'].a.unsqueeze(1).to_broadcast([128, SB, 128])),
                               segcol3, ALU.mult)
                        psY = psB[4]
                        for j in range(2):
                            cj = cjs[j]
                            pr = slice(j * 64, (j + 1) * 64)
                            jc = slice(j * 64, (j + 1) * 64)
                            psN = nextB()
                            mm(psN[:, 0:128], c['bt'][pr, :], c['at'][pr, :])
                            mm(psN[:, 128:256], c['at'][pr, :], c['bt'][pr, :])
                            mm(psN[:, 256:384], c['kt'][pr, :], c['at'][pr, :])
                            tt(cj['X'], psN[:, 0:128], Cg['lt'], ALU.mult)
                            tt(cj['Nn'], psN[:, 128:256], Cg['gt'], ALU.mult)
                            tt(cj['MT'], psN[:, 256:384], Cg['lt'], ALU.mult)
                            psQ = nextB()
                            mm(psQ[:, 0:128], c['kt'][pr, :], c['rt'][pr, :])
                            mm(psQ[:, 128:256], c['bt'][pr, :], c['rt'][pr, :])
                            tt(cj['Aqk'], psQ[:, 0:128], Cg['le'], ALU.mult)
                            tt(cj['Aqb'], psQ[:, 128:256], Cg['le'], ALU.mult)
                            psG = nextB()
                            mm(psG[:, 0:64], cj['MT'], c['v_tok'][:, jc], start=True, stop=False)
                            for seg in range(nseg):
                                lhs = c['at'][pr, :] if kind == 'p' else atm_all[pr, seg, :]
                                T0b = TrwB[pr, p, :] if kind == 'p' else TsB[pr, seg, :]
                                mm(psG[:, 0:64], lhs, T0b, start=False, stop=(seg == nseg - 1))
                            cp(cj['Gb'], psG[:, 0:64], eng='act')
                            tt(cj['PTf'], ident, cj['X'], ALU.subtract)
                            tt(cj['Pf'], ident, cj['Nn'], ALU.subtract)
                        cur = [(cjs[j]['X'], cjs[j]['Nn']) for j in range(2)]
                        for lv in range(nlev):
                            for j in range(2):
                                cj = cjs[j]
                                Xk, Nk = cur[j]
                                X2 = cj['X2'] if lv % 2 == 0 else cj['Xb']
                                N2 = cj['N2'] if lv % 2 == 0 else cj['Nb']
                                psq = nextB()
                                mm(psq[:, 0:128], Nk, Xk)
                                mm(psq[:, 128:256], Xk, Nk)
                                cp(X2, psq[:, 0:128], eng='act')
                                cp(N2, psq[:, 128:256])
                                psp = nextB()
                                mm(psp[:, 0:128], cj['Pf'], X2)
                                mm(psp[:, 128:256], X2, cj['Pf'])
                                tt(cj['PTf'], cj['PTf'], psp[:, 0:128], ALU.add)
                                tt(cj['Pf'], cj['Pf'], psp[:, 128:256], ALU.add)
                                cur[j] = (X2, N2)
                        for j in range(2):
                            cj = cjs[j]
                            pr = slice(j * 64, (j + 1) * 64)
                            jc = slice(j * 64, (j + 1) * 64)
                            Uneg = cj['Uneg']
                            psU = nextB()
                            mm(psU[:, 0:64], cj['PTf'], cj['Gb'])
                            ts(Uneg, psU[:, 0:64], -1.0, ALU.mult)
                            mm(psY[pr, 0:128], c['v_tok'][:, jc], cj['Aqk'], start=True, stop=False)
                            mm(psY[pr, 0:128], Uneg, cj['Aqb'], start=False, stop=False)
                            for seg in range(nseg):
                                c0 = seg * slen
                                T0b = TrwB[pr, p, :] if kind == 'p' else TsB[pr, seg, :]
                                mm(psY[pr, c0:c0 + slen], T0b, c['rt'][pr, c0:c0 + slen], start=False,
                                   stop=(seg == nseg - 1))
                            for seg in range(nseg):
                                c0 = seg * slen
                                if nseg > 1:
                                    k_, b_ = ktm_all[:, seg, :], btm_all[:, seg, :]
                                else:
                                    k_, b_ = c['kt_tok'], c['bt_tok']
                                psD = nextB()
                                mm(psD[pr, 0:64], k_[:, pr], c['v_tok'][:, jc], start=True, stop=False)
                                mm(psD[pr, 0:64], b_[:, pr], Uneg, start=False, stop=True)
                                Tf = Trw[pr, p, :] if kind == 'p' else Ts[pr, seg, :]
                                tt(cj['tmpT'][pr, 0:64], Tf, psD[pr, 0:64], ALU.add)
                                ts(Tf, cj['tmpT'][pr, 0:64], c['ecn'][pr, c0 + slen - 1:c0 + slen], ALU.mult)
                        if kind == 'p':
                            cp(TrwB[:, p, :], Trw[:, p, :])
                            if last_group and ch == NCH - 1:
                                st(O_['p_rw'][l, p], Trw[:, p, :])
                        cp(c['yf'], psY[:, 0:128], eng='act')
                        cp(c['yb'], c['yf'])
                        psm = nextB()
                        mm(psm[:, 0:128], half_b, c['yb'])
                        stt(c['cen'], psm[:, 0:128], -1.0 / 64.0, c['yf'], ALU.mult, ALU.add)
                        act(c['cb'], c['cen'], AF.Square)
                        psv = nextB()
                        mm(psv[:, 0:128], half_b, c['cb'])
                        rsqrt_from(c['rs'], psv[:, 0:128], 1.0 / 64.0, RW_GN_EPS)
                        tt(c['cen'], c['cen'], c['rs'], ALU.mult)
                        ts(c['cen'], c['cen'], rwp[:, 5, p:p + 1], ALU.mult, rwp[:, 6, p:p + 1], ALU.add)
                        tt(c['cen'], c['cen'], bon[:, cs], ALU.add)
                        tt(ysT[:, p, cs], c['cen'], g_[:, cs], ALU.mult)
                    if kind == 's':
                        st(O_['o_rw'][l, :, p].rearrange("b q v -> q b v"), Ts)
                if kind == 'p':
                    if last_group:
                        st(O_['p_shift'][l], carry)
                else:
                    st(O_['o_shift'][l], osh)

        def mixer(l, t0, TG, kind, NCH, nseg, slen, Cg, ysT, last_group):
            rwkv(l, t0, TG, kind, NCH, nseg, slen, Cg, ysT, last_group)
            P.barrier()
            mlstm(l, t0, TG, kind, NCH, nseg, slen, Cg, ysT, last_group)
            P.barrier()
            gl = make_consts(slen)['gl']
            with ExitStack() as rs:
                z = tile(rs, 'rtz', [128, 8, TG])
                cosT = tile(rs, 'cosT', [128, TG])
                sinT = tile(rs, 'sinT', [128, TG])
                ld(cosT, I_['c_cos'][:, t0:t0 + TG])
                ld(sinT, I_['c_sin'][:, t0:t0 + TG])
                t1 = tile(rs, 't1', [128, 128])
                t2 = tile(rs, 't2', [128, 128])
                t3 = tile(rs, 't3', [128, 128])
                t4 = tile(rs, 't4', [128, 128])
                qrf = tile(rs, 'qrf', [128, 2, 128])
                krf = tile(rs, 'krf', [128, 2, 128])
                qr = tile(rs, 'qr', [128, 2, 128], BF16)
                kr = tile(rs, 'kr', [128, 2, 128], BF16)
                qs = tile(rs, 'qs', [128, 2, 128], BF16)
                ktok = tile(rs, 'ktok', [128, 256], BF16)
                ktm = tile(rs, 'ktm', [128, 256], BF16)
                vtok = tile(rs, 'vtok', [128, 256], BF16)
                AT = tile(rs, 'AT', [128, 128], BF16)
                yc = tile(rs, 'yc', [128, 2, 128])
                ysum = tile(rs, 'ysum', [128, 2, 128])
                ysq = tile(rs, 'ysq', [128, 2, 128], BF16)
                rstd = tile(rs, 'rrstd', [128, 128])
                sg = tile(rs, 'rsg', [128, 2, 128])
                Ssm = [tile(rs, 'Ssm%d' % i, [128, 2, 256]) for i in range(2)]
                Sbs = [tile(rs, 'Sbs%d' % i, [128, 2, 256], BF16) for i in range(2)]
                for h in range(4):
                    blocks = [RT0 + 2 * h, RT0 + 2 * h + 1, RT0 + 8 + 2 * h, RT0 + 9 + 2 * h,
                              RT0 + 16 + 2 * h, RT0 + 17 + 2 * h, RT0 + 24 + 2 * h, RT0 + 25 + 2 * h]

                    def ev(bi, ps):
                        cp(z[:, bi, :], ps, eng='act')
                    proj(WBF['w_in'][l], blocks, 16, lambda kb: hT.a[:, kb, 0:TG], [hT], TG, ev)
                    for ch in range(NCH):
                        cs = slice(ch * 128, (ch + 1) * 128)
                        for s0, dstf, dstb in ((0, qrf, qr), (2, krf, kr)):
                            x1 = z[:, s0, cs]
                            x2 = z[:, s0 + 1, cs]
                            tt(t1, x1, cosT[:, cs], ALU.mult)
                            tt(t2, x2, sinT[:, cs], ALU.mult)
                            tt(dstf[:, 0, :], t1, t2, ALU.subtract)
                            tt(t3, x1, sinT[:, cs], ALU.mult, eng='pool')
                            tt(t4, x2, cosT[:, cs], ALU.mult, eng='pool')
                            tt(dstf[:, 1, :], t3, t4, ALU.add, eng='pool')
                            cp(dstb, dstf, eng='act')
                        for b in range(2):
                            tt(qs[:, b, :], qrf[:, b, :], Cg['gp'][:, h, :], ALU.mult)
                        for b in range(2):
                            pst = nextB()
                            tr(pst[:, 0:128], krf[:, b, :], ident)
                            ts(ktok[:, b * 128:(b + 1) * 128], pst[:, 0:128], Cg['kd'][:, h:h + 1], ALU.mult)
                            pst = nextB()
                            tr(pst[:, 0:128], z[:, 4 + b, cs], ident)
                            cp(vtok[:, b * 128:(b + 1) * 128], pst[:, 0:128], eng='act')
                        pss = nextB()
                        for b in range(2):
                            mm(pss[:, 0:128], kr[:, b, :], qr[:, b, :], start=(b == 0), stop=(b == 1))
                        tt(AT, pss[:, 0:128], Cg['dt'][:, h, :], ALU.mult)
                        psy = psB[4]
                        for eb in range(2):
                            mm(psy[:, eb * 128:(eb + 1) * 128], vtok[:, eb * 128:(eb + 1) * 128], AT)
                        for seg in range(nseg):
                            if kind == 'p':
                                S = Srt[:, h]
                                Sb = SrtB[:, h]
                            else:
                                S = Ssm[seg % 2]
                                Sb = Sbs[seg % 2]
                                ld(S, I_['s_rt'][l, seg, h])
                                cp(Sb, S, eng='pool')
                            c0 = seg * slen
                            for eb in range(2):
                                for db in range(2):
                                    mm(psy[:, 256 + eb * 128 + c0:256 + eb * 128 + c0 + slen],
                                       Sb[:, db, eb * 128:(eb + 1) * 128], qs[:, db, c0:c0 + slen],
                                       start=(db == 0), stop=(db == 1))
                            if nseg > 1:
                                ts(ktm, ktok, segcol[:, seg:seg + 1], ALU.mult)
                                kt_ = ktm
                            else:
                                kt_ = ktok
                            for db in range(2):
                                pd = nextB()
                                mm(pd[:, 0:256], kt_[:, db * 128:(db + 1) * 128], vtok)
                                stt(S[:, db, :], S[:, db, :], gl[h], pd[:, 0:256], ALU.mult, ALU.add)
                            if kind == 'p':
                                cp(Sb, S, eng='pool')
                                if last_group and ch == NCH - 1:
                                    st(O_['p_rt'][l, h], S)
                            else:
                                st(O_['o_rt'][l, seg, h], S)
                        cp(yc, V(psy.k, psy.a[:, 256:512].rearrange("p (e t) -> p e t", t=128)), eng='act')
                        tt(ysum, V(psy.k, psy.a[:, 0:256].rearrange("p (e t) -> p e t", t=128)), yc, ALU.add)
                        act(ysq, ysum, AF.Square)
                        pn = nextB()
                        for eb in range(2):
                            mm(pn[:, 0:128], ones_b, ysq[:, eb, :], start=(eb == 0), stop=(eb == 1))
                        rsqrt_from(rstd, pn[:, 0:128], 1.0 / 256.0, EPS)
                        act(sg, z[:, 6:8, cs], AF.Sigmoid)
                        tt(sg, sg, z[:, 6:8, cs], ALU.mult, eng='pool')
                        for eb in range(2):
                            tt(ysum[:, eb, :], ysum[:, eb, :], rstd, ALU.mult)
                            tt(ysT[:, 16 + 2 * h + eb, cs], ysum[:, eb, :], sg[:, eb, :], ALU.mult)

        for l in range(DEPTH):
            xcur = I_['xT'] if l == 0 else xs[0]
            xmid = xs[0] if l == 0 else xs[0]
            ld(gains, I_['gains'][l])
            ld(rwp, I_['rwp'][l])
            ld(mu, I_['rw_mu'][l])
            ld(mlb, I_['ml_b'][l])
            ts(negw0, rwp[:, 0, :], -1.0, ALU.mult)
            ts(mlb, mlb, 1.0 / 15.0, ALU.mult)
            memset(carry, 0.0)
            memset(Trw, 0.0)
            memset(Cml, 0.0)
            memset(mml, 0.0)
            memset(Srt, 0.0)
            memset(SrtB, 0.0)
            memset(CmlB, 0.0)
            memset(TrwB, 0.0)
            memset(NrepB, 0.0)
            with ExitStack() as lst:
                wup = tile(lst, 'wup', [64, 1024], BF16)
                aup = tile(lst, 'aup', [64, 1024], BF16)
                gup = tile(lst, 'gup', [128, 1024], BF16)
                P.dma('pool', wup.a, I_['rw_wup'][l], writes=[wup])
                P.dma('pool', aup.a, I_['rw_aup'][l], writes=[aup])
                P.dma('pool', gup.a, I_['rw_gup'][l], writes=[gup])
                KTp = tile(lst, 'KTp', [128, 4, 256], BF16)
                Vp = tile(lst, 'Vp', [128, 2, 512], BF16)
                with ExitStack() as stk:
                    mT = tile(stk, 'mT', [128, 16, 256])
                    ld(mT, I_['memT'])
                    fm_norm(mT, 6, 256, dst=hT[:, :, 0:256], stk=stk)
                    kf = tile(stk, 'kf', [128, 4, 256])

                    def ev_k(bi, ps):
                        cp(kf[:, bi, :], ps, eng='act')
                        cp(KTp[:, bi, :], ps)
                    proj(WBF['x_wkv'][l], [0, 1, 2, 3], 16, lambda kb: hT.a[:, kb, 0:256], [hT], 256, ev_k)
                    st(O_['p_kT'][l].rearrange("h p m -> p h m"), kf)
                    vf = tile(stk, 'vf', [128, 2, 512])
                    for bi in range(4):
                        wb = wbuf[wi[0] % NWB]
                        wi[0] += 1
                        wv = V(wb.k, wb.a[:, 0:2048].rearrange("p (k c) -> p k c", c=128))
                        P.dma('sp', wv.a, WBF['x_wkv'][l][4 + bi], writes=[wb])
                        for mb in range(2):
                            ps = nextA()
                            for kb in range(16):
                                mm(ps[:, 0:128], hT[:, kb, mb * 128:(mb + 1) * 128], wv[:, kb, :],
                                   start=(kb == 0), stop=(kb == 15))
                            cp(vf[:, mb, bi * 128:(bi + 1) * 128], ps[:, 0:128], eng='act')
                            cp(Vp[:, mb, bi * 128:(bi + 1) * 128], ps[:, 0:128])
                    st(O_['p_v'][l], vf)
                    P.barrier()

                for gi, (t0, TG, kind) in enumerate(groups):
                    NCH = TG // 128
                    nseg = 1 if kind == 'p' else SB
                    slen = 128 // nseg
                    Cg = C[kind]
                    last_group = (kind == 'p' and t0 + TG == NPT) or kind == 's'
                    sub_begin(xcur, t0, TG, 0)
                    with ExitStack() as mst:
                        ysT = tile(mst, 'ysT', [128, 24, TG], BF16)
                        mixer(l, t0, TG, kind, NCH, nseg, slen, Cg, ysT, last_group)
                        P.barrier()
                        import os as _os
                        for _z in _os.environ.get('MK_ZERO', ''):
                            memset(ysT[:, int(_z) * 8:int(_z) * 8 + 8, :], 0.0, eng='dve')
                        mergedT = tile(mst, 'mergedT', [128, 16, TG], BF16)
                        acc = tile(mst, 'acc', [128, TG])
                        sg = tile(mst, 'sg', [128, TG])
                        for cb in range(16):
                            for c in range(3):
                                def ev_g(bi, ps):
                                    act(sg, ps, AF.Sigmoid)
                                proj(WBF['w_in'][l], [GT0 + c * 16 + cb], 16, lambda kb: hT.a[:, kb, 0:TG], [hT], TG, ev_g)

                                def ev_p(bi, ps, c=c):
                                    if c == 0:
                                        tt(acc, sg, ps, ALU.mult)
                                    else:
                                        tt(sg, sg, ps, ALU.mult)
                                        tt(acc, acc, sg, ALU.add, eng='pool')
                                proj(WBF['w_br'][l][c], [cb], 8, lambda kb, c=c: ysT.a[:, c * 8 + kb, :], [ysT], TG, ev_p)
                            cp(mergedT[:, cb, :], acc, eng='act')

                        def ev_o(bi, ps):
                            cp(postT[:, bi, 0:TG], ps, eng='act')
                        proj(WBF['w_out'][l], list(range(16)), 16, lambda kb: mergedT.a[:, kb, :], [mergedT], TG, ev_o)
                        P.barrier()
                    sub_end(xcur, xs[0], t0, TG, 1)
                    sub_begin(xs[0], t0, TG, 2)
                    with ExitStack() as ast:
                        qT = tile(ast, 'qT', [128, 4, TG], BF16)

                        def ev_q(bi, ps):
                            act(qT[:, bi, :], ps, AF.Copy, scale=128.0 ** -0.5)
                        proj(WBF['x_wq'][l], [0, 1, 2, 3], 16, lambda kb: hT.a[:, kb, 0:TG], [hT], TG, ev_q)
                        oT = tile(ast, 'oT', [128, 4, TG], BF16)
                        if kind == 's':
                            KTs = tile(ast, 'KTs', [128, SB, 4, 256], BF16)
                            Vs = tile(ast, 'Vs', [128, SB, 2, 512], BF16)
                            for b in range(SB):
                                P.dma('pool', KTs.a[:, b], I_['s_kT'][l, b].rearrange("h p m -> p h m"), writes=[KTs])
                                P.dma('pool', Vs.a[:, b], I_['s_v'][l, b], writes=[Vs])
                        eT = tile(ast, 'eT', [128, 2, 128], BF16)
                        rec = tile(ast, 'rec', [128, 128])
                        for ch in range(NCH):
                            cs = slice(ch * 128, (ch + 1) * 128)
                            for h in range(4):
                                pss = nextB()
                                pso = nextB()
                                if kind == 'p':
                                    for mb in range(2):
                                        mm(pss[:, mb * 128:(mb + 1) * 128], KTp[:, h, mb * 128:(mb + 1) * 128], qT[:, h, cs])
                                else:
                                    for b in range(SB):
                                        for mb in range(2):
                                            mm(pss[:, mb * 128 + b * 8:mb * 128 + b * 8 + 8],
                                               KTs[:, b, h, mb * 128:(mb + 1) * 128], qT[:, h, b * 8:b * 8 + 8])
                                act(eT, V(pss.k, pss.a[:, 0:256].rearrange("p (m t) -> p m t", t=128)), AF.Exp)
                                for mb in range(2):
                                    mm(pso[:, 128:256], ones_b, eT[:, mb, :], start=(mb == 0), stop=(mb == 1))
                                if kind == 'p':
                                    for mb in range(2):
                                        mm(pso[:, 0:128], Vp[:, mb, h * 128:(h + 1) * 128], eT[:, mb, :],
                                           start=(mb == 0), stop=(mb == 1))
                                else:
                                    for b in range(SB):
                                        for mb in range(2):
                                            mm(pso[:, b * 8:b * 8 + 8], Vs[:, b, mb, h * 128:(h + 1) * 128],
                                               eT[:, mb, b * 8:b * 8 + 8], start=(mb == 0), stop=(mb == 1))
                                recip(rec, pso[:, 128:256])
                                tt(oT[:, h, cs], pso[:, 0:128], rec, ALU.mult)

                        def ev_xo(bi, ps):
                            cp(postT[:, bi, 0:TG], ps, eng='act')
                        proj(WBF['x_wo'][l], list(range(16)), 4, lambda kb: oT.a[:, kb, :], [oT], TG, ev_xo)
                        P.barrier()
                    sub_end(xs[0], xs[0], t0, TG, 3)
                    sub_begin(xs[0], t0, TG, 4)
                    with ExitStack() as fst:
                        aT = tile(fst, 'aT', [128, 64, TG], BF16)
                        rl = [tile(fst, 'rl%d' % i, [128, TG]) for i in range(2)]

                        def ev_1(bi, ps):
                            r_ = rl[bi % 2]
                            act(r_, ps, AF.Relu)
                            tt(aT[:, bi, :], r_, r_, ALU.mult)
                        proj(WBF['ff_w1'][l], list(range(64)), 16, lambda kb: hT.a[:, kb, 0:TG], [hT], TG, ev_1)

                        def ev_2(bi, ps):
                            cp(postT[:, bi, 0:TG], ps, eng='act')
                        proj(WBF['ff_w2'][l], list(range(16)), 64, lambda kb: aT.a[:, kb, :], [aT], TG, ev_2)
                        P.barrier()
                    if l == DEPTH - 1:
                        sub_end(xs[0], O_['yT'], t0, TG, 5)
                    else:
                        sub_end(xs[0], xs[0], t0, TG, 5)
                P.barrier()
        P.finish()
    return nc


def _blk(w, kb):
    K, N = w.shape
    return np.ascontiguousarray(w.reshape(kb, 128, N // 128, 128).transpose(2, 1, 0, 3))


def _fm(v, nb):
    return np.ascontiguousarray(v.reshape(nb, 128).T)


def _pad_cols(w, blocks):
    out = np.zeros((w.shape[0], 128 * len(blocks)), w.dtype)
    for i, (s, n) in enumerate(blocks):
        out[:, i * 128:i * 128 + n] = w[:, s:s + n]
    return out


def _win_blocks():
    b = []
    o = 0
    for i in range(24):
        b.append((o + i * 128, 128))
    o = 3072
    b += [(o, 64), (o + 64, 64), (o + 128, 128)]
    o = 3328
    for i in range(32):
        b.append((o + i * 128, 128))
    o = 3328 + 4096
    b += [(o, 8), (o + 8, 8)]
    o = 3328 + 4112
    for i in range(32):
        b.append((o + i * 128, 128))
    o = 3328 + 4112 + 4096
    for i in range(48):
        b.append((o + i * 128, 128))
    assert len(b) == NBLK_IN
    return b


_NC_CACHE = {}


def kernel(**inp):
    inp = {k: np.asarray(v) for k, v in inp.items()}
    seq = inp['x_prompt'].shape[1]
    NPT = seq
    NT = NPT + 128
    if seq not in _NC_CACHE:
        _NC_CACHE[seq] = build(seq)
    nc = _NC_CACHE[seq]
    f32 = np.float32
    wb = _win_blocks()
    shared = {}
    shared['w_in'] = np.stack([_blk(_pad_cols(inp['w_in'][l], wb), 16) for l in range(DEPTH)])
    shared['w_br'] = np.stack([np.stack([_blk(inp['w_br'][l, c], 8) for c in range(3)]) for l in range(DEPTH)])
    shared['w_out'] = np.stack([_blk(inp['w_out'][l], 16) for l in range(DEPTH)])
    shared['x_wq'] = np.stack([_blk(inp['x_wq'][l], 16) for l in range(DEPTH)])
    shared['x_wkv'] = np.stack([_blk(inp['x_wkv'][l], 16) for l in range(DEPTH)])
    shared['x_wo'] = np.stack([_blk(inp['x_wo'][l], 4) for l in range(DEPTH)])
    shared['ff_w1'] = np.stack([_blk(inp['ff_w1'][l], 16) for l in range(DEPTH)])
    shared['ff_w2'] = np.stack([_blk(inp['ff_w2'][l], 64) for l in range(DEPTH)])
    gn = ['g_pre_mix', 'g_post_mix', 'g_pre_x', 'g_post_x', 'g_pre_ff', 'g_post_ff', 'g_mem']
    shared['gains'] = np.stack([np.stack([_fm(inp[n][l], 16) for n in gn], axis=1) for l in range(DEPTH)])
    rn = ['rw_w0', 'rw_a0', 'rw_k_k', 'rw_k_a', 'rw_r_k', 'rw_gn_g', 'rw_gn_b', 'ml_norm_g']
    shared['rwp'] = np.stack([np.stack([_fm(inp[n][l].reshape(-1), 8) for n in rn], axis=1) for l in range(DEPTH)])
    mu_pad = np.zeros((DEPTH, 27 * 128), f32)
    for l in range(DEPTH):
        mu_pad[l] = _pad_cols(inp['rw_mu'][l][None, :], wb[:27])[0]
    shared['rw_mu'] = np.stack([_fm(mu_pad[l], 27) for l in range(DEPTH)])
    shared['rw_wup'] = inp['rw_w_up']
    shared['rw_aup'] = inp['rw_a_up']
    shared['rw_gup'] = inp['rw_g_up']
    shared['ml_b'] = np.stack([inp['ml_i_b'], inp['ml_f_b']], axis=-1)
    for g_, sl in (('p', 128), ('s', 8)):
        c = make_consts(sl)
        shared['c_le_' + g_] = c['le']
        shared['c_lt_' + g_] = c['lt']
        shared['c_gt_' + g_] = c['gt']
        shared['c_dt_' + g_] = np.ascontiguousarray(c['dt'].transpose(1, 0, 2))
        shared['c_gp_' + g_] = np.ascontiguousarray(c['gp'].transpose(1, 0, 2))
        shared['c_kd_' + g_] = c['kd']
        shared['c_reset_' + g_] = c['reset']
        shared['c_el_' + g_] = c['elast']
        if g_ == 's':
            shared['c_segcol'] = c['segcol']
            shared['c_segrow'] = np.ascontiguousarray(np.broadcast_to(c['segcol'].T[None, :, :], (128, 16, 128)))
    pos = np.concatenate([np.arange(NPT), PAST_LEN + (np.arange(128) % 8)]).astype(f32)
    shared['c_cos'], shared['c_sin'] = rope_tables(pos)
    shared['c_ident'] = np.eye(128, dtype=f32)
    hb = np.zeros((128, 128), f32)
    hb[:64, :64] = 1
    hb[64:, 64:] = 1
    shared['c_half'] = hb
    selm = np.zeros((8, 8, 128), f32)
    for h in range(8):
        selm[h, h, :] = 1
    shared['c_sel'] = selm

    def fmT(x):
        return np.ascontiguousarray(x.T.reshape(16, 128, x.shape[0]).transpose(1, 0, 2))

    in_maps = []
    for c in range(NCORE):
        m = dict(shared)
        bp = c // 2
        bs = slice(c * SB, (c + 1) * SB)
        xtok = np.concatenate([inp['x_prompt'][bp], inp['x_sample'][bs].reshape(SB * DEC_T, D)], axis=0)
        m['xT'] = fmT(xtok)
        m['memT'] = fmT(inp['mem_prompt'][bp])
        sh = np.zeros((DEPTH, SB, 27 * 128), f32)
        for l in range(DEPTH):
            sh[l] = _pad_cols(inp['state_rwkv_shift'][l, bs], wb[:27])
        m['s_shift'] = np.ascontiguousarray(sh.reshape(DEPTH, SB, 27, 128).transpose(0, 3, 2, 1))
        srw = inp['state_rwkv'][:, bs]
        m['s_rw'] = np.ascontiguousarray(srw.transpose(0, 1, 2, 4, 3).reshape(DEPTH, SB, 8, 128, 64))
        m['s_mlc'] = np.ascontiguousarray(np.concatenate(
            [inp['state_mlstm_c'][:, bs], inp['state_mlstm_n'][:, bs][..., None]], axis=-1))
        m['s_mlm'] = np.ascontiguousarray(inp['state_mlstm_m'][:, bs].transpose(0, 2, 1))
        srt = inp['state_ret'][:, bs]
        m['s_rt'] = np.ascontiguousarray(srt.reshape(DEPTH, SB, 4, 2, 128, 256).transpose(0, 1, 2, 4, 3, 5))
        ck = inp['cache_mem_k'][:, bs]
        m['s_kT'] = np.ascontiguousarray(ck.transpose(0, 1, 3, 4, 2))
        cv = inp['cache_mem_v'][:, bs]
        m['s_v'] = np.ascontiguousarray(cv.reshape(DEPTH, SB, 2, 128, 512).transpose(0, 1, 3, 2, 4))
        in_maps.append(m)
    res = run_bass_kernel_spmd(nc, in_maps, core_ids=list(range(NCORE)))
    R = res.results

    def tokT(y):
        return y.transpose(1, 0, 2).reshape(D, -1).T

    nb = inp['x_prompt'].shape[0]
    y_p = np.stack([tokT(R[2 * b]['yT'][:, :, :NPT]) for b in range(nb)]).astype(f32)
    y_s = np.concatenate([tokT(R[c]['yT'][:, :, NPT:]).reshape(SB, DEC_T, D) for c in range(NCORE)]).astype(f32)

    def unpad_shift(a):
        return np.concatenate([a[..., 0:3072], a[..., 3072:3136], a[..., 3200:3264], a[..., 3328:3456]], axis=-1)

    pc = [2 * b for b in range(nb)]
    p_shift = np.stack([unpad_shift(R[c]['p_shift'].transpose(0, 2, 1).reshape(DEPTH, 27 * 128)) for c in pc], axis=1)
    p_rw = np.stack([R[c]['p_rw'].reshape(DEPTH, 16, 64, 64).transpose(0, 1, 3, 2) for c in pc], axis=1)
    p_mlc = np.stack([R[c]['p_mlc'][..., :128] for c in pc], axis=1)
    p_mln = np.stack([R[c]['p_mlc'][..., 128] for c in pc], axis=1)
    p_mlm = np.stack([R[c]['p_mlm'][..., 0] for c in pc], axis=1)
    p_rt = np.stack([R[c]['p_rt'].transpose(0, 1, 3, 2, 4).reshape(DEPTH, 4, 256, 256) for c in pc], axis=1)
    p_mk = np.stack([R[c]['p_kT'].transpose(0, 3, 1, 2) for c in pc], axis=1)
    p_mv = np.stack([R[c]['p_v'].transpose(0, 2, 1, 3).reshape(DEPTH, 256, 4, 128) for c in pc], axis=1)
    s_shift = np.concatenate([unpad_shift(R[c]['o_shift'].transpose(0, 3, 2, 1).reshape(DEPTH, SB, 27 * 128))
                              for c in range(NCORE)], axis=1)
    s_rw = np.concatenate([R[c]['o_rw'].reshape(DEPTH, SB, 16, 64, 64).transpose(0, 1, 2, 4, 3)
                           for c in range(NCORE)], axis=1)
    s_mlc = np.concatenate([R[c]['o_mlc'][..., :128] for c in range(NCORE)], axis=1)
    s_mln = np.concatenate([R[c]['o_mlc'][..., 128] for c in range(NCORE)], axis=1)
    s_mlm = np.concatenate([R[c]['o_mlm'].transpose(0, 2, 1) for c in range(NCORE)], axis=1)
    s_rt = np.concatenate([R[c]['o_rt'].transpose(0, 1, 2, 4, 3, 5).reshape(DEPTH, SB, 4, 256, 256)
                           for c in range(NCORE)], axis=1)
    outs = (y_p, y_s, p_shift, p_rw, p_mlc, p_mln, p_mlm, p_rt, p_mk, p_mv,
            s_shift, s_rw, s_mlc, s_mln, s_mlm, s_rt)
    return tuple(np.ascontiguousarray(o, dtype=f32) for o in outs)
```
